# Optimizing a Trainium2 kernel written in Bass

```python
import math
import jax, jax.numpy as jnp
from jax import lax
import numpy as np

D_MODEL = 1024
BATCH = 2
SEQ = 8192
DEPTH = 2

N_META = 16
N_A_LAYERS = DEPTH // 2
N_B_LAYERS = DEPTH - N_A_LAYERS
NORM_EPS = 1e-6

GDN_QK_HEADS = 8
GDN_V_HEADS = 16
GDN_DK = 128
GDN_DV = 128
GDN_CONV = 4
GDN_CHUNK = 64
GDN_QK_W = GDN_QK_HEADS * GDN_DK
GDN_V_W = GDN_V_HEADS * GDN_DV
GDN_CONV_W = 2 * GDN_QK_W + GDN_V_W
GDN_IN_W = GDN_CONV_W + GDN_V_W + 2 * GDN_V_HEADS

MLA_HEADS = 16
MLA_NOPE = 128
MLA_ROPE = 64
MLA_V = 128
MLA_Q_RANK = 256
MLA_KV_RANK = 128
MLA_QK = MLA_NOPE + MLA_ROPE
MLA_V_W = MLA_HEADS * MLA_V
MLA_IN_W = MLA_Q_RANK + MLA_V_W
ROPE_THETA = 10000.0
Q_BLOCK = 128

kernel_name = "yoco_gdn_mla_hybrid"


def rmsnorm(x, g):
    xf = x.astype(jnp.float32)
    y = xf * lax.rsqrt(jnp.mean(xf * xf, axis=-1, keepdims=True) + NORM_EPS)
    return (y * g.astype(jnp.float32)).astype(x.dtype)


def l2norm(x):
    xf = x.astype(jnp.float32)
    return (xf * lax.rsqrt(jnp.sum(xf * xf, axis=-1, keepdims=True) + NORM_EPS)).astype(x.dtype)


def causal_depthwise_conv(x, w):
    k_len, ch = w.shape
    return lax.conv_general_dilated(x, w[:, None, :].astype(x.dtype), window_strides=(1,),
                                    padding=[(k_len - 1, 0)],
                                    dimension_numbers=('NWC', 'WIO', 'NWC'),
                                    feature_group_count=ch)


def rope_tables(length):
    inv = ROPE_THETA ** (-jnp.arange(0, MLA_ROPE, 2, dtype=jnp.float32) / MLA_ROPE)
    ang = jnp.arange(length, dtype=jnp.float32)[:, None] * inv[None, :]
    return jnp.cos(ang), jnp.sin(ang)


def apply_rope(x, cos, sin):
    xf = x.astype(jnp.float32)
    half = MLA_ROPE // 2
    x1, x2 = xf[..., :half], xf[..., half:]
    return jnp.concatenate([x1 * cos - x2 * sin, x2 * cos + x1 * sin], axis=-1).astype(x.dtype)


def gated_delta_rule_chunked(q, k, v, beta, g):
    B, L, H, dk = q.shape
    dv = v.shape[-1]
    C = GDN_CHUNK
    pad = (-L) % C
    n_chunks = (L + pad) // C

    def blocks(t):
        t = jnp.pad(t.astype(jnp.float32), [(0, 0), (pad, 0)] + [(0, 0)] * (t.ndim - 2))
        t = t.reshape((B, n_chunks, C) + t.shape[2:])
        return jnp.moveaxis(t, 3, 1)

    q, k, v, beta, g = blocks(q), blocks(k), blocks(v), blocks(beta), blocks(g)
    gc = jnp.cumsum(g, axis=-1)
    idx = jnp.arange(C)
    incl = idx[:, None] >= idx[None, :]
    strict = idx[:, None] > idx[None, :]
    decay = jnp.exp(jnp.where(incl, gc[..., :, None] - gc[..., None, :], -jnp.inf))

    kb = k * beta[..., None]
    vb = v * beta[..., None]
    m = jnp.einsum('bhnid,bhnjd->bhnij', kb, k) * jnp.where(strict, decay, 0.0)
    eye = jnp.eye(C, dtype=jnp.float32)
    rhs = jnp.concatenate([vb, kb * jnp.exp(gc)[..., None]], axis=-1)
    sol = lax.linalg.triangular_solve(m + eye, rhs, left_side=True, lower=True, unit_diagonal=True)
    u, w = sol[..., :dv], sol[..., dv:]

    attn = jnp.einsum('bhnid,bhnjd->bhnij', q, k) * decay
    q_dec = q * jnp.exp(gc)[..., None]
    k_dec = k * jnp.exp(gc[..., -1:] - gc)[..., None]
    g_last = jnp.exp(gc[..., -1])

    def step(state, xs):
        u_c, w_c, qd_c, kd_c, a_c, gl_c = xs
        v_new = u_c - jnp.einsum('bhcd,bhde->bhce', w_c, state)
        o_c = jnp.einsum('bhcd,bhde->bhce', qd_c, state) + jnp.einsum('bhij,bhje->bhie', a_c, v_new)
        state = state * gl_c[..., None, None] + jnp.einsum('bhcd,bhce->bhde', kd_c, v_new)
        return state, o_c

    xs = tuple(jnp.moveaxis(t, 2, 0) for t in (u, w, q_dec, k_dec, attn, g_last))
    s0 = jnp.zeros((B, H, dk, dv), jnp.float32)
    _, o = lax.scan(step, s0, xs)
    o = jnp.transpose(o, (1, 0, 3, 2, 4)).reshape(B, n_chunks * C, H, dv)
    return o[:, pad:]


def gdn_mixer(h, w_in, conv_w, a_log, dt_bias, out_norm, w_out):
    B, L, _ = h.shape
    proj = h @ w_in
    s1, s2, s3 = GDN_CONV_W, GDN_CONV_W + GDN_V_W, GDN_CONV_W + GDN_V_W + GDN_V_HEADS
    qkv, z, b, a = proj[..., :s1], proj[..., s1:s2], proj[..., s2:s3], proj[..., s3:]
    qkv = jax.nn.silu(causal_depthwise_conv(qkv, conv_w))
    q = l2norm(qkv[..., :GDN_QK_W].reshape(B, L, GDN_QK_HEADS, GDN_DK)) * (GDN_DK ** -0.5)
    k = l2norm(qkv[..., GDN_QK_W:2 * GDN_QK_W].reshape(B, L, GDN_QK_HEADS, GDN_DK))
    v = qkv[..., 2 * GDN_QK_W:].reshape(B, L, GDN_V_HEADS, GDN_DV)
    rep = GDN_V_HEADS // GDN_QK_HEADS
    q = jnp.repeat(q, rep, axis=2)
    k = jnp.repeat(k, rep, axis=2)
    beta = jax.nn.sigmoid(b.astype(jnp.float32))
    g = -jnp.exp(a_log.astype(jnp.float32)) * jax.nn.softplus(a.astype(jnp.float32) + dt_bias.astype(jnp.float32))
    o = gated_delta_rule_chunked(q, k, v, beta, g)
    o = rmsnorm(o, out_norm) * jax.nn.silu(z.astype(jnp.float32).reshape(B, L, GDN_V_HEADS, GDN_DV))
    return o.reshape(B, L, GDN_V_W).astype(h.dtype) @ w_out


def mla_shared_kv(h, kv_norm, kv_w_down, kv_latent_norm, kv_w_up, cos, sin):
    B, L, _ = h.shape
    ckr = rmsnorm(h, kv_norm) @ kv_w_down
    c_kv = rmsnorm(ckr[..., :MLA_KV_RANK], kv_latent_norm)
    k_rope = apply_rope(ckr[..., MLA_KV_RANK:], cos, sin)
    kv = (c_kv @ kv_w_up).reshape(B, L, MLA_HEADS, MLA_NOPE + MLA_V)
    return kv[..., :MLA_NOPE], k_rope, kv[..., MLA_NOPE:]


def causal_block_attention(q_nope, q_rope, k_nope, k_rope, v):
    B, L, H, _ = q_nope.shape
    n_blocks = -(-L // Q_BLOCK)
    pad = n_blocks * Q_BLOCK - L

    def blocks(t):
        t = jnp.pad(t, [(0, 0), (0, pad)] + [(0, 0)] * (t.ndim - 2))
        return jnp.moveaxis(t.reshape((B, n_blocks, Q_BLOCK) + t.shape[2:]), 1, 0)

    scale = MLA_QK ** -0.5
    k_pos = jnp.arange(L)

    def one_block(args):
        qn_b, qr_b, blk = args
        s = (jnp.einsum('bqhd,bkhd->bhqk', qn_b, k_nope, preferred_element_type=jnp.float32)
             + jnp.einsum('bqhr,bkr->bhqk', qr_b, k_rope, preferred_element_type=jnp.float32))
        q_pos = blk * Q_BLOCK + jnp.arange(Q_BLOCK)
        s = jnp.where(q_pos[:, None] >= k_pos[None, :], s * scale, -jnp.inf)
        p = jax.nn.softmax(s, axis=-1)
        return jnp.einsum('bhqk,bkhd->bqhd', p.astype(v.dtype), v)

    o = lax.map(one_block, (blocks(q_nope), blocks(q_rope), jnp.arange(n_blocks)))
    return jnp.moveaxis(o, 0, 1).reshape(B, n_blocks * Q_BLOCK, H, MLA_V)[:, :L]


def mla_mixer(h, w_in, q_latent_norm, w_q_up, w_out, k_nope, k_rope, v, cos, sin):
    B, L, _ = h.shape
    proj = h @ w_in
    c_q = rmsnorm(proj[..., :MLA_Q_RANK], q_latent_norm)
    z = proj[..., MLA_Q_RANK:]
    q = (c_q @ w_q_up).reshape(B, L, MLA_HEADS, MLA_QK)
    q_nope = q[..., :MLA_NOPE]
    q_rope = apply_rope(q[..., MLA_NOPE:], cos[:, None, :], sin[:, None, :])
    o = causal_block_attention(q_nope, q_rope, k_nope, k_rope, v).reshape(B, L, MLA_V_W)
    return (o * jax.nn.silu(z)) @ w_out


def setup_inputs(seed: int = 0) -> dict:
    key = jax.random.key(seed)
    ks = jax.random.split(key, 20)
    nrm = lambda k, shape, s: jax.random.normal(k, shape, jnp.float32) * s
    gain = lambda k, shape: 1.0 + 0.02 * jax.random.normal(k, shape, jnp.float32)
    dt = jnp.exp(jax.random.uniform(ks[5], (N_A_LAYERS, GDN_V_HEADS), jnp.float32,
                                    math.log(1e-3), math.log(1e-1)))
    return {
        "x": nrm(ks[0], (BATCH, SEQ, D_MODEL), 1.0),
        "meta_tokens": nrm(ks[1], (N_META, D_MODEL), 1.0),
        "pre_norm": gain(ks[2], (DEPTH, D_MODEL)),
        "post_norm": gain(ks[3], (DEPTH, D_MODEL)),
        "gdn_w_in": nrm(ks[4], (N_A_LAYERS, D_MODEL, GDN_IN_W), D_MODEL ** -0.5),
        "gdn_conv_w": nrm(ks[6], (N_A_LAYERS, GDN_CONV, GDN_CONV_W), GDN_CONV ** -0.5),
        "gdn_a_log": jnp.log(jax.random.uniform(ks[7], (N_A_LAYERS, GDN_V_HEADS), jnp.float32, 1.0, 16.0)),
        "gdn_dt_bias": dt + jnp.log(-jnp.expm1(-dt)),
        "gdn_out_norm": gain(ks[8], (N_A_LAYERS, GDN_DV)),
        "gdn_w_out": nrm(ks[9], (N_A_LAYERS, GDN_V_W, D_MODEL), GDN_V_W ** -0.5),
        "kv_norm": gain(ks[10], (D_MODEL,)),
        "kv_w_down": nrm(ks[11], (D_MODEL, MLA_KV_RANK + MLA_ROPE), D_MODEL ** -0.5),
        "kv_latent_norm": gain(ks[12], (MLA_KV_RANK,)),
        "kv_w_up": nrm(ks[13], (MLA_KV_RANK, MLA_HEADS * (MLA_NOPE + MLA_V)), MLA_KV_RANK ** -0.5),
        "mla_w_in": nrm(ks[14], (N_B_LAYERS, D_MODEL, MLA_IN_W), D_MODEL ** -0.5),
        "mla_q_latent_norm": gain(ks[15], (N_B_LAYERS, MLA_Q_RANK)),
        "mla_w_q_up": nrm(ks[16], (N_B_LAYERS, MLA_Q_RANK, MLA_HEADS * MLA_QK), MLA_Q_RANK ** -0.5),
        "mla_w_out": nrm(ks[17], (N_B_LAYERS, MLA_V_W, D_MODEL), MLA_V_W ** -0.5),
    }


def reference(x, meta_tokens, pre_norm, post_norm, gdn_w_in, gdn_conv_w, gdn_a_log, gdn_dt_bias,
              gdn_out_norm, gdn_w_out, kv_norm, kv_w_down, kv_latent_norm, kv_w_up,
              mla_w_in, mla_q_latent_norm, mla_w_q_up, mla_w_out):
    B = x.shape[0]
    meta = jnp.broadcast_to(meta_tokens[None].astype(x.dtype), (B, N_META, D_MODEL))
    h = jnp.concatenate([meta, x], axis=1)
    cos, sin = rope_tables(h.shape[1])
    shared_kv = None
    for layer in range(DEPTH):
        hn = rmsnorm(h, pre_norm[layer])
        if layer < N_A_LAYERS:
            y = gdn_mixer(hn, gdn_w_in[layer], gdn_conv_w[layer], gdn_a_log[layer], gdn_dt_bias[layer],
                          gdn_out_norm[layer], gdn_w_out[layer])
        else:
            if layer == N_A_LAYERS:
                shared_kv = mla_shared_kv(h, kv_norm, kv_w_down, kv_latent_norm, kv_w_up, cos, sin)
            j = layer - N_A_LAYERS
            k_nope, k_rope, v = shared_kv
            y = mla_mixer(hn, mla_w_in[j], mla_q_latent_norm[j], mla_w_q_up[j], mla_w_out[j],
                          k_nope, k_rope, v, cos, sin)
        h = h + rmsnorm(y, post_norm[layer])
    return h[:, N_META:]
```

```python
import contextlib
import numpy as np
import ml_dtypes
import concourse.bass as bass
import concourse.mybir as mybir
from concourse.bass_utils import run_bass_kernel_spmd

F32 = mybir.dt.float32
BF16 = mybir.dt.bfloat16
AF = mybir.ActivationFunctionType
ALU = mybir.AluOpType
EPS = 1e-6


class Tk:
    __slots__ = ("name", "w", "r")

    def __init__(self, name):
        self.name = name
        self.w = None
        self.r = []


class Prog:
    ENGS = ("pe", "act", "dve", "pool", "sp")

    def __init__(self, nc, stack, semst=None, pfx="", prew=()):
        self.nc = nc
        self.stack = stack
        self.semst = semst if semst is not None else stack
        self.pfx = pfx
        self.prew = list(prew)
        self.ops = {e: [] for e in self.ENGS}
        self.cnt = {e: 0 for e in self.ENGS}
        self.known = {e: {} for e in self.ENGS}
        self.sems = {}
        self.dcnt = {}
        for e in self.ENGS:
            self.sems[e] = self.semst.enter_context(nc.semaphore(pfx + "s_" + e))

    def final_events(self):
        ev = [(self.sems[e], self.cnt[e]) for e in self.ENGS if self.cnt[e] > 0]
        ev += [(self.sems[n], c) for n, c in self.dcnt.items() if c > 0]
        return ev

    def sb(self, name, shape, dt):
        return self.stack.enter_context(self.nc.sbuf_tensor(self.pfx + name, list(shape), dt))

    def ps(self, name, shape, dt):
        return self.stack.enter_context(self.nc.psum_tensor(self.pfx + name, list(shape), dt))

    def dmasem(self, name):
        self.sems[name] = self.semst.enter_context(self.nc.semaphore(self.pfx + "d_" + name))
        self.dcnt[name] = 0
        return name

    def _need(self, eng, ev, waits):
        if ev is None:
            return
        key, val = ev
        if key == eng and eng == "pe":
            return
        if self.known[eng].get(key, 0) >= val:
            return
        if key in self.ENGS and key != eng:
            assert self.cnt[key] >= val, (eng, ev, self.cnt[key])
        self.known[eng][key] = val
        for i, (k, v) in enumerate(waits):
            if k == key:
                waits[i] = (k, max(v, val))
                return
        waits.append((key, val))

    def _deps(self, eng, reads, writes):
        waits = []
        for t in reads:
            self._need(eng, t.w, waits)
        for t in writes:
            self._need(eng, t.w, waits)
            for ev in t.r:
                self._need(eng, ev, waits)
        return waits

    def _mark(self, ev, reads, writes):
        for t in writes:
            t.w = ev
            t.r = []
        for t in reads:
            if t not in writes:
                t.r.append(ev)
                if len(t.r) > 8:
                    d = {}
                    for k, v in t.r:
                        d[k] = max(d.get(k, 0), v)
                    t.r = list(d.items())

    def op(self, eng, fn, reads=(), writes=(), inc=True):
        waits = self._deps(eng, reads, writes)
        if inc:
            self.cnt[eng] += 1
            ev = (eng, self.cnt[eng])
        else:
            ev = (eng, self.cnt[eng] + 1)
        self._mark(ev, reads, writes)
        self.ops[eng].append((fn, waits, ("c", inc)))

    def dma(self, q, sem, fn, reads=(), writes=()):
        waits = self._deps(q, reads, writes)
        self.dcnt[sem] += 16
        ev = (sem, self.dcnt[sem])
        self._mark(ev, reads, writes)
        self.ops[q].append((fn, waits, ("d", sem)))

    def cc(self, semname, hsem, src, dst, reads=(), writes=()):
        if semname not in self.sems:
            self.sems[semname] = hsem
            self.dcnt[semname] = 0
        waits = self._deps("pool", reads, writes)
        self.dcnt[semname] += 1
        ev = (semname, self.dcnt[semname])
        self._mark(ev, reads, writes)
        fn = lambda e: e.collective_compute("AllReduce", ALU.add, replica_groups=GROUPS, ins=[src], outs=[dst])
        self.ops["pool"].append((fn, waits, ("k", semname)))

    def finish(self, eng, tks):
        waits = []
        for t in tks:
            self._need(eng, t.w, waits)
            for ev in t.r:
                self._need(eng, ev, waits)
        self.ops[eng].append((None, waits, ("w", None)))

    def emit(self):
        nc, sems, ops = self.nc, self.sems, self.ops
        prew = self.prew
        with nc.Block() as block:
            def run(name, e):
                for hsem, v in prew:
                    e.wait_ge(hsem, v)
                for fn, waits, kind in ops[name]:
                    for k, v in waits:
                        e.wait_ge(sems[k], v)
                    if fn is None:
                        continue
                    ins = fn(e)
                    if kind[0] == "c":
                        if kind[1]:
                            ins.then_inc(sems[name], 1)
                    elif kind[0] == "k":
                        ins.then_inc(sems[kind[1]], 1)
                    else:
                        ins.then_inc(sems[kind[1]], 16)

            @block.tensor
            def _(e):
                run("pe", e)

            @block.scalar
            def _(e):
                run("act", e)

            @block.vector
            def _(e):
                run("dve", e)

            @block.gpsimd
            def _(e):
                run("pool", e)

            @block.sync
            def _(e):
                run("sp", e)


class K:
    def __init__(self, P):
        self.P = P

    def act(self, out, in_, func, r, w, **kw):
        self.P.op("act", lambda e: e.activation(out=out, in_=in_, func=func, **kw), r, w)

    def tt(self, out, in0, in1, op, r, w, eng="dve"):
        self.P.op(eng, lambda e: e.tensor_tensor(out=out, in0=in0, in1=in1, op=op), r, w)

    def ts(self, out, in0, s1, op0, r, w, s2=None, op1=None, eng="dve"):
        if op1 is None:
            self.P.op(eng, lambda e: e.tensor_scalar(out=out, in0=in0, scalar1=s1, scalar2=None, op0=op0), r, w)
        else:
            self.P.op(eng, lambda e: e.tensor_scalar(out=out, in0=in0, scalar1=s1, scalar2=s2, op0=op0, op1=op1), r, w)

    def stt(self, out, in0, scalar, in1, op0, op1, r, w, eng="dve"):
        self.P.op(eng, lambda e: e.scalar_tensor_tensor(out=out, in0=in0, scalar=scalar, in1=in1, op0=op0, op1=op1), r, w)

    def cp(self, out, in_, r, w, eng="dve"):
        self.P.op(eng, lambda e: e.tensor_copy(out=out, in_=in_), r, w)

    def rcp(self, out, in_, r, w):
        self.P.op("dve", lambda e: e.reciprocal(out=out, in_=in_), r, w)

    def mm(self, out, lhsT, rhs, r, w, start=True, stop=True, inc=True, sgc=False):
        self.P.op("pe", lambda e: e.matmul(out, lhsT=lhsT, rhs=rhs, start=start, stop=stop, skip_group_check=sgc), r, w, inc=inc)

    def tr(self, out, in_, ident, r, w, inc=True):
        self.P.op("pe", lambda e: e.transpose(out, in_, ident), r, w, inc=inc)

    def ld(self, sem, out, in_, w, r=(), q="sp"):
        self.P.dma(q, sem, lambda e: e.dma_start(out=out, in_=in_), r, w)


def _consts():
    c = np.zeros((128, 128 * 2 + 64 * 3), np.float32)
    c[:, 0:128] = np.eye(128, dtype=np.float32)
    c[:, 128:256] = 1.0
    kk = np.arange(64)
    c[0:64, 256:320] = (kk[:, None] <= kk[None, :])
    c[0:64, 320:384] = (kk[None, :] >= kk[:, None])
    c[0:64, 384:448] = (kk[None, :] > kk[:, None])
    return c


def gdn_decl(nc, NTOK):
    def din(n, s, dt=F32):
        return nc.dram_tensor(n, list(s), dt, kind="ExternalInput").ap()
    io = {}
    io["xp"] = din("xp", [NTOK, 1024])
    io["gpre"] = din("gpre", [1, 1024])
    io["wq"] = din("wq", [1024, 1536])
    io["wba"] = din("wba", [1024, 8])
    io["convw"] = din("convw", [128, 32])
    io["alog"] = din("alog", [1, 4])
    io["dtb"] = din("dtb", [1, 4])
    io["onorm"] = din("onorm", [128, 1])
    io["cst"] = din("cst", [128, 448])
    return io


def build_gdn(NTOK):
    assert NTOK % 128 == 0
    nc = bass.Bass("TRN2", target_bir_lowering=False)
    io = gdn_decl(nc, NTOK)
    io["oT"] = nc.dram_tensor("oT", [512, NTOK], BF16, kind="ExternalOutput").ap()
    emit_gdn(nc, None, io, NTOK)
    return nc


def emit_gdn(nc, semst, io, NTOK, prew=(), fz=None):
    xp, gpre, wq, wba, convw, alog, dtb, onorm, cst = (io[n] for n in ("xp", "gpre", "wq", "wba", "convw", "alog", "dtb", "onorm", "cst"))
    oT = io.get("oT")
    with contextlib.ExitStack() as st:
        P = Prog(nc, st, semst, "g", prew)
        k = K(P)
        sb, ps = P.sb, P.ps
        if fz is not None:
            w0_b = sb("w0_b", [128, 4, 1024], BF16)
            yst = sb("yst", [128, 2, 1024], F32)
        wb = sb("wb", [128, 8, 1536], BF16)
        wbab = sb("wbab", [128, 8, 8], BF16)
        gpre_b = sb("gpre_b", [128, 1024], F32)
        convw_s = sb("convw_s", [128, 32], F32)
        alog_b = sb("alog_b", [64, 4], F32)
        dtb_b = sb("dtb_b", [64, 4], F32)
        negA = sb("negA", [64, 4], F32)
        onorm_s = sb("onorm_s", [128, 1], F32)
        cst_s = sb("cst_s", [128, 448], F32)
        ident_b = sb("ident_b", [128, 128], BF16)
        xc = sb("xc", [128, 8, 3 + 512], F32)
        S_f = sb("S_f", [128, 4, 128], F32)
        S_b = sb("S_b", [128, 4, 128], BF16)
        ident_f = cst_s[:, 0:128]
        ones_f = cst_s[:, 128:256]
        tri = cst_s[0:64, 256:320]
        mincl = cst_s[0:64, 320:384]
        mstrict = cst_s[0:64, 384:448]
        xs = sb("xs", [128, 4, 1024], F32)
        wstage = xs[:].rearrange("p (a b) d -> p a (b d)", a=2)
        junk = sb("junk", [128, 1024], BF16)
        ss = sb("ss", [128, 4], F32)
        rstd = sb("rstd", [128, 4], F32)
        hn = sb("hn", [128, 2, 1024], BF16)
        hnT = sb("hnT", [128, 8, 512], BF16)
        acc = sb("acc", [128, 512], F32)
        qkf = sb("qkf", [128, 512], F32)
        sq = sb("sq", [128, 512], F32)
        rtmp = sb("rtmp", [128, 512], F32)
        qT = sb("qT", [128, 2, 512], BF16)
        kT2 = sb("kT2", [128, 2, 2, 512], BF16)
        vT = sb("vT", [128, 4, 512], BF16)
        zs2 = sb("zs2", [128, 2, 4, 512], F32)
        bet = sb("bet", [64, 8, 4], F32)
        gt = sb("gt", [64, 8, 4], F32)
        gg = sb("gg", [64, 8, 4], F32)
        gc = sb("gc", [64, 8, 4], F32)
        kap = sb("kap", [64, 8, 4], F32)
        ngam = sb("ngam", [64, 8, 4], F32)
        bk = sb("bk", [64, 8, 4], F32)
        glast = sb("glast", [128, 8, 4], F32)
        gB = sb("gB", [64, 8, 128], F32)
        egc = sb("egc", [128, 512], F32)
        qdT = sb("qdT", [128, 4, 512], BF16)
        attnT = sb("attnT", [64, 4, 512], BF16)
        U = sb("U", [64, 4, 512], F32)
        UT = sb("UT", [64, 4, 2, 512], F32)
        Rr = sb("Rr", [64, 4, 512], F32)
        Rb = sb("Rb", [64, 4, 512], BF16)
        ktok = sb("ktok", [64, 2, 8, 128], BF16)
        vtok = sb("vtok", [64, 4, 8, 128], BF16)
        rr = sb("rr", [64, 4, 128], BF16)
        vn = sb("vn", [64, 4, 128], BF16)
        vnk = sb("vnk", [64, 4, 128], BF16)
        osq = sb("osq", [128, 512], F32)
        otmp = sb("otmp", [128, 512], F32)
        og = sb("og", [128, 4, 128], BF16)
        ps_tr = ps("ps_tr", [128, 1024], BF16)
        ps_pj = ps("ps_pj", [128, 512], F32)
        ps_ms = ps("ps_ms", [128, 512], F32)
        ps_bc = ps("ps_bc", [128, 512], F32)
        ps_sc = ps("ps_sc", [128, 4, 512], F32)
        B_TR, B_PJ, B_MS, B_BC = Tk("B_TR"), Tk("B_PJ"), Tk("B_MS"), Tk("B_BC")
        B_S = [Tk("B_S%d" % h) for h in range(4)]

        T = {}

        def tk(n):
            if n not in T:
                T[n] = Tk(n)
            return T[n]

        s_c = P.dmasem("c")
        s_w = [P.dmasem("w0"), P.dmasem("w1")]
        s_x = P.dmasem("x")
        s_o = P.dmasem("o")
        s_y = [P.dmasem("y0"), P.dmasem("y1")]
        k.ld(s_c, cst_s[:], cst[:, :], [tk("cst")])
        k.ld(s_c, gpre_b[:], gpre[0:1, :].partition_broadcast(128), [tk("gpre")])
        k.ld(s_c, convw_s[:], convw[:, :], [tk("convw")])
        k.ld(s_c, alog_b[:], alog[0:1, :].partition_broadcast(64), [tk("alog")])
        k.ld(s_c, dtb_b[:], dtb[0:1, :].partition_broadcast(64), [tk("dtb")])
        k.ld(s_c, onorm_s[:], onorm[:, :], [tk("onorm")])
        for _n in ("cst", "gpre", "convw", "alog", "dtb", "onorm"):
            tk(_n).w = (s_c, P.dcnt[s_c])
        k.cp(ident_b[:], ident_f, [tk("cst")], [tk("identb")])
        k.act(negA[:], alog_b[:], AF.Exp, [tk("alog")], [tk("negA")])
        k.ts(negA[:], negA[:], -1.0, ALU.mult, [], [tk("negA")])
        for kc in range(8):
            sl = kc % 2
            k.ld(s_w[sl], wstage[:, sl, 0:1536], wq[kc * 128:(kc + 1) * 128, :], [tk("wst%d" % sl)])
            k.ld(s_w[sl], wstage[:, sl, 1536:1544], wba[kc * 128:(kc + 1) * 128, :], [tk("wst%d" % sl)])
            k.cp(wb[:, kc, :], wstage[:, sl, 0:1536], [tk("wst%d" % sl)], [tk("wb")], eng="pool")
            k.cp(wbab[:, kc, :], wstage[:, sl, 1536:1544], [tk("wst%d" % sl)], [tk("wb")], eng="pool")
        if fz is not None:
            for h in range(4):
                sl = h % 2
                k.ld(s_w[sl], wstage[:, sl, 0:1024], fz["w0"][h * 128:(h + 1) * 128, :], [tk("wst%d" % sl)])
                k.cp(w0_b[:, h, :], wstage[:, sl, 0:1024], [tk("wst%d" % sl)], [tk("w0b")], eng="pool")
        P.op("dve", lambda e: e.memset(S_f[:], 0.0), [], [tk("S_f0"), tk("S_f1"), tk("S_f2"), tk("S_f3")])
        P.op("dve", lambda e: e.memset(S_b[:], 0.0), [], [tk("S_b0"), tk("S_b1"), tk("S_b2"), tk("S_b3")])
        P.op("dve", lambda e: e.memset(xc[:], 0.0), [], [tk("xc%d" % g) for g in range(8)])

        def gen_AB(ti):
            t0 = ti * 512
            TT = min(512, NTOK - t0)
            NS = TT // 128
            p = ti % 2
            kT = kT2[:, p]
            zs = zs2[:, p]
            kTn = "kT%d_" % p + "%d"
            k.ld(s_x, xs[:, 0:NS, :], xp[t0:t0 + TT, :].rearrange("(s p) d -> p s d", p=128),
                 [tk("xs")] + ([tk("wst0"), tk("wst1")] if ti == 0 else []))
            for s in range(NS):
                k.act(junk[:], xs[:, s, :], AF.Square, [tk("xs")], [tk("junk"), tk("ss")], accum_out=ss[:, s:s + 1])
            k.act(rstd[:, 0:NS], ss[:, 0:NS], AF.Sqrt, [tk("ss")], [tk("rstd")], scale=1.0 / 1024, bias=EPS)
            k.rcp(rstd[:, 0:NS], rstd[:, 0:NS], [], [tk("rstd")])
            yield
            for s in range(NS):
                k.stt(hn[:, s % 2, :], xs[:, s, :], rstd[:, s:s + 1], gpre_b[:], ALU.mult, ALU.mult,
                      [tk("xs"), tk("rstd"), tk("gpre")], [tk("hn%d" % (s % 2))])
                for kc in range(8):
                    k.tr(ps_tr[:, kc * 128:(kc + 1) * 128], hn[:, s % 2, kc * 128:(kc + 1) * 128], ident_b[:],
                         [tk("hn%d" % (s % 2)), tk("identb")], [B_TR], inc=(kc == 7))
                k.act(hnT[:, :, s * 128:(s + 1) * 128], ps_tr[:].rearrange("p (k t) -> p k t", t=128), AF.Identity,
                      [], [B_TR, tk("hnT")])
                yield
            for g in range(12):
                for kc in range(8):
                    k.mm(ps_pj[:, 0:TT], wb[:, kc, g * 128:(g + 1) * 128], hnT[:, kc, 0:TT], [tk("wb"), tk("hnT")], [B_PJ],
                         start=(kc == 0), stop=(kc == 7), inc=(kc == 7))
                if g < 8:
                    xg = tk("xc%d" % g)
                    k.cp(xc[:, g, 0:3], xc[:, g, 512:515], [], [xg])
                    k.act(xc[:, g, 3:3 + TT], ps_pj[:, 0:TT], AF.Identity, [], [B_PJ, xg])
                    k.ts(acc[:, 0:TT], xc[:, g, 3:3 + TT], convw_s[:, g * 4 + 3:g * 4 + 4], ALU.mult, [xg, tk("convw")], [tk("acc")])
                    for j in (2, 1, 0):
                        k.stt(acc[:, 0:TT], xc[:, g, j:j + TT], convw_s[:, g * 4 + j:g * 4 + j + 1], acc[:, 0:TT], ALU.mult, ALU.add,
                              [xg, tk("convw")], [tk("acc")])
                    if TT < 512:
                        pass
                    if g < 4:
                        dst = qT[:, g, 0:TT] if g < 2 else kT[:, g - 2, 0:TT]
                        dtk = tk("qT%d" % g) if g < 2 else tk(kTn % (g - 2))
                        k.act(qkf[:, 0:TT], acc[:, 0:TT], AF.Silu, [tk("acc")], [tk("qkf")])
                        k.act(sq[:, 0:TT], qkf[:, 0:TT], AF.Square, [tk("qkf")], [tk("sq")])
                        k.mm(ps_bc[:, 0:TT], ones_f, sq[:, 0:TT], [tk("cst"), tk("sq")], [B_BC])
                        if g < 2:
                            k.act(rtmp[:, 0:TT], ps_bc[:, 0:TT], AF.Ln, [], [B_BC, tk("rtmp")], scale=128.0, bias=128.0 * EPS)
                        else:
                            k.act(rtmp[:, 0:TT], ps_bc[:, 0:TT], AF.Ln, [], [B_BC, tk("rtmp")], scale=1.0, bias=EPS)
                        k.act(rtmp[:, 0:TT], rtmp[:, 0:TT], AF.Exp, [], [tk("rtmp")], scale=-0.5)
                        k.tt(dst, qkf[:, 0:TT], rtmp[:, 0:TT], ALU.mult, [tk("qkf"), tk("rtmp")], [dtk])
                    else:
                        k.act(vT[:, g - 4, 0:TT], acc[:, 0:TT], AF.Silu, [tk("acc")], [tk("vT%d" % (g - 4))])
                else:
                    k.act(zs[:, g - 8, 0:TT], ps_pj[:, 0:TT], AF.Silu, [], [B_PJ, tk("zs%d" % p)])
                yield

        def step(g_, n=1):
            if g_ is None:
                return
            for _ in range(n):
                try:
                    next(g_)
                except StopIteration:
                    return

        def drain(g_):
            if g_ is None:
                return
            for _ in g_:
                pass

        ntiles = (NTOK + 511) // 512
        cc_next = [0]
        drain(gen_AB(0))
        for ti in range(ntiles):
            t0 = ti * 512
            TT = min(512, NTOK - t0)
            NS = TT // 128
            NCH = TT // 64
            p = ti % 2
            kT = kT2[:, p]
            zs = zs2[:, p]
            kTn = "kT%d_" % p + "%d"
            g_next = gen_AB(ti + 1) if ti + 1 < ntiles else None
            for n in range(NCH):
                for kc in range(8):
                    k.mm(ps_ms[0:64, n * 8:(n + 1) * 8], hnT[:, kc, n * 64:(n + 1) * 64], wbab[:, kc, :], [tk("hnT"), tk("wb")], [B_MS],
                         start=(kc == 0), stop=(kc == 7), inc=(kc == 7 and n == NCH - 1))
            bav = ps_ms[0:64, 0:NCH * 8].rearrange("p (n c) -> p n c", c=8)
            k.act(bet[:, 0:NCH, :], bav[:, :, 0:4], AF.Sigmoid, [], [B_MS, tk("bet")])
            k.tt(gt[:, 0:NCH, :], bav[:, :, 4:8], dtb_b[:].unsqueeze(1).to_broadcast([64, NCH, 4]), ALU.add, [tk("dtb")], [B_MS, tk("gt")])
            k.act(gt[:, 0:NCH, :], gt[:, 0:NCH, :], AF.Exp, [], [tk("gt")])
            k.act(gt[:, 0:NCH, :], gt[:, 0:NCH, :], AF.Ln, [], [tk("gt")], bias=1.0)
            k.tt(gg[:, 0:NCH, :], gt[:, 0:NCH, :], negA[:].unsqueeze(1).to_broadcast([64, NCH, 4]), ALU.mult, [tk("gt"), tk("negA")], [tk("gg")])
            ggf = gg[:, 0:NCH, :].rearrange("p n h -> p (n h)")
            k.mm(ps_ms[0:64, 64:64 + NCH * 4], tri, ggf, [tk("cst"), tk("gg")], [B_MS])
            k.mm(ps_ms[:, 128:128 + NCH * 4], ones_f[0:64, :], ggf, [tk("cst"), tk("gg")], [B_MS])
            gcv = ps_ms[0:64, 64:64 + NCH * 4].rearrange("p (n h) -> p n h", h=4)
            glv = ps_ms[:, 128:128 + NCH * 4].rearrange("p (n h) -> p n h", h=4)
            k.cp(gc[:, 0:NCH, :], gcv, [], [B_MS, tk("gc")])
            k.act(glast[:, 0:NCH, :], glv, AF.Exp, [], [B_MS, tk("glast")])
            k.tt(kap[:, 0:NCH, :], glv[0:64], gc[:, 0:NCH, :], ALU.subtract, [tk("gc")], [B_MS, tk("kap")])
            k.act(kap[:, 0:NCH, :], kap[:, 0:NCH, :], AF.Exp, [], [tk("kap")])
            k.act(ngam[:, 0:NCH, :], gc[:, 0:NCH, :], AF.Exp, [tk("gc")], [tk("ngam")])
            k.ts(ngam[:, 0:NCH, :], ngam[:, 0:NCH, :], -1.0, ALU.mult, [], [tk("ngam")])
            k.tt(bk[:, 0:NCH, :], bet[:, 0:NCH, :], kap[:, 0:NCH, :], ALU.mult, [tk("bet"), tk("kap")], [tk("bk")])
            for j in range(6):
                src = kT[:, j, :] if j < 2 else vT[:, j - 2, :]
                stk = tk(kTn % j) if j < 2 else tk("vT%d" % (j - 2))
                for n in range(NCH):
                    k.tr(ps_tr[0:64, n * 128:(n + 1) * 128], src[:, n * 64:(n + 1) * 64], ident_b[:], [stk, tk("identb")], [B_TR],
                         inc=(n == NCH - 1))
                dst = ktok[:, j, 0:NCH, :] if j < 2 else vtok[:, j - 2, 0:NCH, :]
                dtk = tk("ktok%d" % j) if j < 2 else tk("vtok%d" % (j - 2))
                k.act(dst, ps_tr[0:64, 0:NCH * 128].rearrange("p (n d) -> p n d", d=128), AF.Identity, [], [B_TR, dtk])
            W = NCH * 64
            v3 = lambda ap_: ap_.rearrange("p (n c) -> p n c", c=64)
            for h in range(4):
                k.cp(gB[:, 0:NCH, :], gg[:, 0:NCH, h:h + 1].to_broadcast([64, NCH, 128]), [tk("gg")], [tk("gB")])
                for n in range(NCH):
                    k.mm(ps_sc[:, h, n * 64:(n + 1) * 64], gB[:, n, :], tri, [tk("gB"), tk("cst")], [B_S[h]], inc=(n == NCH - 1))
            for h in range(4):
                qh = h // 2
                t1h = UT[:, h, 1, 0:W]
                k.act(egc[:, 0:W], ps_sc[:, h, 0:W], AF.Exp, [], [B_S[h], tk("egc")])
                k.tt(qdT[:, h, 0:W], qT[:, qh, 0:W], egc[:, 0:W], ALU.mult, [tk("qT%d" % qh), tk("egc")], [tk("qdT%d" % h)])
                k.tt(v3(t1h), v3(ps_sc[0:64, h, 0:W]), gc[:, 0:NCH, h:h + 1].to_broadcast([64, NCH, 64]),
                     ALU.subtract, [tk("gc")], [B_S[h], tk("UT%d_1" % h)])
            for h in range(4):
                t1h, Dmh, Bsh = UT[:, h, 1, 0:W], Rr[:, h, 0:W], UT[:, h, 0, 0:W]
                k.ts(t1h, t1h, 0.0, ALU.min, [], [tk("UT%d_1" % h)])
                k.act(t1h, t1h, AF.Exp, [], [tk("UT%d_1" % h)])
                k.tt(v3(Dmh), v3(t1h), mincl.unsqueeze(1).to_broadcast([64, NCH, 64]), ALU.mult, [tk("UT%d_1" % h), tk("cst")], [tk("R%d" % h)])
                k.tt(v3(Bsh), mstrict.unsqueeze(1).to_broadcast([64, NCH, 64]), bet[:, 0:NCH, h:h + 1].to_broadcast([64, NCH, 64]), ALU.mult,
                     [tk("cst"), tk("bet")], [tk("UT%d_0" % h)])
            for h in range(4):
                qh = h // 2
                for n in range(NCH):
                    k.mm(ps_sc[0:64, h, n * 64:(n + 1) * 64], kT[:, qh, n * 64:(n + 1) * 64], qT[:, qh, n * 64:(n + 1) * 64],
                         [tk(kTn % qh), tk("qT%d" % qh)], [B_S[h]], inc=(n == NCH - 1))
                k.tt(attnT[:, h, 0:W], ps_sc[0:64, h, 0:W], Rr[:, h, 0:W], ALU.mult, [tk("R%d" % h)], [B_S[h], tk("attnT%d" % h)])
            for h in range(4):
                qh = h // 2
                for n in range(NCH):
                    k.mm(ps_sc[0:64, h, n * 64:(n + 1) * 64], kT[:, qh, n * 64:(n + 1) * 64], kT[:, qh, n * 64:(n + 1) * 64],
                         [tk(kTn % qh)], [B_S[h]], inc=(n == NCH - 1))
                k.tt(U[:, h, 0:W], ps_sc[0:64, h, 0:W], Rr[:, h, 0:W], ALU.mult, [tk("R%d" % h)], [B_S[h], tk("U%d" % h)])
                k.tt(U[:, h, 0:W], U[:, h, 0:W], UT[:, h, 0, 0:W], ALU.mult, [tk("UT%d_0" % h)], [tk("U%d" % h)])
            for h in range(4):
                for n in range(NCH):
                    k.tr(ps_sc[0:64, h, n * 64:(n + 1) * 64], U[:, h, n * 64:(n + 1) * 64], ident_f[0:64, 0:64], [tk("U%d" % h), tk("cst")], [B_S[h]],
                         inc=(n == NCH - 1))
                k.cp(UT[:, h, 0, 0:W], ps_sc[0:64, h, 0:W], [], [B_S[h], tk("UT%d_0" % h)])
            for h in range(4):
                k.stt(v3(Rr[:, h, 0:W]), v3(U[:, h, 0:W]), -1.0,
                      ident_f[0:64, 0:64].unsqueeze(1).to_broadcast([64, NCH, 64]), ALU.mult, ALU.add, [tk("U%d" % h), tk("cst")], [tk("R%d" % h)])
            W = NCH * 64
            cur = 0
            for lvl in range(1, 6):
                nxt = 1 - cur
                last = (lvl == 5)
                for h in range(4):
                    for n in range(NCH):
                        c = slice(n * 64, (n + 1) * 64)
                        k.mm(ps_sc[0:64, h, c], U[:, h, c], UT[:, h, cur, c], [tk("U%d" % h), tk("UT%d_%d" % (h, cur))], [B_S[h]], inc=(n == NCH - 1))
                    k.cp(UT[:, h, nxt, 0:W], ps_sc[0:64, h, 0:W], [], [B_S[h], tk("UT%d_%d" % (h, nxt))])
                step(g_next)
                if not last:
                    for h in range(4):
                        for n in range(NCH):
                            c = slice(n * 64, (n + 1) * 64)
                            k.mm(ps_sc[0:64, h, c], UT[:, h, cur, c], U[:, h, c], [tk("U%d" % h), tk("UT%d_%d" % (h, cur))], [B_S[h]], inc=(n == NCH - 1))
                        k.act(U[:, h, 0:W], ps_sc[0:64, h, 0:W], AF.Identity, [], [B_S[h], tk("U%d" % h)])
                    step(g_next)
                for h in range(4):
                    for n in range(NCH):
                        c = slice(n * 64, (n + 1) * 64)
                        k.mm(ps_sc[0:64, h, c], UT[:, h, nxt, c], Rr[:, h, c], [tk("UT%d_%d" % (h, nxt)), tk("R%d" % h)], [B_S[h]], inc=(n == NCH - 1))
                    if not last:
                        k.tt(Rr[:, h, 0:W], ps_sc[0:64, h, 0:W], Rr[:, h, 0:W], ALU.add, [], [B_S[h], tk("R%d" % h)])
                    else:
                        k.tt(Rb[:, h, 0:W], ps_sc[0:64, h, 0:W], Rr[:, h, 0:W], ALU.add, [tk("R%d" % h)], [B_S[h], tk("Rb%d" % h)])
                cur = nxt
                step(g_next)
            drain(g_next)
            for n in range(NCH):
                c = slice(n * 64, (n + 1) * 64)
                par = n % 2
                for h in range(4):
                    qh = h // 2
                    k.mm(ps_sc[0:64, h, 0:128], kT[:, qh, c], S_b[:, h, :], [tk(kTn % qh), tk("S_b%d" % h)], [B_S[h]])
                for h in range(4):
                    k.stt(rr[:, h, :], ps_sc[0:64, h, 0:128], ngam[:, n, h:h + 1], vtok[:, h, n, :], ALU.mult, ALU.add,
                          [tk("ngam"), tk("vtok%d" % h)], [B_S[h], tk("rr%d" % h)])
                for h in range(4):
                    k.mm(ps_sc[0:64, h, 128:256], Rb[:, h, c], rr[:, h, :], [tk("Rb%d" % h), tk("rr%d" % h)], [B_S[h]])
                for h in range(4):
                    k.act(vn[:, h, :], ps_sc[0:64, h, 128:256], AF.Identity, [tk("bet")], [B_S[h], tk("vn%d" % h)], scale=bet[:, n, h:h + 1])
                    k.ts(vnk[:, h, :], ps_sc[0:64, h, 128:256], bk[:, n, h:h + 1], ALU.mult, [tk("bk")], [B_S[h], tk("vnk%d" % h)])
                for h in range(4):
                    qh = h // 2
                    oc = slice(384 + par * 64, 384 + par * 64 + 64)
                    k.mm(ps_sc[:, h, oc], S_b[:, h, :], qdT[:, h, c], [tk("S_b%d" % h), tk("qdT%d" % h)], [B_S[h]], start=True, stop=False, inc=False)
                    k.mm(ps_sc[:, h, oc], vn[:, h, :], attnT[:, h, c], [tk("vn%d" % h), tk("attnT%d" % h)], [B_S[h]], start=False, stop=True, inc=False)
                    k.mm(ps_sc[:, h, 256:384], ktok[:, qh, n, :], vnk[:, h, :], [tk("ktok%d" % qh), tk("vnk%d" % h)], [B_S[h]])
                for h in range(4):
                    k.stt(S_f[:, h, :], S_f[:, h, :], glast[:, n, h:h + 1], ps_sc[:, h, 256:384], ALU.mult, ALU.add,
                          [tk("glast")], [B_S[h], tk("S_f%d" % h)])
                    k.act(S_b[:, h, :], S_f[:, h, :], AF.Identity, [tk("S_f%d" % h)], [tk("S_b%d" % h)])
                if par == 1:
                    ov = ps_sc[:, :, 384:512]
                    tc0 = (n - 1) * 64
                    k.act(osq[:].rearrange("p (h t) -> p h t", t=128), ov, AF.Square, [], B_S + [tk("osq")])
                    k.mm(ps_bc[:, :], ones_f, osq[:], [tk("cst"), tk("osq")], [B_BC])
                    k.act(otmp[:], ps_bc[:], AF.Ln, [], [B_BC, tk("otmp")], scale=1.0 / 128, bias=EPS)
                    k.act(otmp[:], otmp[:], AF.Exp, [], [tk("otmp")], scale=-0.5)
                    k.tt(otmp[:].rearrange("p (h t) -> p h t", t=128), ov, otmp[:].rearrange("p (h t) -> p h t", t=128), ALU.mult,
                         [], B_S + [tk("otmp")])
                    k.stt(og[:], otmp[:].rearrange("p (h t) -> p h t", t=128), onorm_s[:, 0:1], zs[:, :, tc0:tc0 + 128], ALU.mult, ALU.mult,
                          [tk("otmp"), tk("onorm"), tk("zs%d" % p)], [tk("og")])
                    if fz is None:
                        k.ld(s_o, oT[:, t0 + tc0:t0 + tc0 + 128].rearrange("(h e) t -> e h t", e=128), og[:], [], r=[tk("og")])
                    else:
                        ysl = (t0 + tc0) // 128 % 2
                        for half, (pst, btk) in enumerate(((ps_pj, B_PJ), (ps_ms, B_MS))):
                            for h in range(4):
                                k.mm(pst[:, :], og[:, h, :], w0_b[:, h, half * 512:(half + 1) * 512], [tk("og"), tk("w0b")], [btk],
                                     start=(h == 0), stop=(h == 3), inc=(h == 3))
                        k.act(yst[:, ysl, 0:512], ps_pj[:, :], AF.Identity, [], [B_PJ, tk("yst%d" % ysl)])
                        k.cp(yst[:, ysl, 512:1024], ps_ms[:, :], [], [B_MS, tk("yst%d" % ysl)])
                        pos0 = t0 + tc0 - 48
                        nrows = fz["y0p"].shape[0]
                        if pos0 < 0:
                            k.ld(s_y[ysl], fz["y0p"][0:128 + pos0, :], yst[-pos0:128, ysl, :], [], r=[tk("yst%d" % ysl), tk("y0rows")])
                            rend = 128 + pos0
                        else:
                            nr = min(128, nrows - pos0)
                            rend = pos0 + max(nr, 0)
                            if nr > 0:
                                k.ld(s_y[ysl], fz["y0p"][pos0:pos0 + nr, :], yst[0:nr, ysl, :], [], r=[tk("yst%d" % ysl), tk("y0rows")])
                        while cc_next[0] < nrows and rend >= min(nrows, cc_next[0] + CC_ROWS):
                            r0, r1 = cc_next[0], min(nrows, cc_next[0] + CC_ROWS)
                            P.cc("cc", fz["scc"], fz["y0p"][r0:r1, :], fz["y0f"][r0:r1, :], [], [tk("y0rows")])
                            cc_next[0] = r1
        P.finish("sp", [tk("og")] + ([tk("yst0"), tk("yst1")] if fz is not None else []))
        P.emit()
        return P.final_events()


def gdn_inputs(inp, core, NTOK):
    b, hg = core // 4, core % 4
    L = 16 + inp["x"].shape[1]
    xp = np.zeros((NTOK, 1024), np.float32)
    n_real = min(L, NTOK - 48)
    xp[48:64] = inp["meta_tokens"]
    xp[64:48 + n_real] = inp["x"][b, :n_real - 16]
    W = inp["gdn_w_in"][0]
    qcols = np.arange(2 * hg * 128, (2 * hg + 2) * 128)
    kcols = 1024 + qcols
    vcols = 2048 + np.arange(4 * hg * 128, (4 * hg + 4) * 128)
    zcols = 4096 + np.arange(4 * hg * 128, (4 * hg + 4) * 128)
    bcols = 6144 + np.arange(4 * hg, 4 * hg + 4)
    acols = 6160 + np.arange(4 * hg, 4 * hg + 4)
    wq = np.ascontiguousarray(W[:, np.concatenate([qcols, kcols, vcols, zcols])])
    wba = np.ascontiguousarray(W[:, np.concatenate([bcols, acols])])
    cw = inp["gdn_conv_w"][0][:, np.concatenate([qcols, kcols, vcols])]
    convw = np.ascontiguousarray(cw.reshape(4, 8, 128).transpose(2, 1, 0).reshape(128, 32))
    return {
        "xp": xp, "gpre": inp["pre_norm"][0:1].copy(), "wq": wq, "wba": wba, "convw": convw,
        "alog": inp["gdn_a_log"][0:1, 4 * hg:4 * hg + 4].copy(), "dtb": inp["gdn_dt_bias"][0:1, 4 * hg:4 * hg + 4].copy(),
        "onorm": inp["gdn_out_norm"][0].reshape(128, 1).copy(), "cst": _consts(),
    }


def build_wout(NBLK):
    nc = bass.Bass("TRN2", target_bir_lowering=False)

    def din(n, s, dt=F32):
        return nc.dram_tensor(n, list(s), dt, kind="ExternalInput").ap()

    NT = NBLK * 128
    oTin = din("oTin", [2048, NT], BF16)
    resid = din("resid", [NT, 1024])
    w = din("w", [2048, 1024])
    gpost = din("gpost", [1, 1024])
    out = nc.dram_tensor("out", [NT, 1024], F32, kind="ExternalOutput").ap()
    with contextlib.ExitStack() as st:
        P = Prog(nc, st)
        k = K(P)
        sb, ps = P.sb, P.ps
        wsb = sb("wsb", [128, 16, 1024], BF16)
        wstage = sb("wstage", [128, 2, 1024], F32)
        gp_b = sb("gp_b", [128, 1024], F32)
        oTs = sb("oTs", [128, 2, 16, 128], BF16)
        rs = sb("rs", [128, 2, 1024], F32)
        junk = sb("junk", [128, 512], BF16)
        ss = sb("ss", [128, 2], F32)
        rstd = sb("rstd", [128, 1], F32)
        ot = sb("ot", [128, 2, 1024], F32)
        ps_y = ps("ps_y", [128, 2, 512], F32)
        B_Y = [Tk("B_Y0"), Tk("B_Y1")]
        T = {}

        def tk(n):
            if n not in T:
                T[n] = Tk(n)
            return T[n]
        s_c = P.dmasem("c")
        s_w = [P.dmasem("w0"), P.dmasem("w1")]
        s_i = [P.dmasem("i0"), P.dmasem("i1")]
        s_o = [P.dmasem("o0"), P.dmasem("o1")]
        k.ld(s_c, gp_b[:], gpost[0:1, :].partition_broadcast(128), [tk("gp")])
        for kc in range(16):
            sl = kc % 2
            k.ld(s_w[sl], wstage[:, sl, :], w[kc * 128:(kc + 1) * 128, :], [tk("wst%d" % sl)])
            k.cp(wsb[:, kc, :], wstage[:, sl, :], [tk("wst%d" % sl)], [tk("wsb")], eng="pool")
        for blk in range(NBLK):
            sl = blk % 2
            c0 = blk * 128
            k.ld(s_i[sl], oTs[:, sl, :, :], oTin[:, c0:c0 + 128].rearrange("(k p) t -> p k t", p=128), [tk("in%d" % sl)])
            k.ld(s_i[sl], rs[:, sl, :], resid[c0:c0 + 128, :], [tk("in%d" % sl)])
            for half in range(2):
                for kc in range(16):
                    k.mm(ps_y[:, half, :], oTs[:, sl, kc, :], wsb[:, kc, half * 512:(half + 1) * 512], [tk("in%d" % sl), tk("wsb")], [B_Y[half]],
                         start=(kc == 0), stop=(kc == 15), inc=(kc == 15))
                k.act(junk[:], ps_y[:, half, :], AF.Square, [], [B_Y[half], tk("junk"), tk("ss")], accum_out=ss[:, half:half + 1])
            k.tt(rstd[:], ss[:, 0:1], ss[:, 1:2], ALU.add, [tk("ss")], [tk("rstd")])
            k.act(rstd[:], rstd[:], AF.Sqrt, [], [tk("rstd")], scale=1.0 / 1024, bias=EPS)
            k.rcp(rstd[:], rstd[:], [], [tk("rstd")])
            for half in range(2):
                hs = slice(half * 512, (half + 1) * 512)
                k.stt(ot[:, sl, hs], ps_y[:, half, :], rstd[:, 0:1], gp_b[:, hs], ALU.mult, ALU.mult, [tk("rstd"), tk("gp")], [B_Y[half], tk("ot%d" % sl)])
            k.tt(ot[:, sl, :], ot[:, sl, :], rs[:, sl, :], ALU.add, [tk("in%d" % sl)], [tk("ot%d" % sl)])
            k.ld(s_o[sl], out[c0:c0 + 128, :], ot[:, sl, :], [], r=[tk("ot%d" % sl)])
        P.finish("sp", [tk("ot0"), tk("ot1")])
        P.emit()
    return nc


SCALE = 192.0 ** -0.5


def _consts_mla():
    c = np.zeros((128, 384), np.float32)
    c[:, 0:128] = np.eye(128, dtype=np.float32)
    c[:, 128:256] = 1.0
    kk = np.arange(128)
    c[:, 256:384] = (kk[None, :] >= kk[:, None])
    return c


def mla_decl(nc, NTOK2):
    def din(n, s, dt=F32):
        return nc.dram_tensor(n, list(s), dt, kind="ExternalInput").ap()
    io = {}
    io["g1"] = din("g1", [1, 1024])
    io["gkv"] = din("gkv", [1, 1024])
    io["glat"] = din("glat", [128, 1])
    io["gq"] = din("gq", [1, 256])
    io["wkvd"] = din("wkvd", [1024, 256])
    io["wuk"] = din("wuk", [128, 512])
    io["wuv"] = din("wuv", [128, 512])
    io["wmi"] = din("wmi", [1024, 768])
    io["wqu"] = din("wqu", [256, 1024])
    io["cos2T"] = din("cos2T", [64, NTOK2])
    io["sinsT"] = din("sinsT", [64, NTOK2])
    io["cstm"] = din("cstm", [128, 384])
    return io


def build_mla(NTOK2):
    assert NTOK2 % 128 == 0
    nc = bass.Bass("TRN2", target_bir_lowering=False)
    io = mla_decl(nc, NTOK2)
    io["h1p"] = nc.dram_tensor("h1p", [NTOK2, 1024], F32, kind="ExternalInput").ap()
    io["o1T"] = nc.dram_tensor("o1T", [512, NTOK2], BF16, kind="ExternalOutput").ap()
    emit_mla(nc, None, io, NTOK2)
    return nc


def emit_mla(nc, semst, io, NTOK2, prew=(), fz=None):
    NBK = NTOK2 // 128
    g1, gkv, glat, gq, wkvd, wuk, wuv, wmi, wqu, cos2T, sinsT, cst = (
        io[n] for n in ("g1", "gkv", "glat", "gq", "wkvd", "wuk", "wuv", "wmi", "wqu", "cos2T", "sinsT", "cstm"))
    h1p = io.get("h1p")
    o1T = io.get("o1T")
    with contextlib.ExitStack() as st:
        P = Prog(nc, st, semst, "m", prew)
        k = K(P)
        sb, ps = P.sb, P.ps
        if fz is not None:
            gp0_b = sb("gp0_b", [128, 1024], F32)
            w1_b = sb("w1_b", [128, 4, 1024], BF16)
            ys = sb("ys", [128, 2, 1024], F32)
            ssy = sb("ssy", [128, 1], F32)
            y1st = sb("y1st", [128, 1024], F32)
        ckvT = sb("ckvT", [128, NTOK2], BF16)
        kropeT = sb("kropeT", [128, NTOK2], BF16)
        ckvtok = sb("ckvtok", [128, NBK, 129], BF16)
        wkvd_b = sb("wkvd_b", [128, 8, 256], BF16)
        wuk_b = sb("wuk_b", [128, 4, 128], BF16)
        wukT_b = sb("wukT_b", [128, 4, 128], BF16)
        wuv_b = sb("wuv_b", [128, 4, 128], BF16)
        wmi_b = sb("wmi_b", [128, 8, 768], BF16)
        wqu_b = sb("wqu_b", [128, 2, 1024], BF16)
        wstage = sb("wstage", [128, 2, 1024], F32)
        g1_b = sb("g1_b", [128, 1024], F32)
        gkv_b = sb("gkv_b", [128, 1024], F32)
        gq_b = sb("gq_b", [128, 256], F32)
        glat_s = sb("glat_s", [128, 1], F32)
        cst_s = sb("cst_s", [128, 384], F32)
        ident_b = sb("ident_b", [128, 128], BF16)
        tri_b = sb("tri_b", [128, 128], BF16)
        zb = sb("zb", [128, 512], BF16)
        ones_f = cst_s[:, 128:256]
        hs = sb("hs", [128, 4, 1024], F32)
        junk = sb("junk", [128, 1024], BF16)
        ss = sb("ss", [128, 4], F32)
        rstd = sb("rstd", [128, 4], F32)
        hn1 = sb("hn1", [128, 1024], BF16)
        hkv = sb("hkv", [128, 1024], BF16)
        hn1T = sb("hn1T", [128, 8, 512], BF16)
        hkvT = sb("hkvT", [128, 8, 512], BF16)
        cs = sb("cs", [64, 512], F32)
        sn = sb("sn", [64, 512], F32)
        ckf = sb("ckf", [128, 512], F32)
        sq = sb("sq", [128, 512], F32)
        rt = sb("rt", [128, 512], F32)
        ra = sb("ra", [64, 2, 512], F32)
        rbb = sb("rbb", [64, 2, 512], F32)
        ssq = sb("ssq", [128, 1], F32)
        cqn = sb("cqn", [128, 256], BF16)
        cqT = sb("cqT", [128, 2, 512], BF16)
        zs1 = sb("zs1", [128, 4, 512], F32)
        qnT = sb("qnT", [128, 2, 512], BF16)
        qpT = sb("qpT", [128, 4, 512], BF16)
        qrT = sb("qrT", [128, 4, 512], BF16)
        pT = sb("pT", [128, 3, 512], BF16)
        pacc = sb("pacc", [128, 2, 512], F32)
        rdb = sb("rdb", [128, 512], F32)
        ocn = sb("ocn", [128, 512], BF16)
        og1 = sb("og1", [128, 4, 512], BF16)
        ps_tr = ps("ps_tr", [128, 1024], BF16)
        ps_pj = ps("ps_pj", [128, 512], F32)
        ps_p2 = ps("ps_p2", [128, 512], F32)
        ps_v = ps("ps_v", [128, 512], F32)
        ps_s = ps("ps_s", [128, 3, 512], F32)
        ps_o = ps("ps_o", [128, 512], F32)
        B_TR, B_PJ, B_P2, B_V = Tk("B_TR"), Tk("B_PJ"), Tk("B_P2"), Tk("B_V")
        B_S = [Tk("B_S0"), Tk("B_S1"), Tk("B_S2")]
        B_O = Tk("B_O")
        T = {}

        def tk(n):
            if n not in T:
                T[n] = Tk(n)
            return T[n]

        s_c = P.dmasem("c")
        s_w = [P.dmasem("w0"), P.dmasem("w1")]
        s_x = P.dmasem("x")
        s_o = P.dmasem("o")
        k.ld(s_c, cst_s[:], cst[:, :], [tk("cst")])
        k.ld(s_c, g1_b[:], g1[0:1, :].partition_broadcast(128), [tk("g1")])
        k.ld(s_c, gkv_b[:], gkv[0:1, :].partition_broadcast(128), [tk("gkv")])
        k.ld(s_c, gq_b[:], gq[0:1, :].partition_broadcast(128), [tk("gq")])
        k.ld(s_c, glat_s[:], glat[:, :], [tk("glat")])
        if fz is not None:
            s_yl = [P.dmasem("yl0"), P.dmasem("yl1")]
            s_h = P.dmasem("h")
            k.ld(s_c, gp0_b[:], fz["gp0"][0:1, :].partition_broadcast(128), [tk("gp0")])
            tk("gp0").w = None
        for _n in ("cst", "g1", "gkv", "gq", "glat", "gp0"):
            tk(_n).w = (s_c, P.dcnt[s_c])
        k.cp(ident_b[:], cst_s[:, 0:128], [tk("cst")], [tk("identb")])
        k.cp(tri_b[:], cst_s[:, 256:384], [tk("cst")], [tk("trib")])
        P.op("dve", lambda e: e.memset(zb[:], 0.0), [], [tk("zb")])
        P.op("pool", lambda e: e.memset(ckvtok[:], 1.0), [], [tk("ckvtok")])
        P.op("pool", lambda e: e.memset(kropeT[:], 0.0), [], [tk("kropeT")])
        P.op("pool", lambda e: e.memset(qrT[:], 0.0), [], [tk("qrT%d" % h_) for h_ in range(4)])
        wl = []
        for kc in range(8):
            wl.append((wkvd[kc * 128:(kc + 1) * 128, :], 256, wkvd_b[:, kc, :]))
        wl.append((wuk[:, :], 512, wuk_b[:].rearrange("p h d -> p (h d)")))
        wl.append((wuv[:, :], 512, wuv_b[:].rearrange("p h d -> p (h d)")))
        for kc in range(8):
            wl.append((wmi[kc * 128:(kc + 1) * 128, :], 768, wmi_b[:, kc, :]))
        for c2 in range(2):
            wl.append((wqu[c2 * 128:(c2 + 1) * 128, :], 1024, wqu_b[:, c2, :]))
        if fz is not None:
            for h in range(4):
                wl.append((fz["w1"][h * 128:(h + 1) * 128, :], 1024, w1_b[:, h, :]))
        for i, (src, n, dst) in enumerate(wl):
            sl = i % 2
            k.ld(s_w[sl], wstage[:, sl, 0:n], src, [tk("wst%d" % sl)])
            k.cp(dst, wstage[:, sl, 0:n], [tk("wst%d" % sl)], [tk("wts")], eng="pool")
        for h in range(4):
            k.tr(ps_tr[:, h * 128:(h + 1) * 128], wuk_b[:, h, :], ident_b[:], [tk("wts"), tk("identb")], [B_TR], inc=(h == 3))
        k.act(wukT_b[:].rearrange("p h d -> p (h d)"), ps_tr[:, 0:512], AF.Identity, [], [B_TR, tk("wukT")])

        ntiles = (NTOK2 + 511) // 512
        cc_next = [0]
        for ti in range(ntiles):
            t0 = ti * 512
            TT = min(512, NTOK2 - t0)
            NS = TT // 128
            blk0 = t0 // 128
            hsrc = h1p[t0:t0 + TT, :] if fz is None else fz["xp"][48 + t0:48 + t0 + TT, :]
            k.ld(s_x, hs[:, 0:NS, :], hsrc.rearrange("(s p) d -> p s d", p=128), [tk("hs")])
            k.ld(s_x, cs[:, 0:TT], cos2T[:, t0:t0 + TT], [tk("cs")])
            k.ld(s_x, sn[:, 0:TT], sinsT[:, t0:t0 + TT], [tk("cs")])
            tk("hs").w = (s_x, P.dcnt[s_x])
            tk("cs").w = (s_x, P.dcnt[s_x])
            if fz is not None:
                for s in range(NS):
                    ysl = s % 2
                    ytk = tk("ys%d" % ysl)
                    k.ld(s_yl[ysl], ys[:, ysl, :], fz["y0f"][t0 + s * 128:t0 + (s + 1) * 128, :], [ytk])
                    k.act(junk[:], ys[:, ysl, :], AF.Square, [ytk], [tk("junk"), tk("ssy")], accum_out=ssy[:])
                    k.act(ssy[:], ssy[:], AF.Sqrt, [], [tk("ssy")], scale=1.0 / 1024, bias=EPS)
                    k.rcp(ssy[:], ssy[:], [], [tk("ssy")])
                    k.stt(ys[:, ysl, :], ys[:, ysl, :], ssy[:, 0:1], gp0_b[:], ALU.mult, ALU.mult, [tk("ssy"), tk("gp0")], [ytk])
                    k.tt(hs[:, s, :], hs[:, s, :], ys[:, ysl, :], ALU.add, [ytk], [tk("hs")])
                k.ld(s_h, fz["h1s"][t0:t0 + TT, :].rearrange("(s p) d -> p s d", p=128), hs[:, 0:NS, :], [], r=[tk("hs")])
            for s in range(NS):
                k.act(junk[:], hs[:, s, :], AF.Square, [tk("hs")], [tk("junk"), tk("ss")], accum_out=ss[:, s:s + 1])
            k.act(rstd[:, 0:NS], ss[:, 0:NS], AF.Sqrt, [tk("ss")], [tk("rstd")], scale=1.0 / 1024, bias=EPS)
            k.rcp(rstd[:, 0:NS], rstd[:, 0:NS], [], [tk("rstd")])
            for s in range(NS):
                for (gb, gt_, dstT, nm) in ((g1_b, "g1", hn1T, "hn1"), (gkv_b, "gkv", hkvT, "hkv")):
                    buf = hn1 if nm == "hn1" else hkv
                    k.stt(buf[:], hs[:, s, :], rstd[:, s:s + 1], gb[:], ALU.mult, ALU.mult, [tk("hs"), tk("rstd"), tk(gt_)], [tk(nm)])
                    for kc in range(8):
                        k.tr(ps_tr[:, kc * 128:(kc + 1) * 128], buf[:, kc * 128:(kc + 1) * 128], ident_b[:], [tk(nm), tk("identb")], [B_TR],
                             inc=(kc == 7))
                    k.act(dstT[:, :, s * 128:(s + 1) * 128], ps_tr[:].rearrange("p (k t) -> p k t", t=128), AF.Identity, [], [B_TR, tk(nm + "T")])
            tsl = slice(t0, t0 + TT)
            for kc in range(8):
                k.mm(ps_pj[:, 0:TT], wkvd_b[:, kc, 0:128], hkvT[:, kc, 0:TT], [tk("wts"), tk("hkvT")], [B_PJ], start=(kc == 0), stop=(kc == 7), inc=(kc == 7))
            k.act(ckf[:, 0:TT], ps_pj[:, 0:TT], AF.Identity, [], [B_PJ, tk("ckf")])
            k.act(sq[:, 0:TT], ckf[:, 0:TT], AF.Square, [tk("ckf")], [tk("sq")])
            k.mm(ps_p2[:, 0:TT], ones_f, sq[:, 0:TT], [tk("cst"), tk("sq")], [B_P2])
            k.act(rt[:, 0:TT], ps_p2[:, 0:TT], AF.Ln, [], [B_P2, tk("rt")], scale=1.0 / 128, bias=EPS)
            k.act(rt[:, 0:TT], rt[:, 0:TT], AF.Exp, [], [tk("rt")], scale=-0.5)
            k.stt(ckvT[:, tsl], ckf[:, 0:TT], glat_s[:, 0:1], rt[:, 0:TT], ALU.mult, ALU.mult, [tk("ckf"), tk("glat"), tk("rt")], [tk("ckvT")])
            for kc in range(8):
                k.mm(ps_pj[0:64, 0:TT], wkvd_b[:, kc, 128:192], hkvT[:, kc, 0:TT], [tk("wts"), tk("hkvT")], [B_PJ], start=(kc == 0), stop=(kc == 7), inc=(kc == 7))
            for kc in range(8):
                k.mm(ps_p2[0:64, 0:TT], wkvd_b[:, kc, 192:256], hkvT[:, kc, 0:TT], [tk("wts"), tk("hkvT")], [B_P2], start=(kc == 0), stop=(kc == 7), inc=(kc == 7))
            k.tt(ra[:, 0, 0:TT], ps_pj[0:64, 0:TT], cs[:, 0:TT], ALU.mult, [tk("cs")], [B_PJ, tk("ra0")])
            k.tt(rbb[:, 0, 0:TT], ps_p2[0:64, 0:TT], sn[:, 0:TT], ALU.mult, [tk("cs")], [B_P2, tk("rbb0")])
            k.tt(kropeT[0:64, tsl], ra[:, 0, 0:TT], rbb[:, 0, 0:TT], ALU.add, [tk("ra0"), tk("rbb0")], [tk("kropeT")])
            for s in range(NS):
                k.tr(ps_tr[:, s * 128:(s + 1) * 128], ckvT[:, t0 + s * 128:t0 + (s + 1) * 128], ident_b[:], [tk("ckvT"), tk("identb")], [B_TR], inc=(s == NS - 1))
            k.act(ckvtok[:, blk0:blk0 + NS, 0:128], ps_tr[:, 0:NS * 128].rearrange("p (s d) -> p s d", d=128), AF.Identity, [], [B_TR, tk("ckvtok")])
            for s in range(NS):
                for kc in range(8):
                    k.mm(ps_v[:, 0:256], hn1T[:, kc, s * 128:(s + 1) * 128], wmi_b[:, kc, 0:256], [tk("hn1T"), tk("wts")], [B_V], start=(kc == 0), stop=(kc == 7), inc=(kc == 7))
                k.act(junk[:, 0:256], ps_v[:, 0:256], AF.Square, [], [B_V, tk("junk"), tk("ssq")], accum_out=ssq[:])
                k.act(ssq[:], ssq[:], AF.Sqrt, [], [tk("ssq")], scale=1.0 / 256, bias=EPS)
                k.rcp(ssq[:], ssq[:], [], [tk("ssq")])
                k.stt(cqn[:], ps_v[:, 0:256], ssq[:, 0:1], gq_b[:], ALU.mult, ALU.mult, [tk("ssq"), tk("gq")], [B_V, tk("cqn")])
                for c2 in range(2):
                    k.tr(ps_tr[:, c2 * 128:(c2 + 1) * 128], cqn[:, c2 * 128:(c2 + 1) * 128], ident_b[:], [tk("cqn"), tk("identb")], [B_TR], inc=(c2 == 1))
                k.act(cqT[:, :, s * 128:(s + 1) * 128], ps_tr[:, 0:256].rearrange("p (c t) -> p c t", t=128), AF.Identity, [], [B_TR, tk("cqT")])
            for h in range(4):
                for kc in range(8):
                    k.mm(ps_pj[:, 0:TT], wmi_b[:, kc, 256 + h * 128:256 + (h + 1) * 128], hn1T[:, kc, 0:TT], [tk("wts"), tk("hn1T")], [B_PJ],
                         start=(kc == 0), stop=(kc == 7), inc=(kc == 7))
                k.act(zs1[:, h, 0:TT], ps_pj[:, 0:TT], AF.Silu, [], [B_PJ, tk("zs1")])
            for hp in range(2):
                hh = (2 * hp, 2 * hp + 1)
                sets = {hh[0]: (ps_pj, B_PJ, ps_p2, B_P2, 0), hh[1]: (ps_v, B_V, ps_o, B_O, 1)}
                for h in hh:
                    pa, ba, pb, bb, u = sets[h]
                    for c2 in range(2):
                        k.mm(pa[:, 0:TT], wqu_b[:, c2, h * 256:h * 256 + 128], cqT[:, c2, 0:TT], [tk("wts"), tk("cqT")], [ba], start=(c2 == 0), stop=(c2 == 1), inc=(c2 == 1))
                for h in hh:
                    pa, ba, pb, bb, u = sets[h]
                    k.act(qnT[:, u, 0:TT], pa[:, 0:TT], AF.Identity, [], [ba, tk("qnT%d" % u)])
                for h in hh:
                    pa, ba, pb, bb, u = sets[h]
                    k.mm(pb[:, 0:TT], wukT_b[:, h, :], qnT[:, u, 0:TT], [tk("wukT"), tk("qnT%d" % u)], [bb])
                for h in hh:
                    pa, ba, pb, bb, u = sets[h]
                    k.act(qpT[:, h, 0:TT], pb[:, 0:TT], AF.Identity, [], [bb, tk("qpT%d" % h)])
                for h in hh:
                    pa, ba, pb, bb, u = sets[h]
                    for c2 in range(2):
                        k.mm(pa[0:64, 0:TT], wqu_b[:, c2, h * 256 + 128:h * 256 + 192], cqT[:, c2, 0:TT], [tk("wts"), tk("cqT")], [ba], start=(c2 == 0), stop=(c2 == 1), inc=(c2 == 1))
                    for c2 in range(2):
                        k.mm(pb[0:64, 0:TT], wqu_b[:, c2, h * 256 + 192:h * 256 + 256], cqT[:, c2, 0:TT], [tk("wts"), tk("cqT")], [bb], start=(c2 == 0), stop=(c2 == 1), inc=(c2 == 1))
                for h in hh:
                    pa, ba, pb, bb, u = sets[h]
                    k.tt(ra[:, u, 0:TT], pa[0:64, 0:TT], cs[:, 0:TT], ALU.mult, [tk("cs")], [ba, tk("ra%d" % u)])
                    k.tt(rbb[:, u, 0:TT], pb[0:64, 0:TT], sn[:, 0:TT], ALU.mult, [tk("cs")], [bb, tk("rbb%d" % u)])
                    k.tt(qrT[0:64, h, 0:TT], ra[:, u, 0:TT], rbb[:, u, 0:TT], ALU.add, [tk("ra%d" % u), tk("rbb%d" % u)], [tk("qrT%d" % h)])
            nkb = blk0 + NS

            def emit_s(h, j):
                jj = j - blk0
                qlo = max(0, jj) * 128
                buf = j % 3
                ksl = slice(j * 128, (j + 1) * 128)
                k.mm(ps_s[:, buf, qlo:TT], ckvT[:, ksl], qpT[:, h, qlo:TT], [tk("ckvT"), tk("qpT%d" % h)], [B_S[buf]], start=True, stop=False, inc=False)
                k.mm(ps_s[:, buf, qlo:TT], kropeT[:, ksl], qrT[:, h, qlo:TT], [tk("kropeT"), tk("qrT%d" % h)], [B_S[buf]], start=False, stop=True)
                k.act(pT[:, buf, qlo:TT], ps_s[:, buf, qlo:TT], AF.Exp, [], [B_S[buf], tk("pT%d" % buf)], scale=SCALE)
                if jj >= 0:
                    k.tt(pT[:, buf, qlo:qlo + 128], pT[:, buf, qlo:qlo + 128], tri_b[:], ALU.mult, [tk("trib")], [tk("pT%d" % buf)])
                if j == 0:
                    k.cp(pacc[:, h % 2, 0:TT], pT[:, buf, 0:TT], [tk("pT%d" % buf)], [tk("pacc%d" % (h % 2))])
                else:
                    k.tt(pacc[:, h % 2, qlo:TT], pacc[:, h % 2, qlo:TT], pT[:, buf, qlo:TT], ALU.add, [tk("pT%d" % buf)], [tk("pacc%d" % (h % 2))])

            def emit_pv(h, j):
                jj = j - blk0
                qlo = max(0, jj) * 128
                buf = j % 3
                k.mm(ps_o[:, qlo:TT], ckvtok[:, j, 0:128], pT[:, buf, qlo:TT], [tk("pT%d" % buf), tk("ckvtok")], [B_O],
                     start=(j == 0), stop=(j == nkb - 1))

            for h in range(4):
                if h == 0:
                    emit_s(h, 0)
                    if nkb > 1:
                        emit_s(h, 1)
                for j in range(nkb):
                    if j + 2 < nkb:
                        emit_s(h, j + 2)
                    emit_pv(h, j)
                if h < 3:
                    emit_s(h + 1, 0)
                    if nkb > 1:
                        emit_s(h + 1, 1)
                k.mm(ps_v[:, 0:TT], ones_f, pacc[:, h % 2, 0:TT], [tk("cst"), tk("pacc%d" % (h % 2))], [B_V])
                k.act(rdb[:, 0:TT], ps_v[:, 0:TT], AF.Ln, [], [B_V, tk("rdb")])
                k.act(rdb[:, 0:TT], rdb[:, 0:TT], AF.Exp, [], [tk("rdb")], scale=-1.0)
                k.tt(ocn[:, 0:TT], ps_o[:, 0:TT], rdb[:, 0:TT], ALU.mult, [tk("rdb")], [B_O, tk("ocn")])
                k.mm(ps_pj[:, 0:TT], wuv_b[:, h, :], ocn[:, 0:TT], [tk("wts"), tk("ocn")], [B_PJ])
                k.tt(og1[:, h, 0:TT], ps_pj[:, 0:TT], zs1[:, h, 0:TT], ALU.mult, [tk("zs1")], [B_PJ, tk("og1")])
            if fz is None:
                k.ld(s_o, o1T[:, t0:t0 + TT].rearrange("(h e) t -> e h t", e=128), og1[:, :, 0:TT], [], r=[tk("og1")])
            else:
                for s in range(NS):
                    for half, (pst, btk) in enumerate(((ps_pj, B_PJ), (ps_p2, B_P2))):
                        for h in range(4):
                            k.mm(pst[:, :], og1[:, h, s * 128:(s + 1) * 128], w1_b[:, h, half * 512:(half + 1) * 512], [tk("og1"), tk("wts")], [btk],
                                 start=(h == 0), stop=(h == 3), inc=(h == 3))
                    k.act(y1st[:, 0:512], ps_pj[:, :], AF.Identity, [], [B_PJ, tk("y1st")])
                    k.cp(y1st[:, 512:1024], ps_p2[:, :], [], [B_P2, tk("y1st")])
                    k.ld(s_o, fz["y1p"][t0 + s * 128:t0 + (s + 1) * 128, :], y1st[:], [], r=[tk("y1st"), tk("y1rows")])
                rend = t0 + TT
                while cc_next[0] < NTOK2 and rend >= min(NTOK2, cc_next[0] + CC_ROWS):
                    r0, r1 = cc_next[0], min(NTOK2, cc_next[0] + CC_ROWS)
                    P.cc("cc", fz["scc"], fz["y1p"][r0:r1, :], fz["y1f"][r0:r1, :], [], [tk("y1rows")])
                    cc_next[0] = r1
        P.finish("sp", [tk("og1")] + ([tk("y1st"), tk("hs")] if fz is not None else []))
        P.emit()
        return P.final_events()


def rope_tables_T(n):
    inv = (np.float32(10000.0) ** (-(np.arange(0, 64, 2, dtype=np.float32)) / np.float32(64))).astype(np.float32)
    ang = (np.arange(n, dtype=np.float32)[:, None] * inv[None, :]).astype(np.float32)
    cos, sin = np.cos(ang).astype(np.float32), np.sin(ang).astype(np.float32)
    cos2T = np.ascontiguousarray(np.concatenate([cos, cos], 1).T)
    sinsT = np.ascontiguousarray(np.concatenate([-sin, sin], 1).T)
    return cos2T, sinsT


def mla_inputs(inp, core, h1b, NTOK2):
    hg = core % 4
    h1p = None
    if h1b is not None:
        L = h1b.shape[0]
        h1p = np.zeros((NTOK2, 1024), np.float32)
        h1p[:L] = h1b
    kd = inp["kv_w_down"]
    wkvd = np.ascontiguousarray(np.concatenate([kd[:, 0:128], kd[:, 128:192], kd[:, 160:192], kd[:, 128:160]], 1))
    ku = inp["kv_w_up"].reshape(128, 16, 256)[:, 4 * hg:4 * hg + 4]
    wuk = np.ascontiguousarray(ku[:, :, 0:128].reshape(128, 512))
    wuv = np.ascontiguousarray(ku[:, :, 128:256].reshape(128, 512))
    mi = inp["mla_w_in"][0]
    wmi = np.ascontiguousarray(np.concatenate([mi[:, 0:256], mi[:, 256 + 512 * hg:256 + 512 * (hg + 1)]], 1))
    qu = inp["mla_w_q_up"][0].reshape(256, 16, 192)[:, 4 * hg:4 * hg + 4]
    wqu = np.ascontiguousarray(np.concatenate([qu[:, :, 0:128], qu[:, :, 128:192], qu[:, :, 160:192], qu[:, :, 128:160]], 2).reshape(256, 1024))
    cos2T, sinsT = rope_tables_T(NTOK2)
    d = {} if h1p is None else {"h1p": h1p}
    d.update(_mla_rest(inp, wkvd, wuk, wuv, wmi, wqu, cos2T, sinsT))
    return d


def _mla_rest(inp, wkvd, wuk, wuv, wmi, wqu, cos2T, sinsT):
    return {
        "g1": inp["pre_norm"][1:2].copy(), "gkv": inp["kv_norm"].reshape(1, 1024).copy(),
        "glat": inp["kv_latent_norm"].reshape(128, 1).copy(), "gq": inp["mla_q_latent_norm"][0:1].copy(),
        "wkvd": wkvd, "wuk": wuk, "wuv": wuv, "wmi": wmi, "wqu": wqu, "cos2T": cos2T, "sinsT": sinsT, "cstm": _consts_mla(),
    }


def emit_fin(nc, semst, h1s, y1f, gp1, out, NBK, prew=()):
    with contextlib.ExitStack() as st:
        P = Prog(nc, st, semst, "f", prew)
        k = K(P)
        sb = P.sb
        gp_b = sb("gp_b", [128, 1024], F32)
        NB_ = 6
        hb = sb("hb", [128, NB_, 1024], F32)
        yb = sb("yb", [128, NB_, 1024], F32)
        junk = sb("junk", [128, 1024], BF16)
        ss = sb("ss", [128, 1], F32)
        T = {}

        def tk(n):
            if n not in T:
                T[n] = Tk(n)
            return T[n]
        s_c = P.dmasem("c")
        s_i = [P.dmasem("i%d" % i_) for i_ in range(NB_)]
        s_o = [P.dmasem("o%d" % i_) for i_ in range(NB_)]
        k.ld(s_c, gp_b[:], gp1[0:1, :].partition_broadcast(128), [tk("gp")])
        for blk in range(NBK):
            sl = blk % NB_
            rows = slice(blk * 128, (blk + 1) * 128)
            k.ld(s_i[sl], hb[:, sl, :], h1s[rows, :], [tk("hb%d" % sl)])
            k.ld(s_i[sl], yb[:, sl, :], y1f[rows, :], [tk("yb%d" % sl)])
            tk("hb%d" % sl).w = (s_i[sl], P.dcnt[s_i[sl]])
            k.act(junk[:], yb[:, sl, :], AF.Square, [tk("yb%d" % sl)], [tk("junk"), tk("ss")], accum_out=ss[:])
            k.act(ss[:], ss[:], AF.Sqrt, [], [tk("ss")], scale=1.0 / 1024, bias=EPS)
            k.rcp(ss[:], ss[:], [], [tk("ss")])
            k.stt(yb[:, sl, :], yb[:, sl, :], ss[:, 0:1], gp_b[:], ALU.mult, ALU.mult, [tk("ss"), tk("gp")], [tk("yb%d" % sl)])
            k.tt(yb[:, sl, :], yb[:, sl, :], hb[:, sl, :], ALU.add, [tk("hb%d" % sl)], [tk("yb%d" % sl)], eng=("pool" if blk % 3 == 0 else "dve"))
            k.ld(s_o[sl], out[rows, :], yb[:, sl, :], [], r=[tk("yb%d" % sl)])
        P.finish("sp", [tk("yb%d" % i_) for i_ in range(NB_)])
        P.emit()
        return P.final_events()


GROUPS = [[0, 1, 2, 3], [4, 5, 6, 7]]
CC_ROWS = 1024


def emit_allreduce(nc, ev, src, dst, scc):
    with nc.Block() as block:
        @block.gpsimd
        def _(g):
            for hsem, v in ev:
                g.wait_ge(hsem, v)
            rows = src.ap().shape[0]
            n = 0
            for r0 in range(0, rows, CC_ROWS):
                r1 = min(rows, r0 + CC_ROWS)
                g.collective_compute("AllReduce", ALU.add, replica_groups=GROUPS,
                                     ins=[src.ap()[r0:r1, :]], outs=[dst.ap()[r0:r1, :]]).then_inc(scc)
                n += 1
            g.wait_ge(scc, n)
    rows_ = src.ap().shape[0]
    return (rows_ + CC_ROWS - 1) // CC_ROWS


def build_fused(NTOK, NTOK2):
    nc = bass.Bass("TRN2", target_bir_lowering=False)

    def din(n, s_, dt=F32):
        return nc.dram_tensor(n, list(s_), dt, kind="ExternalInput").ap()
    ioG = gdn_decl(nc, NTOK)
    ioM = mla_decl(nc, NTOK2)
    w0 = din("w0", [512, 1024])
    gp0 = din("gp0", [1, 1024])
    w1 = din("w1", [512, 1024])
    gp1 = din("gp1", [1, 1024])
    out = nc.dram_tensor("out", [NTOK2, 1024], F32, kind="ExternalOutput").ap()
    y0p = nc.dram_tensor("y0p", [NTOK2, 1024], F32)
    y0f = nc.dram_tensor("y0f", [NTOK2, 1024], F32)
    h1s = nc.dram_tensor("h1s", [NTOK2, 1024], F32)
    y1p = nc.dram_tensor("y1p", [NTOK2, 1024], F32)
    y1f = nc.dram_tensor("y1f", [NTOK2, 1024], F32)
    with contextlib.ExitStack() as semst:
        scc0 = semst.enter_context(nc.semaphore("cc0"))
        scc1 = semst.enter_context(nc.semaphore("cc1"))
        ev = emit_gdn(nc, semst, ioG, NTOK, fz=dict(y0p=y0p.ap(), y0f=y0f.ap(), scc=scc0, w0=w0))
        ev = emit_mla(nc, semst, ioM, NTOK2, prew=ev,
                      fz=dict(xp=ioG["xp"], y0f=y0f.ap(), gp0=gp0, h1s=h1s.ap(), y1p=y1p.ap(), y1f=y1f.ap(), scc=scc1, w1=w1))
        emit_fin(nc, semst, h1s.ap(), y1f.ap(), gp1, out, NTOK2 // 128, prew=ev)
    return nc


def fused_inputs(inp, core, NTOK, NTOK2):
    hg = core % 4
    d = gdn_inputs(inp, core, NTOK)
    d.update(mla_inputs(inp, core, None, NTOK2))
    d["w0"] = np.ascontiguousarray(inp["gdn_w_out"][0][hg * 512:(hg + 1) * 512])
    d["w1"] = np.ascontiguousarray(inp["mla_w_out"][0][hg * 512:(hg + 1) * 512])
    d["gp0"] = inp["post_norm"][0:1].copy()
    d["gp1"] = inp["post_norm"][1:2].copy()
    return d


_NC_CACHE = {}


def _get(name, fn, *a):
    key = (name,) + a
    if key not in _NC_CACHE:
        _NC_CACHE[key] = fn(*a)
    return _NC_CACHE[key]


def kernel(**inputs):
    inp = {k_: np.ascontiguousarray(np.asarray(v)) for k_, v in inputs.items()}
    B, SEQ, D = inp["x"].shape
    L = SEQ + 16
    NTOK2 = ((L + 127) // 128) * 128
    NTOK = ((NTOK2 + 48 + 127) // 128) * 128
    cores = list(range(8))
    nc = _get("fused", build_fused, NTOK, NTOK2)
    res = run_bass_kernel_spmd(nc, [fused_inputs(inp, c, NTOK, NTOK2) for c in cores], core_ids=cores).results
    return np.stack([np.asarray(res[4 * b]["out"])[16:L] for b in range(B)], 0).astype(np.float32)
```

```python
import contextlib
import numpy as np
import ml_dtypes
import concourse.bass as bass
import concourse.mybir as mybir
from concourse.bass_utils import run_bass_kernel_spmd

F32 = mybir.dt.float32
BF16 = mybir.dt.bfloat16
AF = mybir.ActivationFunctionType
ALU = mybir.AluOpType
EPS = 1e-6


class Tk:
    __slots__ = ("name", "w", "r")

    def __init__(self, name):
        self.name = name
        self.w = None
        self.r = []


class Prog:
    ENGS = ("pe", "act", "dve", "pool", "sp")

    def __init__(self, nc, stack, semst=None, pfx="", prew=()):
        self.nc = nc
        self.stack = stack
        self.semst = semst if semst is not None else stack
        self.pfx = pfx
        self.prew = list(prew)
        self.ops = {e: [] for e in self.ENGS}
        self.cnt = {e: 0 for e in self.ENGS}
        self.known = {e: {} for e in self.ENGS}
        self.sems = {}
        self.dcnt = {}
        for e in self.ENGS:
            self.sems[e] = self.semst.enter_context(nc.semaphore(pfx + "s_" + e))

    def final_events(self):
        ev = [(self.sems[e], self.cnt[e]) for e in self.ENGS if self.cnt[e] > 0]
        ev += [(self.sems[n], c) for n, c in self.dcnt.items() if c > 0]
        return ev

    def sb(self, name, shape, dt):
        return self.stack.enter_context(self.nc.sbuf_tensor(self.pfx + name, list(shape), dt))

    def ps(self, name, shape, dt):
        return self.stack.enter_context(self.nc.psum_tensor(self.pfx + name, list(shape), dt))

    def dmasem(self, name):
        self.sems[name] = self.semst.enter_context(self.nc.semaphore(self.pfx + "d_" + name))
        self.dcnt[name] = 0
        return name

    def _need(self, eng, ev, waits):
        if ev is None:
            return
        key, val = ev
        if key == eng and eng == "pe":
            return
        if self.known[eng].get(key, 0) >= val:
            return
        if key in self.ENGS and key != eng:
            assert self.cnt[key] >= val, (eng, ev, self.cnt[key])
        self.known[eng][key] = val
        for i, (k, v) in enumerate(waits):
            if k == key:
                waits[i] = (k, max(v, val))
                return
        waits.append((key, val))

    def _deps(self, eng, reads, writes):
        waits = []
        for t in reads:
            self._need(eng, t.w, waits)
        for t in writes:
            self._need(eng, t.w, waits)
            for ev in t.r:
                self._need(eng, ev, waits)
        return waits

    def _mark(self, ev, reads, writes):
        for t in writes:
            t.w = ev
            t.r = []
        for t in reads:
            if t not in writes:
                t.r.append(ev)
                if len(t.r) > 8:
                    d = {}
                    for k, v in t.r:
                        d[k] = max(d.get(k, 0), v)
                    t.r = list(d.items())

    def op(self, eng, fn, reads=(), writes=(), inc=True):
        waits = self._deps(eng, reads, writes)
        if inc:
            self.cnt[eng] += 1
            ev = (eng, self.cnt[eng])
        else:
            ev = (eng, self.cnt[eng] + 1)
        self._mark(ev, reads, writes)
        self.ops[eng].append((fn, waits, ("c", inc)))

    def dma(self, q, sem, fn, reads=(), writes=()):
        waits = self._deps(q, reads, writes)
        self.dcnt[sem] += 16
        ev = (sem, self.dcnt[sem])
        self._mark(ev, reads, writes)
        self.ops[q].append((fn, waits, ("d", sem)))

    def cc(self, semname, hsem, src, dst, reads=(), writes=()):
        if semname not in self.sems:
            self.sems[semname] = hsem
            self.dcnt[semname] = 0
        waits = self._deps("pool", reads, writes)
        self.dcnt[semname] += 1
        ev = (semname, self.dcnt[semname])
        self._mark(ev, reads, writes)
        fn = lambda e: e.collective_compute("AllReduce", ALU.add, replica_groups=GROUPS, ins=[src], outs=[dst])
        self.ops["pool"].append((fn, waits, ("k", semname)))

    def finish(self, eng, tks):
        waits = []
        for t in tks:
            self._need(eng, t.w, waits)
            for ev in t.r:
                self._need(eng, ev, waits)
        self.ops[eng].append((None, waits, ("w", None)))

    def emit(self):
        nc, sems, ops = self.nc, self.sems, self.ops
        prew = self.prew
        with nc.Block() as block:
            def run(name, e):
                for hsem, v in prew:
                    e.wait_ge(hsem, v)
                for fn, waits, kind in ops[name]:
                    for k, v in waits:
                        e.wait_ge(sems[k], v)
                    if fn is None:
                        continue
                    ins = fn(e)
                    if kind[0] == "c":
                        if kind[1]:
                            ins.then_inc(sems[name], 1)
                    elif kind[0] == "k":
                        ins.then_inc(sems[kind[1]], 1)
                    else:
                        ins.then_inc(sems[kind[1]], 16)

            @block.tensor
            def _(e):
                run("pe", e)

            @block.scalar
            def _(e):
                run("act", e)

            @block.vector
            def _(e):
                run("dve", e)

            @block.gpsimd
            def _(e):
                run("pool", e)

            @block.sync
            def _(e):
                run("sp", e)


class K:
    def __init__(self, P):
        self.P = P

    def act(self, out, in_, func, r, w, **kw):
        self.P.op("act", lambda e: e.activation(out=out, in_=in_, func=func, **kw), r, w)

    def tt(self, out, in0, in1, op, r, w, eng="dve"):
        self.P.op(eng, lambda e: e.tensor_tensor(out=out, in0=in0, in1=in1, op=op), r, w)

    def ts(self, out, in0, s1, op0, r, w, s2=None, op1=None, eng="dve"):
        if op1 is None:
            self.P.op(eng, lambda e: e.tensor_scalar(out=out, in0=in0, scalar1=s1, scalar2=None, op0=op0), r, w)
        else:
            self.P.op(eng, lambda e: e.tensor_scalar(out=out, in0=in0, scalar1=s1, scalar2=s2, op0=op0, op1=op1), r, w)

    def stt(self, out, in0, scalar, in1, op0, op1, r, w, eng="dve"):
        self.P.op(eng, lambda e: e.scalar_tensor_tensor(out=out, in0=in0, scalar=scalar, in1=in1, op0=op0, op1=op1), r, w)

    def cp(self, out, in_, r, w, eng="dve"):
        self.P.op(eng, lambda e: e.tensor_copy(out=out, in_=in_), r, w)

    def rcp(self, out, in_, r, w):
        self.P.op("dve", lambda e: e.reciprocal(out=out, in_=in_), r, w)

    def mm(self, out, lhsT, rhs, r, w, start=True, stop=True, inc=True, sgc=False):
        self.P.op("pe", lambda e: e.matmul(out, lhsT=lhsT, rhs=rhs, start=start, stop=stop, skip_group_check=sgc), r, w, inc=inc)

    def tr(self, out, in_, ident, r, w, inc=True):
        self.P.op("pe", lambda e: e.transpose(out, in_, ident), r, w, inc=inc)

    def ld(self, sem, out, in_, w, r=(), q="sp"):
        self.P.dma(q, sem, lambda e: e.dma_start(out=out, in_=in_), r, w)


def _consts():
    c = np.zeros((128, 128 * 2 + 64 * 3), np.float32)
    c[:, 0:128] = np.eye(128, dtype=np.float32)
    c[:, 128:256] = 1.0
    kk = np.arange(64)
    c[0:64, 256:320] = (kk[:, None] <= kk[None, :])
    c[0:64, 320:384] = (kk[None, :] >= kk[:, None])
    c[0:64, 384:448] = (kk[None, :] > kk[:, None])
    return c


def gdn_decl(nc, NTOK):
    def din(n, s, dt=F32):
        return nc.dram_tensor(n, list(s), dt, kind="ExternalInput").ap()
    io = {}
    io["xp"] = din("xp", [NTOK, 1024])
    io["gpre"] = din("gpre", [1, 1024])
    io["wq"] = din("wq", [1024, 1536])
    io["wba"] = din("wba", [1024, 8])
    io["convw"] = din("convw", [128, 32])
    io["alog"] = din("alog", [1, 4])
    io["dtb"] = din("dtb", [1, 4])
    io["onorm"] = din("onorm", [128, 1])
    io["cst"] = din("cst", [128, 448])
    return io


def build_gdn(NTOK):
    assert NTOK % 128 == 0
    nc = bass.Bass("TRN2", target_bir_lowering=False)
    io = gdn_decl(nc, NTOK)
    io["oT"] = nc.dram_tensor("oT", [512, NTOK], BF16, kind="ExternalOutput").ap()
    emit_gdn(nc, None, io, NTOK)
    return nc


def emit_gdn(nc, semst, io, NTOK, prew=(), fz=None):
    xp, gpre, wq, wba, convw, alog, dtb, onorm, cst = (io[n] for n in ("xp", "gpre", "wq", "wba", "convw", "alog", "dtb", "onorm", "cst"))
    oT = io.get("oT")
    with contextlib.ExitStack() as st:
        P = Prog(nc, st, semst, "g", prew)
        k = K(P)
        sb, ps = P.sb, P.ps
        if fz is not None:
            w0_b = sb("w0_b", [128, 4, 1024], BF16)
            yst = sb("yst", [128, 2, 1024], F32)
        wb = sb("wb", [128, 8, 1536], BF16)
        wbab = sb("wbab", [128, 8, 8], BF16)
        gpre_b = sb("gpre_b", [128, 1024], F32)
        convw_s = sb("convw_s", [128, 32], F32)
        alog_b = sb("alog_b", [64, 4], F32)
        dtb_b = sb("dtb_b", [64, 4], F32)
        negA = sb("negA", [64, 4], F32)
        onorm_s = sb("onorm_s", [128, 1], F32)
        cst_s = sb("cst_s", [128, 448], F32)
        ident_b = sb("ident_b", [128, 128], BF16)
        xc = sb("xc", [128, 8, 3 + 512], F32)
        S_f = sb("S_f", [128, 4, 128], F32)
        S_b = sb("S_b", [128, 4, 128], BF16)
        ident_f = cst_s[:, 0:128]
        ones_f = cst_s[:, 128:256]
        tri = cst_s[0:64, 256:320]
        mincl = cst_s[0:64, 320:384]
        mstrict = cst_s[0:64, 384:448]
        xs = sb("xs", [128, 4, 1024], F32)
        wstage = xs[:].rearrange("p (a b) d -> p a (b d)", a=2)
        junk = sb("junk", [128, 1024], BF16)
        ss = sb("ss", [128, 4], F32)
        rstd = sb("rstd", [128, 4], F32)
        hn = sb("hn", [128, 2, 1024], BF16)
        hnT = sb("hnT", [128, 8, 512], BF16)
        acc = sb("acc", [128, 512], F32)
        qkf = sb("qkf", [128, 512], F32)
        sq = sb("sq", [128, 512], F32)
        rtmp = sb("rtmp", [128, 512], F32)
        qT = sb("qT", [128, 2, 512], BF16)
        kT2 = sb("kT2", [128, 2, 2, 512], BF16)
        vT = sb("vT", [128, 4, 512], BF16)
        zs2 = sb("zs2", [128, 2, 4, 512], F32)
        bet = sb("bet", [64, 8, 4], F32)
        gt = sb("gt", [64, 8, 4], F32)
        gg = sb("gg", [64, 8, 4], F32)
        gc = sb("gc", [64, 8, 4], F32)
        kap = sb("kap", [64, 8, 4], F32)
        ngam = sb("ngam", [64, 8, 4], F32)
        bk = sb("bk", [64, 8, 4], F32)
        glast = sb("glast", [128, 8, 4], F32)
        gB = sb("gB", [64, 8, 128], F32)
        egc = sb("egc", [128, 512], F32)
        qdT = sb("qdT", [128, 4, 512], BF16)
        attnT = sb("attnT", [64, 4, 512], BF16)
        U = sb("U", [64, 4, 512], F32)
        UT = sb("UT", [64, 4, 2, 512], F32)
        Rr = sb("Rr", [64, 4, 512], F32)
        Rb = sb("Rb", [64, 4, 512], BF16)
        ktok = sb("ktok", [64, 2, 8, 128], BF16)
        vtok = sb("vtok", [64, 4, 8, 128], BF16)
        rr = sb("rr", [64, 4, 128], BF16)
        vn = sb("vn", [64, 4, 128], BF16)
        vnk = sb("vnk", [64, 4, 128], BF16)
        osq = sb("osq", [128, 512], F32)
        otmp = sb("otmp", [128, 512], F32)
        og = sb("og", [128, 4, 128], BF16)
        ps_tr = ps("ps_tr", [128, 1024], BF16)
        ps_pj = ps("ps_pj", [128, 512], F32)
        ps_ms = ps("ps_ms", [128, 512], F32)
        ps_bc = ps("ps_bc", [128, 512], F32)
        ps_sc = ps("ps_sc", [128, 4, 512], F32)
        B_TR, B_PJ, B_MS, B_BC = Tk("B_TR"), Tk("B_PJ"), Tk("B_MS"), Tk("B_BC")
        B_S = [Tk("B_S%d" % h) for h in range(4)]

        T = {}

        def tk(n):
            if n not in T:
                T[n] = Tk(n)
            return T[n]

        s_c = P.dmasem("c")
        s_w = [P.dmasem("w0"), P.dmasem("w1")]
        s_x = P.dmasem("x")
        s_o = P.dmasem("o")
        s_y = [P.dmasem("y0"), P.dmasem("y1")]
        k.ld(s_c, cst_s[:], cst[:, :], [tk("cst")])
        k.ld(s_c, gpre_b[:], gpre[0:1, :].partition_broadcast(128), [tk("gpre")])
        k.ld(s_c, convw_s[:], convw[:, :], [tk("convw")])
        k.ld(s_c, alog_b[:], alog[0:1, :].partition_broadcast(64), [tk("alog")])
        k.ld(s_c, dtb_b[:], dtb[0:1, :].partition_broadcast(64), [tk("dtb")])
        k.ld(s_c, onorm_s[:], onorm[:, :], [tk("onorm")])
        for _n in ("cst", "gpre", "convw", "alog", "dtb", "onorm"):
            tk(_n).w = (s_c, P.dcnt[s_c])
        k.cp(ident_b[:], ident_f, [tk("cst")], [tk("identb")])
        k.act(negA[:], alog_b[:], AF.Exp, [tk("alog")], [tk("negA")])
        k.ts(negA[:], negA[:], -1.0, ALU.mult, [], [tk("negA")])
        for kc in range(8):
            sl = kc % 2
            k.ld(s_w[sl], wstage[:, sl, 0:1536], wq[kc * 128:(kc + 1) * 128, :], [tk("wst%d" % sl)])
            k.ld(s_w[sl], wstage[:, sl, 1536:1544], wba[kc * 128:(kc + 1) * 128, :], [tk("wst%d" % sl)])
            k.cp(wb[:, kc, :], wstage[:, sl, 0:1536], [tk("wst%d" % sl)], [tk("wb")], eng="pool")
            k.cp(wbab[:, kc, :], wstage[:, sl, 1536:1544], [tk("wst%d" % sl)], [tk("wb")], eng="pool")
        if fz is not None:
            for h in range(4):
                sl = h % 2
                k.ld(s_w[sl], wstage[:, sl, 0:1024], fz["w0"][h * 128:(h + 1) * 128, :], [tk("wst%d" % sl)])
                k.cp(w0_b[:, h, :], wstage[:, sl, 0:1024], [tk("wst%d" % sl)], [tk("w0b")], eng="pool")
        P.op("dve", lambda e: e.memset(S_f[:], 0.0), [], [tk("S_f0"), tk("S_f1"), tk("S_f2"), tk("S_f3")])
        P.op("dve", lambda e: e.memset(S_b[:], 0.0), [], [tk("S_b0"), tk("S_b1"), tk("S_b2"), tk("S_b3")])
        P.op("dve", lambda e: e.memset(xc[:], 0.0), [], [tk("xc%d" % g) for g in range(8)])

        def gen_AB(ti):
            t0 = ti * 512
            TT = min(512, NTOK - t0)
            NS = TT // 128
            p = ti % 2
            kT = kT2[:, p]
            zs = zs2[:, p]
            kTn = "kT%d_" % p + "%d"
            k.ld(s_x, xs[:, 0:NS, :], xp[t0:t0 + TT, :].rearrange("(s p) d -> p s d", p=128),
                 [tk("xs")] + ([tk("wst0"), tk("wst1")] if ti == 0 else []))
            for s in range(NS):
                k.act(junk[:], xs[:, s, :], AF.Square, [tk("xs")], [tk("junk"), tk("ss")], accum_out=ss[:, s:s + 1])
            k.act(rstd[:, 0:NS], ss[:, 0:NS], AF.Sqrt, [tk("ss")], [tk("rstd")], scale=1.0 / 1024, bias=EPS)
            k.rcp(rstd[:, 0:NS], rstd[:, 0:NS], [], [tk("rstd")])
            yield
            for s in range(NS):
                k.stt(hn[:, s % 2, :], xs[:, s, :], rstd[:, s:s + 1], gpre_b[:], ALU.mult, ALU.mult,
                      [tk("xs"), tk("rstd"), tk("gpre")], [tk("hn%d" % (s % 2))])
                for kc in range(8):
                    k.tr(ps_tr[:, kc * 128:(kc + 1) * 128], hn[:, s % 2, kc * 128:(kc + 1) * 128], ident_b[:],
                         [tk("hn%d" % (s % 2)), tk("identb")], [B_TR], inc=(kc == 7))
                k.act(hnT[:, :, s * 128:(s + 1) * 128], ps_tr[:].rearrange("p (k t) -> p k t", t=128), AF.Identity,
                      [], [B_TR, tk("hnT")])
                yield
            for g in range(12):
                for kc in range(8):
                    k.mm(ps_pj[:, 0:TT], wb[:, kc, g * 128:(g + 1) * 128], hnT[:, kc, 0:TT], [tk("wb"), tk("hnT")], [B_PJ],
                         start=(kc == 0), stop=(kc == 7), inc=(kc == 7))
                if g < 8:
                    xg = tk("xc%d" % g)
                    k.cp(xc[:, g, 0:3], xc[:, g, 512:515], [], [xg])
                    k.act(xc[:, g, 3:3 + TT], ps_pj[:, 0:TT], AF.Identity, [], [B_PJ, xg])
                    k.ts(acc[:, 0:TT], xc[:, g, 3:3 + TT], convw_s[:, g * 4 + 3:g * 4 + 4], ALU.mult, [xg, tk("convw")], [tk("acc")])
                    for j in (2, 1, 0):
                        k.stt(acc[:, 0:TT], xc[:, g, j:j + TT], convw_s[:, g * 4 + j:g * 4 + j + 1], acc[:, 0:TT], ALU.mult, ALU.add,
                              [xg, tk("convw")], [tk("acc")])
                    if TT < 512:
                        pass
                    if g < 4:
                        dst = qT[:, g, 0:TT] if g < 2 else kT[:, g - 2, 0:TT]
                        dtk = tk("qT%d" % g) if g < 2 else tk(kTn % (g - 2))
                        k.act(qkf[:, 0:TT], acc[:, 0:TT], AF.Silu, [tk("acc")], [tk("qkf")])
                        k.act(sq[:, 0:TT], qkf[:, 0:TT], AF.Square, [tk("qkf")], [tk("sq")])
                        k.mm(ps_bc[:, 0:TT], ones_f, sq[:, 0:TT], [tk("cst"), tk("sq")], [B_BC])
                        if g < 2:
                            k.act(rtmp[:, 0:TT], ps_bc[:, 0:TT], AF.Ln, [], [B_BC, tk("rtmp")], scale=128.0, bias=128.0 * EPS)
                        else:
                            k.act(rtmp[:, 0:TT], ps_bc[:, 0:TT], AF.Ln, [], [B_BC, tk("rtmp")], scale=1.0, bias=EPS)
                        k.act(rtmp[:, 0:TT], rtmp[:, 0:TT], AF.Exp, [], [tk("rtmp")], scale=-0.5)
                        k.tt(dst, qkf[:, 0:TT], rtmp[:, 0:TT], ALU.mult, [tk("qkf"), tk("rtmp")], [dtk])
                    else:
                        k.act(vT[:, g - 4, 0:TT], acc[:, 0:TT], AF.Silu, [tk("acc")], [tk("vT%d" % (g - 4))])
                else:
                    k.act(zs[:, g - 8, 0:TT], ps_pj[:, 0:TT], AF.Silu, [], [B_PJ, tk("zs%d" % p)])
                yield

        def step(g_, n=1):
            if g_ is None:
                return
            for _ in range(n):
                try:
                    next(g_)
                except StopIteration:
                    return

        def drain(g_):
            if g_ is None:
                return
            for _ in g_:
                pass

        ntiles = (NTOK + 511) // 512
        cc_next = [0]
        drain(gen_AB(0))
        for ti in range(ntiles):
            t0 = ti * 512
            TT = min(512, NTOK - t0)
            NS = TT // 128
            NCH = TT // 64
            p = ti % 2
            kT = kT2[:, p]
            zs = zs2[:, p]
            kTn = "kT%d_" % p + "%d"
            g_next = gen_AB(ti + 1) if ti + 1 < ntiles else None
            for n in range(NCH):
                for kc in range(8):
                    k.mm(ps_ms[0:64, n * 8:(n + 1) * 8], hnT[:, kc, n * 64:(n + 1) * 64], wbab[:, kc, :], [tk("hnT"), tk("wb")], [B_MS],
                         start=(kc == 0), stop=(kc == 7), inc=(kc == 7 and n == NCH - 1))
            bav = ps_ms[0:64, 0:NCH * 8].rearrange("p (n c) -> p n c", c=8)
            k.act(bet[:, 0:NCH, :], bav[:, :, 0:4], AF.Sigmoid, [], [B_MS, tk("bet")])
            k.tt(gt[:, 0:NCH, :], bav[:, :, 4:8], dtb_b[:].unsqueeze(1).to_broadcast([64, NCH, 4]), ALU.add, [tk("dtb")], [B_MS, tk("gt")])
            k.act(gt[:, 0:NCH, :], gt[:, 0:NCH, :], AF.Exp, [], [tk("gt")])
            k.act(gt[:, 0:NCH, :], gt[:, 0:NCH, :], AF.Ln, [], [tk("gt")], bias=1.0)
            k.tt(gg[:, 0:NCH, :], gt[:, 0:NCH, :], negA[:].unsqueeze(1).to_broadcast([64, NCH, 4]), ALU.mult, [tk("gt"), tk("negA")], [tk("gg")])
            ggf = gg[:, 0:NCH, :].rearrange("p n h -> p (n h)")
            k.mm(ps_ms[0:64, 64:64 + NCH * 4], tri, ggf, [tk("cst"), tk("gg")], [B_MS])
            k.mm(ps_ms[:, 128:128 + NCH * 4], ones_f[0:64, :], ggf, [tk("cst"), tk("gg")], [B_MS])
            gcv = ps_ms[0:64, 64:64 + NCH * 4].rearrange("p (n h) -> p n h", h=4)
            glv = ps_ms[:, 128:128 + NCH * 4].rearrange("p (n h) -> p n h", h=4)
            k.cp(gc[:, 0:NCH, :], gcv, [], [B_MS, tk("gc")])
            k.act(glast[:, 0:NCH, :], glv, AF.Exp, [], [B_MS, tk("glast")])
            k.tt(kap[:, 0:NCH, :], glv[0:64], gc[:, 0:NCH, :], ALU.subtract, [tk("gc")], [B_MS, tk("kap")])
            k.act(kap[:, 0:NCH, :], kap[:, 0:NCH, :], AF.Exp, [], [tk("kap")])
            k.act(ngam[:, 0:NCH, :], gc[:, 0:NCH, :], AF.Exp, [tk("gc")], [tk("ngam")])
            k.ts(ngam[:, 0:NCH, :], ngam[:, 0:NCH, :], -1.0, ALU.mult, [], [tk("ngam")])
            k.tt(bk[:, 0:NCH, :], bet[:, 0:NCH, :], kap[:, 0:NCH, :], ALU.mult, [tk("bet"), tk("kap")], [tk("bk")])
            for j in range(6):
                src = kT[:, j, :] if j < 2 else vT[:, j - 2, :]
                stk = tk(kTn % j) if j < 2 else tk("vT%d" % (j - 2))
                for n in range(NCH):
                    k.tr(ps_tr[0:64, n * 128:(n + 1) * 128], src[:, n * 64:(n + 1) * 64], ident_b[:], [stk, tk("identb")], [B_TR],
                         inc=(n == NCH - 1))
                dst = ktok[:, j, 0:NCH, :] if j < 2 else vtok[:, j - 2, 0:NCH, :]
                dtk = tk("ktok%d" % j) if j < 2 else tk("vtok%d" % (j - 2))
                k.act(dst, ps_tr[0:64, 0:NCH * 128].rearrange("p (n d) -> p n d", d=128), AF.Identity, [], [B_TR, dtk])
            W = NCH * 64
            v3 = lambda ap_: ap_.rearrange("p (n c) -> p n c", c=64)
            for h in range(4):
                k.cp(gB[:, 0:NCH, :], gg[:, 0:NCH, h:h + 1].to_broadcast([64, NCH, 128]), [tk("gg")], [tk("gB")])
                for n in range(NCH):
                    k.mm(ps_sc[:, h, n * 64:(n + 1) * 64], gB[:, n, :], tri, [tk("gB"), tk("cst")], [B_S[h]], inc=(n == NCH - 1))
            for h in range(4):
                qh = h // 2
                t1h = UT[:, h, 1, 0:W]
                k.act(egc[:, 0:W], ps_sc[:, h, 0:W], AF.Exp, [], [B_S[h], tk("egc")])
                k.tt(qdT[:, h, 0:W], qT[:, qh, 0:W], egc[:, 0:W], ALU.mult, [tk("qT%d" % qh), tk("egc")], [tk("qdT%d" % h)])
                k.tt(v3(t1h), v3(ps_sc[0:64, h, 0:W]), gc[:, 0:NCH, h:h + 1].to_broadcast([64, NCH, 64]),
                     ALU.subtract, [tk("gc")], [B_S[h], tk("UT%d_1" % h)])
            for h in range(4):
                t1h, Dmh, Bsh = UT[:, h, 1, 0:W], Rr[:, h, 0:W], UT[:, h, 0, 0:W]
                k.ts(t1h, t1h, 0.0, ALU.min, [], [tk("UT%d_1" % h)])
                k.act(t1h, t1h, AF.Exp, [], [tk("UT%d_1" % h)])
                k.tt(v3(Dmh), v3(t1h), mincl.unsqueeze(1).to_broadcast([64, NCH, 64]), ALU.mult, [tk("UT%d_1" % h), tk("cst")], [tk("R%d" % h)])
                k.tt(v3(Bsh), mstrict.unsqueeze(1).to_broadcast([64, NCH, 64]), bet[:, 0:NCH, h:h + 1].to_broadcast([64, NCH, 64]), ALU.mult,
                     [tk("cst"), tk("bet")], [tk("UT%d_0" % h)])
            for h in range(4):
                qh = h // 2
                for n in range(NCH):
                    k.mm(ps_sc[0:64, h, n * 64:(n + 1) * 64], kT[:, qh, n * 64:(n + 1) * 64], qT[:, qh, n * 64:(n + 1) * 64],
                         [tk(kTn % qh), tk("qT%d" % qh)], [B_S[h]], inc=(n == NCH - 1))
                k.tt(attnT[:, h, 0:W], ps_sc[0:64, h, 0:W], Rr[:, h, 0:W], ALU.mult, [tk("R%d" % h)], [B_S[h], tk("attnT%d" % h)])
            for h in range(4):
                qh = h // 2
                for n in range(NCH):
                    k.mm(ps_sc[0:64, h, n * 64:(n + 1) * 64], kT[:, qh, n * 64:(n + 1) * 64], kT[:, qh, n * 64:(n + 1) * 64],
                         [tk(kTn % qh)], [B_S[h]], inc=(n == NCH - 1))
                k.tt(U[:, h, 0:W], ps_sc[0:64, h, 0:W], Rr[:, h, 0:W], ALU.mult, [tk("R%d" % h)], [B_S[h], tk("U%d" % h)])
                k.tt(U[:, h, 0:W], U[:, h, 0:W], UT[:, h, 0, 0:W], ALU.mult, [tk("UT%d_0" % h)], [tk("U%d" % h)])
            for h in range(4):
                for n in range(NCH):
                    k.tr(ps_sc[0:64, h, n * 64:(n + 1) * 64], U[:, h, n * 64:(n + 1) * 64], ident_f[0:64, 0:64], [tk("U%d" % h), tk("cst")], [B_S[h]],
                         inc=(n == NCH - 1))
                k.cp(UT[:, h, 0, 0:W], ps_sc[0:64, h, 0:W], [], [B_S[h], tk("UT%d_0" % h)])
            for h in range(4):
                k.stt(v3(Rr[:, h, 0:W]), v3(U[:, h, 0:W]), -1.0,
                      ident_f[0:64, 0:64].unsqueeze(1).to_broadcast([64, NCH, 64]), ALU.mult, ALU.add, [tk("U%d" % h), tk("cst")], [tk("R%d" % h)])
            W = NCH * 64
            cur = 0
            for lvl in range(1, 6):
                nxt = 1 - cur
                last = (lvl == 5)
                for h in range(4):
                    for n in range(NCH):
                        c = slice(n * 64, (n + 1) * 64)
                        k.mm(ps_sc[0:64, h, c], U[:, h, c], UT[:, h, cur, c], [tk("U%d" % h), tk("UT%d_%d" % (h, cur))], [B_S[h]], inc=(n == NCH - 1))
                    k.cp(UT[:, h, nxt, 0:W], ps_sc[0:64, h, 0:W], [], [B_S[h], tk("UT%d_%d" % (h, nxt))])
                step(g_next)
                if not last:
                    for h in range(4):
                        for n in range(NCH):
                            c = slice(n * 64, (n + 1) * 64)
                            k.mm(ps_sc[0:64, h, c], UT[:, h, cur, c], U[:, h, c], [tk("U%d" % h), tk("UT%d_%d" % (h, cur))], [B_S[h]], inc=(n == NCH - 1))
                        k.act(U[:, h, 0:W], ps_sc[0:64, h, 0:W], AF.Identity, [], [B_S[h], tk("U%d" % h)])
                    step(g_next)
                for h in range(4):
                    for n in range(NCH):
                        c = slice(n * 64, (n + 1) * 64)
                        k.mm(ps_sc[0:64, h, c], UT[:, h, nxt, c], Rr[:, h, c], [tk("UT%d_%d" % (h, nxt)), tk("R%d" % h)], [B_S[h]], inc=(n == NCH - 1))
                    if not last:
                        k.tt(Rr[:, h, 0:W], ps_sc[0:64, h, 0:W], Rr[:, h, 0:W], ALU.add, [], [B_S[h], tk("R%d" % h)])
                    else:
                        k.tt(Rb[:, h, 0:W], ps_sc[0:64, h, 0:W], Rr[:, h, 0:W], ALU.add, [tk("R%d" % h)], [B_S[h], tk("Rb%d" % h)])
                cur = nxt
                step(g_next)
            drain(g_next)
            for n in range(NCH):
                c = slice(n * 64, (n + 1) * 64)
                par = n % 2
                for h in range(4):
                    qh = h // 2
                    k.mm(ps_sc[0:64, h, 0:128], kT[:, qh, c], S_b[:, h, :], [tk(kTn % qh), tk("S_b%d" % h)], [B_S[h]])
                for h in range(4):
                    k.stt(rr[:, h, :], ps_sc[0:64, h, 0:128], ngam[:, n, h:h + 1], vtok[:, h, n, :], ALU.mult, ALU.add,
                          [tk("ngam"), tk("vtok%d" % h)], [B_S[h], tk("rr%d" % h)])
                for h in range(4):
                    k.mm(ps_sc[0:64, h, 128:256], Rb[:, h, c], rr[:, h, :], [tk("Rb%d" % h), tk("rr%d" % h)], [B_S[h]])
                for h in range(4):
                    k.ts(vnk[:, h, :], ps_sc[0:64, h, 128:256], bk[:, n, h:h + 1], ALU.mult, [tk("bk")], [B_S[h], tk("vnk%d" % h)])
                    k.act(vn[:, h, :], ps_sc[0:64, h, 128:256], AF.Identity, [tk("bet")], [B_S[h], tk("vn%d" % h)], scale=bet[:, n, h:h + 1])
                for h in range(4):
                    qh = h // 2
                    oc = slice(384 + par * 64, 384 + par * 64 + 64)
                    k.mm(ps_sc[:, h, 256:384], ktok[:, qh, n, :], vnk[:, h, :], [tk("ktok%d" % qh), tk("vnk%d" % h)], [B_S[h]])
                    k.mm(ps_sc[:, h, oc], S_b[:, h, :], qdT[:, h, c], [tk("S_b%d" % h), tk("qdT%d" % h)], [B_S[h]], start=True, stop=False, inc=False)
                    k.mm(ps_sc[:, h, oc], vn[:, h, :], attnT[:, h, c], [tk("vn%d" % h), tk("attnT%d" % h)], [B_S[h]], start=False, stop=True)
                for h in range(4):
                    k.stt(S_f[:, h, :], S_f[:, h, :], glast[:, n, h:h + 1], ps_sc[:, h, 256:384], ALU.mult, ALU.add,
                          [tk("glast")], [B_S[h], tk("S_f%d" % h)])
                    k.act(S_b[:, h, :], S_f[:, h, :], AF.Identity, [tk("S_f%d" % h)], [tk("S_b%d" % h)])
                if par == 1:
                    ov = ps_sc[:, :, 384:512]
                    tc0 = (n - 1) * 64
                    k.act(osq[:].rearrange("p (h t) -> p h t", t=128), ov, AF.Square, [], B_S + [tk("osq")])
                    k.mm(ps_bc[:, :], ones_f, osq[:], [tk("cst"), tk("osq")], [B_BC])
                    k.act(otmp[:], ps_bc[:], AF.Ln, [], [B_BC, tk("otmp")], scale=1.0 / 128, bias=EPS)
                    k.act(otmp[:], otmp[:], AF.Exp, [], [tk("otmp")], scale=-0.5)
                    k.tt(otmp[:].rearrange("p (h t) -> p h t", t=128), ov, otmp[:].rearrange("p (h t) -> p h t", t=128), ALU.mult,
                         [], B_S + [tk("otmp")])
                    k.stt(og[:], otmp[:].rearrange("p (h t) -> p h t", t=128), onorm_s[:, 0:1], zs[:, :, tc0:tc0 + 128], ALU.mult, ALU.mult,
                          [tk("otmp"), tk("onorm"), tk("zs%d" % p)], [tk("og")])
                    if fz is None:
                        k.ld(s_o, oT[:, t0 + tc0:t0 + tc0 + 128].rearrange("(h e) t -> e h t", e=128), og[:], [], r=[tk("og")])
                    else:
                        ysl = (t0 + tc0) // 128 % 2
                        for half, (pst, btk) in enumerate(((ps_pj, B_PJ), (ps_ms, B_MS))):
                            for h in range(4):
                                k.mm(pst[:, :], og[:, h, :], w0_b[:, h, half * 512:(half + 1) * 512], [tk("og"), tk("w0b")], [btk],
                                     start=(h == 0), stop=(h == 3), inc=(h == 3))
                        k.act(yst[:, ysl, 0:512], ps_pj[:, :], AF.Identity, [], [B_PJ, tk("yst%d" % ysl)])
                        k.cp(yst[:, ysl, 512:1024], ps_ms[:, :], [], [B_MS, tk("yst%d" % ysl)])
                        pos0 = t0 + tc0 - 48
                        nrows = fz["y0p"].shape[0]
                        if pos0 < 0:
                            k.ld(s_y[ysl], fz["y0p"][0:128 + pos0, :], yst[-pos0:128, ysl, :], [], r=[tk("yst%d" % ysl), tk("y0rows")])
                            rend = 128 + pos0
                        else:
                            nr = min(128, nrows - pos0)
                            rend = pos0 + max(nr, 0)
                            if nr > 0:
                                k.ld(s_y[ysl], fz["y0p"][pos0:pos0 + nr, :], yst[0:nr, ysl, :], [], r=[tk("yst%d" % ysl), tk("y0rows")])
                        while cc_next[0] < nrows and rend >= min(nrows, cc_next[0] + CC_ROWS):
                            r0, r1 = cc_next[0], min(nrows, cc_next[0] + CC_ROWS)
                            P.cc("cc", fz["scc"], fz["y0p"][r0:r1, :], fz["y0f"][r0:r1, :], [], [tk("y0rows")])
                            cc_next[0] = r1
        P.finish("sp", [tk("og")] + ([tk("yst0"), tk("yst1")] if fz is not None else []))
        P.emit()
        return P.final_events()


def gdn_inputs(inp, core, NTOK):
    b, hg = core // 4, core % 4
    L = 16 + inp["x"].shape[1]
    xp = np.zeros((NTOK, 1024), np.float32)
    n_real = min(L, NTOK - 48)
    xp[48:64] = inp["meta_tokens"]
    xp[64:48 + n_real] = inp["x"][b, :n_real - 16]
    W = inp["gdn_w_in"][0]
    qcols = np.arange(2 * hg * 128, (2 * hg + 2) * 128)
    kcols = 1024 + qcols
    vcols = 2048 + np.arange(4 * hg * 128, (4 * hg + 4) * 128)
    zcols = 4096 + np.arange(4 * hg * 128, (4 * hg + 4) * 128)
    bcols = 6144 + np.arange(4 * hg, 4 * hg + 4)
    acols = 6160 + np.arange(4 * hg, 4 * hg + 4)
    wq = np.ascontiguousarray(W[:, np.concatenate([qcols, kcols, vcols, zcols])])
    wba = np.ascontiguousarray(W[:, np.concatenate([bcols, acols])])
    cw = inp["gdn_conv_w"][0][:, np.concatenate([qcols, kcols, vcols])]
    convw = np.ascontiguousarray(cw.reshape(4, 8, 128).transpose(2, 1, 0).reshape(128, 32))
    return {
        "xp": xp, "gpre": inp["pre_norm"][0:1].copy(), "wq": wq, "wba": wba, "convw": convw,
        "alog": inp["gdn_a_log"][0:1, 4 * hg:4 * hg + 4].copy(), "dtb": inp["gdn_dt_bias"][0:1, 4 * hg:4 * hg + 4].copy(),
        "onorm": inp["gdn_out_norm"][0].reshape(128, 1).copy(), "cst": _consts(),
    }


def build_wout(NBLK):
    nc = bass.Bass("TRN2", target_bir_lowering=False)

    def din(n, s, dt=F32):
        return nc.dram_tensor(n, list(s), dt, kind="ExternalInput").ap()

    NT = NBLK * 128
    oTin = din("oTin", [2048, NT], BF16)
    resid = din("resid", [NT, 1024])
    w = din("w", [2048, 1024])
    gpost = din("gpost", [1, 1024])
    out = nc.dram_tensor("out", [NT, 1024], F32, kind="ExternalOutput").ap()
    with contextlib.ExitStack() as st:
        P = Prog(nc, st)
        k = K(P)
        sb, ps = P.sb, P.ps
        wsb = sb("wsb", [128, 16, 1024], BF16)
        wstage = sb("wstage", [128, 2, 1024], F32)
        gp_b = sb("gp_b", [128, 1024], F32)
        oTs = sb("oTs", [128, 2, 16, 128], BF16)
        rs = sb("rs", [128, 2, 1024], F32)
        junk = sb("junk", [128, 512], BF16)
        ss = sb("ss", [128, 2], F32)
        rstd = sb("rstd", [128, 1], F32)
        ot = sb("ot", [128, 2, 1024], F32)
        ps_y = ps("ps_y", [128, 2, 512], F32)
        B_Y = [Tk("B_Y0"), Tk("B_Y1")]
        T = {}

        def tk(n):
            if n not in T:
                T[n] = Tk(n)
            return T[n]
        s_c = P.dmasem("c")
        s_w = [P.dmasem("w0"), P.dmasem("w1")]
        s_i = [P.dmasem("i0"), P.dmasem("i1")]
        s_o = [P.dmasem("o0"), P.dmasem("o1")]
        k.ld(s_c, gp_b[:], gpost[0:1, :].partition_broadcast(128), [tk("gp")])
        for kc in range(16):
            sl = kc % 2
            k.ld(s_w[sl], wstage[:, sl, :], w[kc * 128:(kc + 1) * 128, :], [tk("wst%d" % sl)])
            k.cp(wsb[:, kc, :], wstage[:, sl, :], [tk("wst%d" % sl)], [tk("wsb")], eng="pool")
        for blk in range(NBLK):
            sl = blk % 2
            c0 = blk * 128
            k.ld(s_i[sl], oTs[:, sl, :, :], oTin[:, c0:c0 + 128].rearrange("(k p) t -> p k t", p=128), [tk("in%d" % sl)])
            k.ld(s_i[sl], rs[:, sl, :], resid[c0:c0 + 128, :], [tk("in%d" % sl)])
            for half in range(2):
                for kc in range(16):
                    k.mm(ps_y[:, half, :], oTs[:, sl, kc, :], wsb[:, kc, half * 512:(half + 1) * 512], [tk("in%d" % sl), tk("wsb")], [B_Y[half]],
                         start=(kc == 0), stop=(kc == 15), inc=(kc == 15))
                k.act(junk[:], ps_y[:, half, :], AF.Square, [], [B_Y[half], tk("junk"), tk("ss")], accum_out=ss[:, half:half + 1])
            k.tt(rstd[:], ss[:, 0:1], ss[:, 1:2], ALU.add, [tk("ss")], [tk("rstd")])
            k.act(rstd[:], rstd[:], AF.Sqrt, [], [tk("rstd")], scale=1.0 / 1024, bias=EPS)
            k.rcp(rstd[:], rstd[:], [], [tk("rstd")])
            for half in range(2):
                hs = slice(half * 512, (half + 1) * 512)
                k.stt(ot[:, sl, hs], ps_y[:, half, :], rstd[:, 0:1], gp_b[:, hs], ALU.mult, ALU.mult, [tk("rstd"), tk("gp")], [B_Y[half], tk("ot%d" % sl)])
            k.tt(ot[:, sl, :], ot[:, sl, :], rs[:, sl, :], ALU.add, [tk("in%d" % sl)], [tk("ot%d" % sl)])
            k.ld(s_o[sl], out[c0:c0 + 128, :], ot[:, sl, :], [], r=[tk("ot%d" % sl)])
        P.finish("sp", [tk("ot0"), tk("ot1")])
        P.emit()
    return nc


SCALE = 192.0 ** -0.5


def _consts_mla():
    c = np.zeros((128, 384), np.float32)
    c[:, 0:128] = np.eye(128, dtype=np.float32)
    c[:, 128:256] = 1.0
    kk = np.arange(128)
    c[:, 256:384] = (kk[None, :] >= kk[:, None])
    return c


def mla_decl(nc, NTOK2):
    def din(n, s, dt=F32):
        return nc.dram_tensor(n, list(s), dt, kind="ExternalInput").ap()
    io = {}
    io["g1"] = din("g1", [1, 1024])
    io["gkv"] = din("gkv", [1, 1024])
    io["glat"] = din("glat", [128, 1])
    io["gq"] = din("gq", [1, 256])
    io["wkvd"] = din("wkvd", [1024, 256])
    io["wuk"] = din("wuk", [128, 512])
    io["wuv"] = din("wuv", [128, 512])
    io["wmi"] = din("wmi", [1024, 768])
    io["wqu"] = din("wqu", [256, 1024])
    io["cos2T"] = din("cos2T", [64, NTOK2])
    io["sinsT"] = din("sinsT", [64, NTOK2])
    io["cstm"] = din("cstm", [128, 384])
    return io


def build_mla(NTOK2):
    assert NTOK2 % 128 == 0
    nc = bass.Bass("TRN2", target_bir_lowering=False)
    io = mla_decl(nc, NTOK2)
    io["h1p"] = nc.dram_tensor("h1p", [NTOK2, 1024], F32, kind="ExternalInput").ap()
    io["o1T"] = nc.dram_tensor("o1T", [512, NTOK2], BF16, kind="ExternalOutput").ap()
    emit_mla(nc, None, io, NTOK2)
    return nc


def emit_mla(nc, semst, io, NTOK2, prew=(), fz=None):
    NBK = NTOK2 // 128
    g1, gkv, glat, gq, wkvd, wuk, wuv, wmi, wqu, cos2T, sinsT, cst = (
        io[n] for n in ("g1", "gkv", "glat", "gq", "wkvd", "wuk", "wuv", "wmi", "wqu", "cos2T", "sinsT", "cstm"))
    h1p = io.get("h1p")
    o1T = io.get("o1T")
    with contextlib.ExitStack() as st:
        P = Prog(nc, st, semst, "m", prew)
        k = K(P)
        sb, ps = P.sb, P.ps
        if fz is not None:
            gp0_b = sb("gp0_b", [128, 1024], F32)
            w1_b = sb("w1_b", [128, 4, 1024], BF16)
            ys = sb("ys", [128, 2, 1024], F32)
            ssy = sb("ssy", [128, 1], F32)
            y1st = sb("y1st", [128, 1024], F32)
        ckvT = sb("ckvT", [128, NTOK2], BF16)
        kropeT = sb("kropeT", [128, NTOK2], BF16)
        ckvtok = sb("ckvtok", [128, NBK, 129], BF16)
        wkvd_b = sb("wkvd_b", [128, 8, 256], BF16)
        wuk_b = sb("wuk_b", [128, 4, 128], BF16)
        wukT_b = sb("wukT_b", [128, 4, 128], BF16)
        wuv_b = sb("wuv_b", [128, 4, 128], BF16)
        wmi_b = sb("wmi_b", [128, 8, 768], BF16)
        wqu_b = sb("wqu_b", [128, 2, 1024], BF16)
        wstage = sb("wstage", [128, 2, 1024], F32)
        g1_b = sb("g1_b", [128, 1024], F32)
        gkv_b = sb("gkv_b", [128, 1024], F32)
        gq_b = sb("gq_b", [128, 256], F32)
        glat_s = sb("glat_s", [128, 1], F32)
        cst_s = sb("cst_s", [128, 384], F32)
        ident_b = sb("ident_b", [128, 128], BF16)
        tri_b = sb("tri_b", [128, 128], BF16)
        zb = sb("zb", [128, 512], BF16)
        ones_f = cst_s[:, 128:256]
        hs = sb("hs", [128, 4, 1024], F32)
        junk = sb("junk", [128, 1024], BF16)
        ss = sb("ss", [128, 4], F32)
        rstd = sb("rstd", [128, 4], F32)
        hn1 = sb("hn1", [128, 1024], BF16)
        hkv = sb("hkv", [128, 1024], BF16)
        hn1T = sb("hn1T", [128, 8, 512], BF16)
        hkvT = sb("hkvT", [128, 8, 512], BF16)
        cs = sb("cs", [64, 512], F32)
        sn = sb("sn", [64, 512], F32)
        ckf = sb("ckf", [128, 512], F32)
        sq = sb("sq", [128, 512], F32)
        rt = sb("rt", [128, 512], F32)
        ra = sb("ra", [64, 2, 512], F32)
        rbb = sb("rbb", [64, 2, 512], F32)
        ssq = sb("ssq", [128, 1], F32)
        cqn = sb("cqn", [128, 256], BF16)
        cqT = sb("cqT", [128, 2, 512], BF16)
        zs1 = sb("zs1", [128, 4, 512], F32)
        qnT = sb("qnT", [128, 2, 512], BF16)
        qpT = sb("qpT", [128, 4, 512], BF16)
        qrT = sb("qrT", [128, 4, 512], BF16)
        pT = sb("pT", [128, 3, 512], BF16)
        pacc = sb("pacc", [128, 2, 512], F32)
        rdb = sb("rdb", [128, 512], F32)
        ocn = sb("ocn", [128, 512], BF16)
        og1 = sb("og1", [128, 4, 512], BF16)
        ps_tr = ps("ps_tr", [128, 1024], BF16)
        ps_pj = ps("ps_pj", [128, 512], F32)
        ps_p2 = ps("ps_p2", [128, 512], F32)
        ps_v = ps("ps_v", [128, 512], F32)
        ps_s = ps("ps_s", [128, 3, 512], F32)
        ps_o = ps("ps_o", [128, 512], F32)
        B_TR, B_PJ, B_P2, B_V = Tk("B_TR"), Tk("B_PJ"), Tk("B_P2"), Tk("B_V")
        B_S = [Tk("B_S0"), Tk("B_S1"), Tk("B_S2")]
        B_O = Tk("B_O")
        T = {}

        def tk(n):
            if n not in T:
                T[n] = Tk(n)
            return T[n]

        s_c = P.dmasem("c")
        s_w = [P.dmasem("w0"), P.dmasem("w1")]
        s_x = P.dmasem("x")
        s_o = P.dmasem("o")
        k.ld(s_c, cst_s[:], cst[:, :], [tk("cst")])
        k.ld(s_c, g1_b[:], g1[0:1, :].partition_broadcast(128), [tk("g1")])
        k.ld(s_c, gkv_b[:], gkv[0:1, :].partition_broadcast(128), [tk("gkv")])
        k.ld(s_c, gq_b[:], gq[0:1, :].partition_broadcast(128), [tk("gq")])
        k.ld(s_c, glat_s[:], glat[:, :], [tk("glat")])
        if fz is not None:
            s_yl = [P.dmasem("yl0"), P.dmasem("yl1")]
            s_h = P.dmasem("h")
            k.ld(s_c, gp0_b[:], fz["gp0"][0:1, :].partition_broadcast(128), [tk("gp0")])
            tk("gp0").w = None
        for _n in ("cst", "g1", "gkv", "gq", "glat", "gp0"):
            tk(_n).w = (s_c, P.dcnt[s_c])
        k.cp(ident_b[:], cst_s[:, 0:128], [tk("cst")], [tk("identb")])
        k.cp(tri_b[:], cst_s[:, 256:384], [tk("cst")], [tk("trib")])
        P.op("dve", lambda e: e.memset(zb[:], 0.0), [], [tk("zb")])
        P.op("pool", lambda e: e.memset(ckvtok[:], 1.0), [], [tk("ckvtok")])
        P.op("pool", lambda e: e.memset(kropeT[:], 0.0), [], [tk("kropeT")])
        P.op("pool", lambda e: e.memset(qrT[:], 0.0), [], [tk("qrT%d" % h_) for h_ in range(4)])
        wl = []
        for kc in range(8):
            wl.append((wkvd[kc * 128:(kc + 1) * 128, :], 256, wkvd_b[:, kc, :]))
        wl.append((wuk[:, :], 512, wuk_b[:].rearrange("p h d -> p (h d)")))
        wl.append((wuv[:, :], 512, wuv_b[:].rearrange("p h d -> p (h d)")))
        for kc in range(8):
            wl.append((wmi[kc * 128:(kc + 1) * 128, :], 768, wmi_b[:, kc, :]))
        for c2 in range(2):
            wl.append((wqu[c2 * 128:(c2 + 1) * 128, :], 1024, wqu_b[:, c2, :]))
        if fz is not None:
            for h in range(4):
                wl.append((fz["w1"][h * 128:(h + 1) * 128, :], 1024, w1_b[:, h, :]))
        for i, (src, n, dst) in enumerate(wl):
            sl = i % 2
            k.ld(s_w[sl], wstage[:, sl, 0:n], src, [tk("wst%d" % sl)])
            k.cp(dst, wstage[:, sl, 0:n], [tk("wst%d" % sl)], [tk("wts")], eng="pool")
        for h in range(4):
            k.tr(ps_tr[:, h * 128:(h + 1) * 128], wuk_b[:, h, :], ident_b[:], [tk("wts"), tk("identb")], [B_TR], inc=(h == 3))
        k.act(wukT_b[:].rearrange("p h d -> p (h d)"), ps_tr[:, 0:512], AF.Identity, [], [B_TR, tk("wukT")])

        ntiles = (NTOK2 + 511) // 512
        cc_next = [0]
        for ti in range(ntiles):
            t0 = ti * 512
            TT = min(512, NTOK2 - t0)
            NS = TT // 128
            blk0 = t0 // 128
            hsrc = h1p[t0:t0 + TT, :] if fz is None else fz["xp"][48 + t0:48 + t0 + TT, :]
            k.ld(s_x, hs[:, 0:NS, :], hsrc.rearrange("(s p) d -> p s d", p=128), [tk("hs")])
            k.ld(s_x, cs[:, 0:TT], cos2T[:, t0:t0 + TT], [tk("cs")])
            k.ld(s_x, sn[:, 0:TT], sinsT[:, t0:t0 + TT], [tk("cs")])
            tk("hs").w = (s_x, P.dcnt[s_x])
            tk("cs").w = (s_x, P.dcnt[s_x])
            if fz is not None:
                for s in range(NS):
                    ysl = s % 2
                    ytk = tk("ys%d" % ysl)
                    k.ld(s_yl[ysl], ys[:, ysl, :], fz["y0f"][t0 + s * 128:t0 + (s + 1) * 128, :], [ytk])
                    k.act(junk[:], ys[:, ysl, :], AF.Square, [ytk], [tk("junk"), tk("ssy")], accum_out=ssy[:])
                    k.act(ssy[:], ssy[:], AF.Sqrt, [], [tk("ssy")], scale=1.0 / 1024, bias=EPS)
                    k.rcp(ssy[:], ssy[:], [], [tk("ssy")])
                    k.stt(ys[:, ysl, :], ys[:, ysl, :], ssy[:, 0:1], gp0_b[:], ALU.mult, ALU.mult, [tk("ssy"), tk("gp0")], [ytk])
                    k.tt(hs[:, s, :], hs[:, s, :], ys[:, ysl, :], ALU.add, [ytk], [tk("hs")])
                k.ld(s_h, fz["h1s"][t0:t0 + TT, :].rearrange("(s p) d -> p s d", p=128), hs[:, 0:NS, :], [], r=[tk("hs")])
            for s in range(NS):
                k.act(junk[:], hs[:, s, :], AF.Square, [tk("hs")], [tk("junk"), tk("ss")], accum_out=ss[:, s:s + 1])
            k.act(rstd[:, 0:NS], ss[:, 0:NS], AF.Sqrt, [tk("ss")], [tk("rstd")], scale=1.0 / 1024, bias=EPS)
            k.rcp(rstd[:, 0:NS], rstd[:, 0:NS], [], [tk("rstd")])
            for s in range(NS):
                for (gb, gt_, dstT, nm) in ((g1_b, "g1", hn1T, "hn1"), (gkv_b, "gkv", hkvT, "hkv")):
                    buf = hn1 if nm == "hn1" else hkv
                    k.stt(buf[:], hs[:, s, :], rstd[:, s:s + 1], gb[:], ALU.mult, ALU.mult, [tk("hs"), tk("rstd"), tk(gt_)], [tk(nm)])
                    for kc in range(8):
                        k.tr(ps_tr[:, kc * 128:(kc + 1) * 128], buf[:, kc * 128:(kc + 1) * 128], ident_b[:], [tk(nm), tk("identb")], [B_TR],
                             inc=(kc == 7))
                    k.act(dstT[:, :, s * 128:(s + 1) * 128], ps_tr[:].rearrange("p (k t) -> p k t", t=128), AF.Identity, [], [B_TR, tk(nm + "T")])
            tsl = slice(t0, t0 + TT)
            for kc in range(8):
                k.mm(ps_pj[:, 0:TT], wkvd_b[:, kc, 0:128], hkvT[:, kc, 0:TT], [tk("wts"), tk("hkvT")], [B_PJ], start=(kc == 0), stop=(kc == 7), inc=(kc == 7))
            k.act(ckf[:, 0:TT], ps_pj[:, 0:TT], AF.Identity, [], [B_PJ, tk("ckf")])
            k.act(sq[:, 0:TT], ckf[:, 0:TT], AF.Square, [tk("ckf")], [tk("sq")])
            k.mm(ps_p2[:, 0:TT], ones_f, sq[:, 0:TT], [tk("cst"), tk("sq")], [B_P2])
            k.act(rt[:, 0:TT], ps_p2[:, 0:TT], AF.Ln, [], [B_P2, tk("rt")], scale=1.0 / 128, bias=EPS)
            k.act(rt[:, 0:TT], rt[:, 0:TT], AF.Exp, [], [tk("rt")], scale=-0.5)
            k.stt(ckvT[:, tsl], ckf[:, 0:TT], glat_s[:, 0:1], rt[:, 0:TT], ALU.mult, ALU.mult, [tk("ckf"), tk("glat"), tk("rt")], [tk("ckvT")])
            for kc in range(8):
                k.mm(ps_pj[0:64, 0:TT], wkvd_b[:, kc, 128:192], hkvT[:, kc, 0:TT], [tk("wts"), tk("hkvT")], [B_PJ], start=(kc == 0), stop=(kc == 7), inc=(kc == 7))
            for kc in range(8):
                k.mm(ps_p2[0:64, 0:TT], wkvd_b[:, kc, 192:256], hkvT[:, kc, 0:TT], [tk("wts"), tk("hkvT")], [B_P2], start=(kc == 0), stop=(kc == 7), inc=(kc == 7))
            k.tt(ra[:, 0, 0:TT], ps_pj[0:64, 0:TT], cs[:, 0:TT], ALU.mult, [tk("cs")], [B_PJ, tk("ra0")])
            k.tt(rbb[:, 0, 0:TT], ps_p2[0:64, 0:TT], sn[:, 0:TT], ALU.mult, [tk("cs")], [B_P2, tk("rbb0")])
            k.tt(kropeT[0:64, tsl], ra[:, 0, 0:TT], rbb[:, 0, 0:TT], ALU.add, [tk("ra0"), tk("rbb0")], [tk("kropeT")])
            for s in range(NS):
                k.tr(ps_tr[:, s * 128:(s + 1) * 128], ckvT[:, t0 + s * 128:t0 + (s + 1) * 128], ident_b[:], [tk("ckvT"), tk("identb")], [B_TR], inc=(s == NS - 1))
            k.act(ckvtok[:, blk0:blk0 + NS, 0:128], ps_tr[:, 0:NS * 128].rearrange("p (s d) -> p s d", d=128), AF.Identity, [], [B_TR, tk("ckvtok")])
            for s in range(NS):
                for kc in range(8):
                    k.mm(ps_v[:, 0:256], hn1T[:, kc, s * 128:(s + 1) * 128], wmi_b[:, kc, 0:256], [tk("hn1T"), tk("wts")], [B_V], start=(kc == 0), stop=(kc == 7), inc=(kc == 7))
                k.act(junk[:, 0:256], ps_v[:, 0:256], AF.Square, [], [B_V, tk("junk"), tk("ssq")], accum_out=ssq[:])
                k.act(ssq[:], ssq[:], AF.Sqrt, [], [tk("ssq")], scale=1.0 / 256, bias=EPS)
                k.rcp(ssq[:], ssq[:], [], [tk("ssq")])
                k.stt(cqn[:], ps_v[:, 0:256], ssq[:, 0:1], gq_b[:], ALU.mult, ALU.mult, [tk("ssq"), tk("gq")], [B_V, tk("cqn")])
                for c2 in range(2):
                    k.tr(ps_tr[:, c2 * 128:(c2 + 1) * 128], cqn[:, c2 * 128:(c2 + 1) * 128], ident_b[:], [tk("cqn"), tk("identb")], [B_TR], inc=(c2 == 1))
                k.act(cqT[:, :, s * 128:(s + 1) * 128], ps_tr[:, 0:256].rearrange("p (c t) -> p c t", t=128), AF.Identity, [], [B_TR, tk("cqT")])
            for h in range(4):
                for kc in range(8):
                    k.mm(ps_pj[:, 0:TT], wmi_b[:, kc, 256 + h * 128:256 + (h + 1) * 128], hn1T[:, kc, 0:TT], [tk("wts"), tk("hn1T")], [B_PJ],
                         start=(kc == 0), stop=(kc == 7), inc=(kc == 7))
                k.act(zs1[:, h, 0:TT], ps_pj[:, 0:TT], AF.Silu, [], [B_PJ, tk("zs1")])
            for hp in range(2):
                hh = (2 * hp, 2 * hp + 1)
                sets = {hh[0]: (ps_pj, B_PJ, ps_p2, B_P2, 0), hh[1]: (ps_v, B_V, ps_o, B_O, 1)}
                for h in hh:
                    pa, ba, pb, bb, u = sets[h]
                    for c2 in range(2):
                        k.mm(pa[:, 0:TT], wqu_b[:, c2, h * 256:h * 256 + 128], cqT[:, c2, 0:TT], [tk("wts"), tk("cqT")], [ba], start=(c2 == 0), stop=(c2 == 1), inc=(c2 == 1))
                for h in hh:
                    pa, ba, pb, bb, u = sets[h]
                    k.act(qnT[:, u, 0:TT], pa[:, 0:TT], AF.Identity, [], [ba, tk("qnT%d" % u)])
                for h in hh:
                    pa, ba, pb, bb, u = sets[h]
                    k.mm(pb[:, 0:TT], wukT_b[:, h, :], qnT[:, u, 0:TT], [tk("wukT"), tk("qnT%d" % u)], [bb])
                for h in hh:
                    pa, ba, pb, bb, u = sets[h]
                    k.act(qpT[:, h, 0:TT], pb[:, 0:TT], AF.Identity, [], [bb, tk("qpT%d" % h)])
                for h in hh:
                    pa, ba, pb, bb, u = sets[h]
                    for c2 in range(2):
                        k.mm(pa[0:64, 0:TT], wqu_b[:, c2, h * 256 + 128:h * 256 + 192], cqT[:, c2, 0:TT], [tk("wts"), tk("cqT")], [ba], start=(c2 == 0), stop=(c2 == 1), inc=(c2 == 1))
                    for c2 in range(2):
                        k.mm(pb[0:64, 0:TT], wqu_b[:, c2, h * 256 + 192:h * 256 + 256], cqT[:, c2, 0:TT], [tk("wts"), tk("cqT")], [bb], start=(c2 == 0), stop=(c2 == 1), inc=(c2 == 1))
                for h in hh:
                    pa, ba, pb, bb, u = sets[h]
                    k.tt(ra[:, u, 0:TT], pa[0:64, 0:TT], cs[:, 0:TT], ALU.mult, [tk("cs")], [ba, tk("ra%d" % u)])
                    k.tt(rbb[:, u, 0:TT], pb[0:64, 0:TT], sn[:, 0:TT], ALU.mult, [tk("cs")], [bb, tk("rbb%d" % u)])
                    k.tt(qrT[0:64, h, 0:TT], ra[:, u, 0:TT], rbb[:, u, 0:TT], ALU.add, [tk("ra%d" % u), tk("rbb%d" % u)], [tk("qrT%d" % h)])
            nkb = blk0 + NS

            def emit_s(h, j):
                jj = j - blk0
                qlo = max(0, jj) * 128
                buf = j % 3
                ksl = slice(j * 128, (j + 1) * 128)
                k.mm(ps_s[:, buf, qlo:TT], ckvT[:, ksl], qpT[:, h, qlo:TT], [tk("ckvT"), tk("qpT%d" % h)], [B_S[buf]], start=True, stop=False, inc=False)
                k.mm(ps_s[:, buf, qlo:TT], kropeT[:, ksl], qrT[:, h, qlo:TT], [tk("kropeT"), tk("qrT%d" % h)], [B_S[buf]], start=False, stop=True)
                k.act(pT[:, buf, qlo:TT], ps_s[:, buf, qlo:TT], AF.Exp, [], [B_S[buf], tk("pT%d" % buf)], scale=SCALE)
                if jj >= 0:
                    k.tt(pT[:, buf, qlo:qlo + 128], pT[:, buf, qlo:qlo + 128], tri_b[:], ALU.mult, [tk("trib")], [tk("pT%d" % buf)])
                if j == 0:
                    k.cp(pacc[:, h % 2, 0:TT], pT[:, buf, 0:TT], [tk("pT%d" % buf)], [tk("pacc%d" % (h % 2))])
                else:
                    k.tt(pacc[:, h % 2, qlo:TT], pacc[:, h % 2, qlo:TT], pT[:, buf, qlo:TT], ALU.add, [tk("pT%d" % buf)], [tk("pacc%d" % (h % 2))])

            def emit_pv(h, j):
                jj = j - blk0
                qlo = max(0, jj) * 128
                buf = j % 3
                k.mm(ps_o[:, qlo:TT], ckvtok[:, j, 0:128], pT[:, buf, qlo:TT], [tk("pT%d" % buf), tk("ckvtok")], [B_O],
                     start=(j == 0), stop=(j == nkb - 1))

            for h in range(4):
                if h == 0:
                    emit_s(h, 0)
                    if nkb > 1:
                        emit_s(h, 1)
                for j in range(nkb):
                    if j + 2 < nkb:
                        emit_s(h, j + 2)
                    emit_pv(h, j)
                if h < 3:
                    emit_s(h + 1, 0)
                    if nkb > 1:
                        emit_s(h + 1, 1)
                k.mm(ps_v[:, 0:TT], ones_f, pacc[:, h % 2, 0:TT], [tk("cst"), tk("pacc%d" % (h % 2))], [B_V])
                k.act(rdb[:, 0:TT], ps_v[:, 0:TT], AF.Ln, [], [B_V, tk("rdb")])
                k.act(rdb[:, 0:TT], rdb[:, 0:TT], AF.Exp, [], [tk("rdb")], scale=-1.0)
                k.tt(ocn[:, 0:TT], ps_o[:, 0:TT], rdb[:, 0:TT], ALU.mult, [tk("rdb")], [B_O, tk("ocn")])
                k.mm(ps_pj[:, 0:TT], wuv_b[:, h, :], ocn[:, 0:TT], [tk("wts"), tk("ocn")], [B_PJ])
                k.tt(og1[:, h, 0:TT], ps_pj[:, 0:TT], zs1[:, h, 0:TT], ALU.mult, [tk("zs1")], [B_PJ, tk("og1")])
            if fz is None:
                k.ld(s_o, o1T[:, t0:t0 + TT].rearrange("(h e) t -> e h t", e=128), og1[:, :, 0:TT], [], r=[tk("og1")])
            else:
                for s in range(NS):
                    for half, (pst, btk) in enumerate(((ps_pj, B_PJ), (ps_p2, B_P2))):
                        for h in range(4):
                            k.mm(pst[:, :], og1[:, h, s * 128:(s + 1) * 128], w1_b[:, h, half * 512:(half + 1) * 512], [tk("og1"), tk("wts")], [btk],
                                 start=(h == 0), stop=(h == 3), inc=(h == 3))
                    k.act(y1st[:, 0:512], ps_pj[:, :], AF.Identity, [], [B_PJ, tk("y1st")])
                    k.cp(y1st[:, 512:1024], ps_p2[:, :], [], [B_P2, tk("y1st")])
                    k.ld(s_o, fz["y1p"][t0 + s * 128:t0 + (s + 1) * 128, :], y1st[:], [], r=[tk("y1st"), tk("y1rows")])
                rend = t0 + TT
                while cc_next[0] < NTOK2 and rend >= min(NTOK2, cc_next[0] + CC_ROWS):
                    r0, r1 = cc_next[0], min(NTOK2, cc_next[0] + CC_ROWS)
                    P.cc("cc", fz["scc"], fz["y1p"][r0:r1, :], fz["y1f"][r0:r1, :], [], [tk("y1rows")])
                    cc_next[0] = r1
        P.finish("sp", [tk("og1")] + ([tk("y1st"), tk("hs")] if fz is not None else []))
        P.emit()
        return P.final_events()


def rope_tables_T(n):
    inv = (np.float32(10000.0) ** (-(np.arange(0, 64, 2, dtype=np.float32)) / np.float32(64))).astype(np.float32)
    ang = (np.arange(n, dtype=np.float32)[:, None] * inv[None, :]).astype(np.float32)
    cos, sin = np.cos(ang).astype(np.float32), np.sin(ang).astype(np.float32)
    cos2T = np.ascontiguousarray(np.concatenate([cos, cos], 1).T)
    sinsT = np.ascontiguousarray(np.concatenate([-sin, sin], 1).T)
    return cos2T, sinsT


def mla_inputs(inp, core, h1b, NTOK2):
    hg = core % 4
    h1p = None
    if h1b is not None:
        L = h1b.shape[0]
        h1p = np.zeros((NTOK2, 1024), np.float32)
        h1p[:L] = h1b
    kd = inp["kv_w_down"]
    wkvd = np.ascontiguousarray(np.concatenate([kd[:, 0:128], kd[:, 128:192], kd[:, 160:192], kd[:, 128:160]], 1))
    ku = inp["kv_w_up"].reshape(128, 16, 256)[:, 4 * hg:4 * hg + 4]
    wuk = np.ascontiguousarray(ku[:, :, 0:128].reshape(128, 512))
    wuv = np.ascontiguousarray(ku[:, :, 128:256].reshape(128, 512))
    mi = inp["mla_w_in"][0]
    wmi = np.ascontiguousarray(np.concatenate([mi[:, 0:256], mi[:, 256 + 512 * hg:256 + 512 * (hg + 1)]], 1))
    qu = inp["mla_w_q_up"][0].reshape(256, 16, 192)[:, 4 * hg:4 * hg + 4]
    wqu = np.ascontiguousarray(np.concatenate([qu[:, :, 0:128], qu[:, :, 128:192], qu[:, :, 160:192], qu[:, :, 128:160]], 2).reshape(256, 1024))
    cos2T, sinsT = rope_tables_T(NTOK2)
    d = {} if h1p is None else {"h1p": h1p}
    d.update(_mla_rest(inp, wkvd, wuk, wuv, wmi, wqu, cos2T, sinsT))
    return d


def _mla_rest(inp, wkvd, wuk, wuv, wmi, wqu, cos2T, sinsT):
    return {
        "g1": inp["pre_norm"][1:2].copy(), "gkv": inp["kv_norm"].reshape(1, 1024).copy(),
        "glat": inp["kv_latent_norm"].reshape(128, 1).copy(), "gq": inp["mla_q_latent_norm"][0:1].copy(),
        "wkvd": wkvd, "wuk": wuk, "wuv": wuv, "wmi": wmi, "wqu": wqu, "cos2T": cos2T, "sinsT": sinsT, "cstm": _consts_mla(),
    }


def emit_fin(nc, semst, h1s, y1f, gp1, out, NBK, prew=()):
    with contextlib.ExitStack() as st:
        P = Prog(nc, st, semst, "f", prew)
        k = K(P)
        sb = P.sb
        gp_b = sb("gp_b", [128, 1024], F32)
        NB_ = 6
        hb = sb("hb", [128, NB_, 1024], F32)
        yb = sb("yb", [128, NB_, 1024], F32)
        junk = sb("junk", [128, 1024], BF16)
        ss = sb("ss", [128, 1], F32)
        T = {}

        def tk(n):
            if n not in T:
                T[n] = Tk(n)
            return T[n]
        s_c = P.dmasem("c")
        s_i = [P.dmasem("i%d" % i_) for i_ in range(NB_)]
        s_o = [P.dmasem("o%d" % i_) for i_ in range(NB_)]
        k.ld(s_c, gp_b[:], gp1[0:1, :].partition_broadcast(128), [tk("gp")])
        for blk in range(NBK):
            sl = blk % NB_
            rows = slice(blk * 128, (blk + 1) * 128)
            k.ld(s_i[sl], hb[:, sl, :], h1s[rows, :], [tk("hb%d" % sl)])
            k.ld(s_i[sl], yb[:, sl, :], y1f[rows, :], [tk("yb%d" % sl)])
            tk("hb%d" % sl).w = (s_i[sl], P.dcnt[s_i[sl]])
            k.act(junk[:], yb[:, sl, :], AF.Square, [tk("yb%d" % sl)], [tk("junk"), tk("ss")], accum_out=ss[:])
            k.act(ss[:], ss[:], AF.Sqrt, [], [tk("ss")], scale=1.0 / 1024, bias=EPS)
            k.rcp(ss[:], ss[:], [], [tk("ss")])
            k.stt(yb[:, sl, :], yb[:, sl, :], ss[:, 0:1], gp_b[:], ALU.mult, ALU.mult, [tk("ss"), tk("gp")], [tk("yb%d" % sl)])
            k.tt(yb[:, sl, :], yb[:, sl, :], hb[:, sl, :], ALU.add, [tk("hb%d" % sl)], [tk("yb%d" % sl)], eng=("pool" if blk % 3 == 0 else "dve"))
            k.ld(s_o[sl], out[rows, :], yb[:, sl, :], [], r=[tk("yb%d" % sl)])
        P.finish("sp", [tk("yb%d" % i_) for i_ in range(NB_)])
        P.emit()
        return P.final_events()


GROUPS = [[0, 1, 2, 3], [4, 5, 6, 7]]
CC_ROWS = 1024


def emit_allreduce(nc, ev, src, dst, scc):
    with nc.Block() as block:
        @block.gpsimd
        def _(g):
            for hsem, v in ev:
                g.wait_ge(hsem, v)
            rows = src.ap().shape[0]
            n = 0
            for r0 in range(0, rows, CC_ROWS):
                r1 = min(rows, r0 + CC_ROWS)
                g.collective_compute("AllReduce", ALU.add, replica_groups=GROUPS,
                                     ins=[src.ap()[r0:r1, :]], outs=[dst.ap()[r0:r1, :]]).then_inc(scc)
                n += 1
            g.wait_ge(scc, n)
    rows_ = src.ap().shape[0]
    return (rows_ + CC_ROWS - 1) // CC_ROWS


def build_fused(NTOK, NTOK2):
    nc = bass.Bass("TRN2", target_bir_lowering=False)

    def din(n, s_, dt=F32):
        return nc.dram_tensor(n, list(s_), dt, kind="ExternalInput").ap()
    ioG = gdn_decl(nc, NTOK)
    ioM = mla_decl(nc, NTOK2)
    w0 = din("w0", [512, 1024])
    gp0 = din("gp0", [1, 1024])
    w1 = din("w1", [512, 1024])
    gp1 = din("gp1", [1, 1024])
    out = nc.dram_tensor("out", [NTOK2, 1024], F32, kind="ExternalOutput").ap()
    y0p = nc.dram_tensor("y0p", [NTOK2, 1024], F32)
    y0f = nc.dram_tensor("y0f", [NTOK2, 1024], F32)
    h1s = nc.dram_tensor("h1s", [NTOK2, 1024], F32)
    y1p = nc.dram_tensor("y1p", [NTOK2, 1024], F32)
    y1f = nc.dram_tensor("y1f", [NTOK2, 1024], F32)
    with contextlib.ExitStack() as semst:
        scc0 = semst.enter_context(nc.semaphore("cc0"))
        scc1 = semst.enter_context(nc.semaphore("cc1"))
        ev = emit_gdn(nc, semst, ioG, NTOK, fz=dict(y0p=y0p.ap(), y0f=y0f.ap(), scc=scc0, w0=w0))
        ev = emit_mla(nc, semst, ioM, NTOK2, prew=ev,
                      fz=dict(xp=ioG["xp"], y0f=y0f.ap(), gp0=gp0, h1s=h1s.ap(), y1p=y1p.ap(), y1f=y1f.ap(), scc=scc1, w1=w1))
        emit_fin(nc, semst, h1s.ap(), y1f.ap(), gp1, out, NTOK2 // 128, prew=ev)
    return nc


def fused_inputs(inp, core, NTOK, NTOK2):
    hg = core % 4
    d = gdn_inputs(inp, core, NTOK)
    d.update(mla_inputs(inp, core, None, NTOK2))
    d["w0"] = np.ascontiguousarray(inp["gdn_w_out"][0][hg * 512:(hg + 1) * 512])
    d["w1"] = np.ascontiguousarray(inp["mla_w_out"][0][hg * 512:(hg + 1) * 512])
    d["gp0"] = inp["post_norm"][0:1].copy()
    d["gp1"] = inp["post_norm"][1:2].copy()
    return d


_NC_CACHE = {}


def _get(name, fn, *a):
    key = (name,) + a
    if key not in _NC_CACHE:
        _NC_CACHE[key] = fn(*a)
    return _NC_CACHE[key]


def kernel(**inputs):
    inp = {k_: np.ascontiguousarray(np.asarray(v)) for k_, v in inputs.items()}
    B, SEQ, D = inp["x"].shape
    L = SEQ + 16
    NTOK2 = ((L + 127) // 128) * 128
    NTOK = ((NTOK2 + 48 + 127) // 128) * 128
    cores = list(range(8))
    nc = _get("fused", build_fused, NTOK, NTOK2)
    res = run_bass_kernel_spmd(nc, [fused_inputs(inp, c, NTOK, NTOK2) for c in cores], core_ids=cores).results
    return np.stack([np.asarray(res[4 * b]["out"])[16:L] for b in range(B)], 0).astype(np.float32)
```

```python
import contextlib
import numpy as np
import ml_dtypes
import concourse.bass as bass
import concourse.mybir as mybir
from concourse.bass_utils import run_bass_kernel_spmd

F32 = mybir.dt.float32
BF16 = mybir.dt.bfloat16
AF = mybir.ActivationFunctionType
ALU = mybir.AluOpType
EPS = 1e-6


class Tk:
    __slots__ = ("name", "w", "r")

    def __init__(self, name):
        self.name = name
        self.w = None
        self.r = []


class Prog:
    ENGS = ("pe", "act", "dve", "pool", "sp")

    def __init__(self, nc, stack, semst=None, pfx="", prew=()):
        self.nc = nc
        self.stack = stack
        self.semst = semst if semst is not None else stack
        self.pfx = pfx
        self.prew = list(prew)
        self.ops = {e: [] for e in self.ENGS}
        self.cnt = {e: 0 for e in self.ENGS}
        self.known = {e: {} for e in self.ENGS}
        self.sems = {}
        self.dcnt = {}
        for e in self.ENGS:
            self.sems[e] = self.semst.enter_context(nc.semaphore(pfx + "s_" + e))

    def final_events(self):
        ev = [(self.sems[e], self.cnt[e]) for e in self.ENGS if self.cnt[e] > 0]
        ev += [(self.sems[n], c) for n, c in self.dcnt.items() if c > 0]
        return ev

    def sb(self, name, shape, dt):
        return self.stack.enter_context(self.nc.sbuf_tensor(self.pfx + name, list(shape), dt))

    def ps(self, name, shape, dt):
        return self.stack.enter_context(self.nc.psum_tensor(self.pfx + name, list(shape), dt))

    def dmasem(self, name):
        self.sems[name] = self.semst.enter_context(self.nc.semaphore(self.pfx + "d_" + name))
        self.dcnt[name] = 0
        return name

    def _need(self, eng, ev, waits):
        if ev is None:
            return
        key, val = ev
        if key == eng and eng == "pe":
            return
        if self.known[eng].get(key, 0) >= val:
            return
        if key in self.ENGS and key != eng:
            assert self.cnt[key] >= val, (eng, ev, self.cnt[key])
        self.known[eng][key] = val
        for i, (k, v) in enumerate(waits):
            if k == key:
                waits[i] = (k, max(v, val))
                return
        waits.append((key, val))

    def _deps(self, eng, reads, writes):
        waits = []
        for t in reads:
            self._need(eng, t.w, waits)
        for t in writes:
            self._need(eng, t.w, waits)
            for ev in t.r:
                self._need(eng, ev, waits)
        return waits

    def _mark(self, ev, reads, writes):
        for t in writes:
            t.w = ev
            t.r = []
        for t in reads:
            if t not in writes:
                t.r.append(ev)
                if len(t.r) > 8:
                    d = {}
                    for k, v in t.r:
                        d[k] = max(d.get(k, 0), v)
                    t.r = list(d.items())

    def op(self, eng, fn, reads=(), writes=(), inc=True):
        waits = self._deps(eng, reads, writes)
        if inc:
            self.cnt[eng] += 1
            ev = (eng, self.cnt[eng])
        else:
            ev = (eng, self.cnt[eng] + 1)
        self._mark(ev, reads, writes)
        self.ops[eng].append((fn, waits, ("c", inc)))

    def dma(self, q, sem, fn, reads=(), writes=()):
        waits = self._deps(q, reads, writes)
        self.dcnt[sem] += 16
        ev = (sem, self.dcnt[sem])
        self._mark(ev, reads, writes)
        self.ops[q].append((fn, waits, ("d", sem)))

    def cc(self, semname, hsem, src, dst, reads=(), writes=()):
        if semname not in self.sems:
            self.sems[semname] = hsem
            self.dcnt[semname] = 0
        waits = self._deps("pool", reads, writes)
        self.dcnt[semname] += 1
        ev = (semname, self.dcnt[semname])
        self._mark(ev, reads, writes)
        fn = lambda e: e.collective_compute("AllReduce", ALU.add, replica_groups=GROUPS, ins=[src], outs=[dst])
        self.ops["pool"].append((fn, waits, ("k", semname)))

    def finish(self, eng, tks):
        waits = []
        for t in tks:
            self._need(eng, t.w, waits)
            for ev in t.r:
                self._need(eng, ev, waits)
        self.ops[eng].append((None, waits, ("w", None)))

    def emit(self):
        nc, sems, ops = self.nc, self.sems, self.ops
        prew = self.prew
        with nc.Block() as block:
            def run(name, e):
                for hsem, v in prew:
                    e.wait_ge(hsem, v)
                for fn, waits, kind in ops[name]:
                    for k, v in waits:
                        e.wait_ge(sems[k], v)
                    if fn is None:
                        continue
                    ins = fn(e)
                    if kind[0] == "c":
                        if kind[1]:
                            ins.then_inc(sems[name], 1)
                    elif kind[0] == "k":
                        ins.then_inc(sems[kind[1]], 1)
                    else:
                        ins.then_inc(sems[kind[1]], 16)

            @block.tensor
            def _(e):
                run("pe", e)

            @block.scalar
            def _(e):
                run("act", e)

            @block.vector
            def _(e):
                run("dve", e)

            @block.gpsimd
            def _(e):
                run("pool", e)

            @block.sync
            def _(e):
                run("sp", e)


class K:
    def __init__(self, P):
        self.P = P

    def act(self, out, in_, func, r, w, **kw):
        self.P.op("act", lambda e: e.activation(out=out, in_=in_, func=func, **kw), r, w)

    def tt(self, out, in0, in1, op, r, w, eng="dve"):
        self.P.op(eng, lambda e: e.tensor_tensor(out=out, in0=in0, in1=in1, op=op), r, w)

    def ts(self, out, in0, s1, op0, r, w, s2=None, op1=None, eng="dve"):
        if op1 is None:
            self.P.op(eng, lambda e: e.tensor_scalar(out=out, in0=in0, scalar1=s1, scalar2=None, op0=op0), r, w)
        else:
            self.P.op(eng, lambda e: e.tensor_scalar(out=out, in0=in0, scalar1=s1, scalar2=s2, op0=op0, op1=op1), r, w)

    def stt(self, out, in0, scalar, in1, op0, op1, r, w, eng="dve"):
        self.P.op(eng, lambda e: e.scalar_tensor_tensor(out=out, in0=in0, scalar=scalar, in1=in1, op0=op0, op1=op1), r, w)

    def cp(self, out, in_, r, w, eng="dve"):
        self.P.op(eng, lambda e: e.tensor_copy(out=out, in_=in_), r, w)

    def rcp(self, out, in_, r, w):
        self.P.op("dve", lambda e: e.reciprocal(out=out, in_=in_), r, w)

    def mm(self, out, lhsT, rhs, r, w, start=True, stop=True, inc=True, sgc=False):
        self.P.op("pe", lambda e: e.matmul(out, lhsT=lhsT, rhs=rhs, start=start, stop=stop, skip_group_check=sgc), r, w, inc=inc)

    def tr(self, out, in_, ident, r, w, inc=True):
        self.P.op("pe", lambda e: e.transpose(out, in_, ident), r, w, inc=inc)

    def ld(self, sem, out, in_, w, r=(), q="sp"):
        self.P.dma(q, sem, lambda e: e.dma_start(out=out, in_=in_), r, w)


def _consts():
    c = np.zeros((128, 128 * 2 + 64 * 3), np.float32)
    c[:, 0:128] = np.eye(128, dtype=np.float32)
    c[:, 128:256] = 1.0
    kk = np.arange(64)
    c[0:64, 256:320] = (kk[:, None] <= kk[None, :])
    c[0:64, 320:384] = (kk[None, :] >= kk[:, None])
    c[0:64, 384:448] = (kk[None, :] > kk[:, None])
    return c


def gdn_decl(nc, NTOK):
    def din(n, s, dt=F32):
        return nc.dram_tensor(n, list(s), dt, kind="ExternalInput").ap()
    io = {}
    io["xp"] = din("xp", [NTOK, 1024])
    io["gpre"] = din("gpre", [1, 1024])
    io["wq"] = din("wq", [1024, 1536])
    io["wba"] = din("wba", [1024, 8])
    io["convw"] = din("convw", [128, 32])
    io["alog"] = din("alog", [1, 4])
    io["dtb"] = din("dtb", [1, 4])
    io["onorm"] = din("onorm", [128, 1])
    io["cst"] = din("cst", [128, 448])
    return io


def build_gdn(NTOK):
    assert NTOK % 128 == 0
    nc = bass.Bass("TRN2", target_bir_lowering=False)
    io = gdn_decl(nc, NTOK)
    io["oT"] = nc.dram_tensor("oT", [512, NTOK], BF16, kind="ExternalOutput").ap()
    emit_gdn(nc, None, io, NTOK)
    return nc


def emit_gdn(nc, semst, io, NTOK, prew=(), fz=None):
    xp, gpre, wq, wba, convw, alog, dtb, onorm, cst = (io[n] for n in ("xp", "gpre", "wq", "wba", "convw", "alog", "dtb", "onorm", "cst"))
    oT = io.get("oT")
    with contextlib.ExitStack() as st:
        P = Prog(nc, st, semst, "g", prew)
        k = K(P)
        sb, ps = P.sb, P.ps
        if fz is not None:
            w0_b = sb("w0_b", [128, 4, 1024], BF16)
            yst = sb("yst", [128, 2, 1024], F32)
        wb = sb("wb", [128, 8, 1536], BF16)
        wbab = sb("wbab", [128, 8, 8], BF16)
        gpre_b = sb("gpre_b", [128, 1024], F32)
        convw_s = sb("convw_s", [128, 32], F32)
        alog_b = sb("alog_b", [64, 4], F32)
        dtb_b = sb("dtb_b", [64, 4], F32)
        negA = sb("negA", [64, 4], F32)
        onorm_s = sb("onorm_s", [128, 1], F32)
        cst_s = sb("cst_s", [128, 448], F32)
        ident_b = sb("ident_b", [128, 128], BF16)
        xc = sb("xc", [128, 8, 3 + 512], F32)
        S_f = sb("S_f", [128, 4, 128], F32)
        S_b = sb("S_b", [128, 4, 128], BF16)
        ident_f = cst_s[:, 0:128]
        ones_f = cst_s[:, 128:256]
        tri = cst_s[0:64, 256:320]
        mincl = cst_s[0:64, 320:384]
        mstrict = cst_s[0:64, 384:448]
        xs = sb("xs", [128, 4, 1024], F32)
        wstage = xs[:].rearrange("p (a b) d -> p a (b d)", a=2)
        junk = sb("junk", [128, 1024], BF16)
        ss = sb("ss", [128, 4], F32)
        rstd = sb("rstd", [128, 4], F32)
        hn = sb("hn", [128, 2, 1024], BF16)
        hnT = sb("hnT", [128, 8, 512], BF16)
        acc = sb("acc", [128, 512], F32)
        qkf = sb("qkf", [128, 512], F32)
        sq = sb("sq", [128, 512], F32)
        rtmp = sb("rtmp", [128, 512], F32)
        qT = sb("qT", [128, 2, 512], BF16)
        kT2 = sb("kT2", [128, 2, 2, 512], BF16)
        vT = sb("vT", [128, 4, 512], BF16)
        zs2 = sb("zs2", [128, 2, 4, 512], F32)
        bet = sb("bet", [64, 8, 4], F32)
        gt = sb("gt", [64, 8, 4], F32)
        gg = sb("gg", [64, 8, 4], F32)
        gc = sb("gc", [64, 8, 4], F32)
        kap = sb("kap", [64, 8, 4], F32)
        ngam = sb("ngam", [64, 8, 4], F32)
        bk = sb("bk", [64, 8, 4], F32)
        glast = sb("glast", [128, 8, 4], F32)
        gB = sb("gB", [64, 8, 128], F32)
        egc = sb("egc", [128, 512], F32)
        qdT = sb("qdT", [128, 4, 512], BF16)
        attnT = sb("attnT", [64, 4, 512], BF16)
        U = sb("U", [64, 4, 512], F32)
        UT = sb("UT", [64, 4, 2, 512], F32)
        Rr = sb("Rr", [64, 4, 512], F32)
        Rb = sb("Rb", [64, 4, 512], BF16)
        ktok = sb("ktok", [64, 2, 8, 128], BF16)
        vtok = sb("vtok", [64, 4, 8, 128], BF16)
        rr = sb("rr", [64, 4, 128], BF16)
        vn = sb("vn", [64, 4, 128], BF16)
        vnk = sb("vnk", [64, 4, 128], BF16)
        osq = sb("osq", [128, 512], F32)
        otmp = sb("otmp", [128, 512], F32)
        og = sb("og", [128, 4, 128], BF16)
        ps_tr = ps("ps_tr", [128, 1024], BF16)
        ps_pj = ps("ps_pj", [128, 512], F32)
        ps_ms = ps("ps_ms", [128, 512], F32)
        ps_bc = ps("ps_bc", [128, 512], F32)
        ps_sc = ps("ps_sc", [128, 4, 512], F32)
        B_TR, B_PJ, B_MS, B_BC = Tk("B_TR"), Tk("B_PJ"), Tk("B_MS"), Tk("B_BC")
        B_S = [Tk("B_S%d" % h) for h in range(4)]

        T = {}

        def tk(n):
            if n not in T:
                T[n] = Tk(n)
            return T[n]

        s_c = P.dmasem("c")
        s_w = [P.dmasem("w0"), P.dmasem("w1")]
        s_x = P.dmasem("x")
        s_o = P.dmasem("o")
        s_y = [P.dmasem("y0"), P.dmasem("y1")]
        k.ld(s_c, cst_s[:], cst[:, :], [tk("cst")])
        k.ld(s_c, gpre_b[:], gpre[0:1, :].partition_broadcast(128), [tk("gpre")])
        k.ld(s_c, convw_s[:], convw[:, :], [tk("convw")])
        k.ld(s_c, alog_b[:], alog[0:1, :].partition_broadcast(64), [tk("alog")])
        k.ld(s_c, dtb_b[:], dtb[0:1, :].partition_broadcast(64), [tk("dtb")])
        k.ld(s_c, onorm_s[:], onorm[:, :], [tk("onorm")])
        for _n in ("cst", "gpre", "convw", "alog", "dtb", "onorm"):
            tk(_n).w = (s_c, P.dcnt[s_c])
        k.cp(ident_b[:], ident_f, [tk("cst")], [tk("identb")])
        k.act(negA[:], alog_b[:], AF.Exp, [tk("alog")], [tk("negA")])
        k.ts(negA[:], negA[:], -1.0, ALU.mult, [], [tk("negA")])
        for kc in range(8):
            sl = kc % 2
            k.ld(s_w[sl], wstage[:, sl, 0:1536], wq[kc * 128:(kc + 1) * 128, :], [tk("wst%d" % sl)])
            k.ld(s_w[sl], wstage[:, sl, 1536:1544], wba[kc * 128:(kc + 1) * 128, :], [tk("wst%d" % sl)])
            k.cp(wb[:, kc, :], wstage[:, sl, 0:1536], [tk("wst%d" % sl)], [tk("wb")], eng="pool")
            k.cp(wbab[:, kc, :], wstage[:, sl, 1536:1544], [tk("wst%d" % sl)], [tk("wb")], eng="pool")
        if fz is not None:
            for h in range(4):
                sl = h % 2
                k.ld(s_w[sl], wstage[:, sl, 0:1024], fz["w0"][h * 128:(h + 1) * 128, :], [tk("wst%d" % sl)])
                k.cp(w0_b[:, h, :], wstage[:, sl, 0:1024], [tk("wst%d" % sl)], [tk("w0b")], eng="pool")
        P.op("dve", lambda e: e.memset(S_f[:], 0.0), [], [tk("S_f0"), tk("S_f1"), tk("S_f2"), tk("S_f3")])
        P.op("dve", lambda e: e.memset(S_b[:], 0.0), [], [tk("S_b0"), tk("S_b1"), tk("S_b2"), tk("S_b3")])
        P.op("dve", lambda e: e.memset(xc[:], 0.0), [], [tk("xc%d" % g) for g in range(8)])

        def gen_AB(ti):
            t0 = ti * 512
            TT = min(512, NTOK - t0)
            NS = TT // 128
            p = ti % 2
            kT = kT2[:, p]
            zs = zs2[:, p]
            kTn = "kT%d_" % p + "%d"
            k.ld(s_x, xs[:, 0:NS, :], xp[t0:t0 + TT, :].rearrange("(s p) d -> p s d", p=128),
                 [tk("xs")] + ([tk("wst0"), tk("wst1")] if ti == 0 else []))
            for s in range(NS):
                k.act(junk[:], xs[:, s, :], AF.Square, [tk("xs")], [tk("junk"), tk("ss")], accum_out=ss[:, s:s + 1])
            k.act(rstd[:, 0:NS], ss[:, 0:NS], AF.Sqrt, [tk("ss")], [tk("rstd")], scale=1.0 / 1024, bias=EPS)
            k.rcp(rstd[:, 0:NS], rstd[:, 0:NS], [], [tk("rstd")])
            yield
            for s in range(NS):
                k.stt(hn[:, s % 2, :], xs[:, s, :], rstd[:, s:s + 1], gpre_b[:], ALU.mult, ALU.mult,
                      [tk("xs"), tk("rstd"), tk("gpre")], [tk("hn%d" % (s % 2))])
                for kc in range(8):
                    k.tr(ps_tr[:, kc * 128:(kc + 1) * 128], hn[:, s % 2, kc * 128:(kc + 1) * 128], ident_b[:],
                         [tk("hn%d" % (s % 2)), tk("identb")], [B_TR], inc=(kc == 7))
                k.act(hnT[:, :, s * 128:(s + 1) * 128], ps_tr[:].rearrange("p (k t) -> p k t", t=128), AF.Identity,
                      [], [B_TR, tk("hnT")])
                yield
            for g in range(12):
                for kc in range(8):
                    k.mm(ps_pj[:, 0:TT], wb[:, kc, g * 128:(g + 1) * 128], hnT[:, kc, 0:TT], [tk("wb"), tk("hnT")], [B_PJ],
                         start=(kc == 0), stop=(kc == 7), inc=(kc == 7))
                if g < 8:
                    xg = tk("xc%d" % g)
                    k.cp(xc[:, g, 0:3], xc[:, g, 512:515], [], [xg])
                    k.act(xc[:, g, 3:3 + TT], ps_pj[:, 0:TT], AF.Identity, [], [B_PJ, xg])
                    k.ts(acc[:, 0:TT], xc[:, g, 3:3 + TT], convw_s[:, g * 4 + 3:g * 4 + 4], ALU.mult, [xg, tk("convw")], [tk("acc")])
                    for j in (2, 1, 0):
                        k.stt(acc[:, 0:TT], xc[:, g, j:j + TT], convw_s[:, g * 4 + j:g * 4 + j + 1], acc[:, 0:TT], ALU.mult, ALU.add,
                              [xg, tk("convw")], [tk("acc")])
                    if TT < 512:
                        pass
                    if g < 4:
                        dst = qT[:, g, 0:TT] if g < 2 else kT[:, g - 2, 0:TT]
                        dtk = tk("qT%d" % g) if g < 2 else tk(kTn % (g - 2))
                        k.act(qkf[:, 0:TT], acc[:, 0:TT], AF.Silu, [tk("acc")], [tk("qkf")])
                        k.act(sq[:, 0:TT], qkf[:, 0:TT], AF.Square, [tk("qkf")], [tk("sq")])
                        k.mm(ps_bc[:, 0:TT], ones_f, sq[:, 0:TT], [tk("cst"), tk("sq")], [B_BC])
                        if g < 2:
                            k.act(rtmp[:, 0:TT], ps_bc[:, 0:TT], AF.Ln, [], [B_BC, tk("rtmp")], scale=128.0, bias=128.0 * EPS)
                        else:
                            k.act(rtmp[:, 0:TT], ps_bc[:, 0:TT], AF.Ln, [], [B_BC, tk("rtmp")], scale=1.0, bias=EPS)
                        k.act(rtmp[:, 0:TT], rtmp[:, 0:TT], AF.Exp, [], [tk("rtmp")], scale=-0.5)
                        k.tt(dst, qkf[:, 0:TT], rtmp[:, 0:TT], ALU.mult, [tk("qkf"), tk("rtmp")], [dtk])
                    else:
                        k.act(vT[:, g - 4, 0:TT], acc[:, 0:TT], AF.Silu, [tk("acc")], [tk("vT%d" % (g - 4))])
                else:
                    k.act(zs[:, g - 8, 0:TT], ps_pj[:, 0:TT], AF.Silu, [], [B_PJ, tk("zs%d" % p)])
                yield

        def step(g_, n=1):
            if g_ is None:
                return
            for _ in range(n):
                try:
                    next(g_)
                except StopIteration:
                    return

        def drain(g_):
            if g_ is None:
                return
            for _ in g_:
                pass

        ntiles = (NTOK + 511) // 512
        cc_next = [0]
        drain(gen_AB(0))
        for ti in range(ntiles):
            t0 = ti * 512
            TT = min(512, NTOK - t0)
            NS = TT // 128
            NCH = TT // 64
            p = ti % 2
            kT = kT2[:, p]
            zs = zs2[:, p]
            kTn = "kT%d_" % p + "%d"
            g_next = gen_AB(ti + 1) if ti + 1 < ntiles else None
            for n in range(NCH):
                for kc in range(8):
                    k.mm(ps_ms[0:64, n * 8:(n + 1) * 8], hnT[:, kc, n * 64:(n + 1) * 64], wbab[:, kc, :], [tk("hnT"), tk("wb")], [B_MS],
                         start=(kc == 0), stop=(kc == 7), inc=(kc == 7 and n == NCH - 1))
            bav = ps_ms[0:64, 0:NCH * 8].rearrange("p (n c) -> p n c", c=8)
            k.act(bet[:, 0:NCH, :], bav[:, :, 0:4], AF.Sigmoid, [], [B_MS, tk("bet")])
            k.tt(gt[:, 0:NCH, :], bav[:, :, 4:8], dtb_b[:].unsqueeze(1).to_broadcast([64, NCH, 4]), ALU.add, [tk("dtb")], [B_MS, tk("gt")])
            k.act(gt[:, 0:NCH, :], gt[:, 0:NCH, :], AF.Exp, [], [tk("gt")])
            k.act(gt[:, 0:NCH, :], gt[:, 0:NCH, :], AF.Ln, [], [tk("gt")], bias=1.0)
            k.tt(gg[:, 0:NCH, :], gt[:, 0:NCH, :], negA[:].unsqueeze(1).to_broadcast([64, NCH, 4]), ALU.mult, [tk("gt"), tk("negA")], [tk("gg")])
            ggf = gg[:, 0:NCH, :].rearrange("p n h -> p (n h)")
            k.mm(ps_ms[0:64, 64:64 + NCH * 4], tri, ggf, [tk("cst"), tk("gg")], [B_MS])
            k.mm(ps_ms[:, 128:128 + NCH * 4], ones_f[0:64, :], ggf, [tk("cst"), tk("gg")], [B_MS])
            gcv = ps_ms[0:64, 64:64 + NCH * 4].rearrange("p (n h) -> p n h", h=4)
            glv = ps_ms[:, 128:128 + NCH * 4].rearrange("p (n h) -> p n h", h=4)
            k.cp(gc[:, 0:NCH, :], gcv, [], [B_MS, tk("gc")])
            k.act(glast[:, 0:NCH, :], glv, AF.Exp, [], [B_MS, tk("glast")])
            k.tt(kap[:, 0:NCH, :], glv[0:64], gc[:, 0:NCH, :], ALU.subtract, [tk("gc")], [B_MS, tk("kap")])
            k.act(kap[:, 0:NCH, :], kap[:, 0:NCH, :], AF.Exp, [], [tk("kap")])
            k.act(ngam[:, 0:NCH, :], gc[:, 0:NCH, :], AF.Exp, [tk("gc")], [tk("ngam")])
            k.ts(ngam[:, 0:NCH, :], ngam[:, 0:NCH, :], -1.0, ALU.mult, [], [tk("ngam")])
            k.tt(bk[:, 0:NCH, :], bet[:, 0:NCH, :], kap[:, 0:NCH, :], ALU.mult, [tk("bet"), tk("kap")], [tk("bk")])
            for j in range(6):
                src = kT[:, j, :] if j < 2 else vT[:, j - 2, :]
                stk = tk(kTn % j) if j < 2 else tk("vT%d" % (j - 2))
                for n in range(NCH):
                    k.tr(ps_tr[0:64, n * 128:(n + 1) * 128], src[:, n * 64:(n + 1) * 64], ident_b[:], [stk, tk("identb")], [B_TR],
                         inc=(n == NCH - 1))
                dst = ktok[:, j, 0:NCH, :] if j < 2 else vtok[:, j - 2, 0:NCH, :]
                dtk = tk("ktok%d" % j) if j < 2 else tk("vtok%d" % (j - 2))
                k.act(dst, ps_tr[0:64, 0:NCH * 128].rearrange("p (n d) -> p n d", d=128), AF.Identity, [], [B_TR, dtk])
            W = NCH * 64
            v3 = lambda ap_: ap_.rearrange("p (n c) -> p n c", c=64)
            for h in range(4):
                k.cp(gB[:, 0:NCH, :], gg[:, 0:NCH, h:h + 1].to_broadcast([64, NCH, 128]), [tk("gg")], [tk("gB")])
                for n in range(NCH):
                    k.mm(ps_sc[:, h, n * 64:(n + 1) * 64], gB[:, n, :], tri, [tk("gB"), tk("cst")], [B_S[h]], inc=(n == NCH - 1))
            for h in range(4):
                qh = h // 2
                t1h = UT[:, h, 1, 0:W]
                k.act(egc[:, 0:W], ps_sc[:, h, 0:W], AF.Exp, [], [B_S[h], tk("egc")])
                k.tt(qdT[:, h, 0:W], qT[:, qh, 0:W], egc[:, 0:W], ALU.mult, [tk("qT%d" % qh), tk("egc")], [tk("qdT%d" % h)])
                k.tt(v3(t1h), v3(ps_sc[0:64, h, 0:W]), gc[:, 0:NCH, h:h + 1].to_broadcast([64, NCH, 64]),
                     ALU.subtract, [tk("gc")], [B_S[h], tk("UT%d_1" % h)])
            for h in range(4):
                t1h, Dmh, Bsh = UT[:, h, 1, 0:W], Rr[:, h, 0:W], UT[:, h, 0, 0:W]
                k.ts(t1h, t1h, 0.0, ALU.min, [], [tk("UT%d_1" % h)])
                k.act(t1h, t1h, AF.Exp, [], [tk("UT%d_1" % h)])
                k.tt(v3(Dmh), v3(t1h), mincl.unsqueeze(1).to_broadcast([64, NCH, 64]), ALU.mult, [tk("UT%d_1" % h), tk("cst")], [tk("R%d" % h)])
                k.tt(v3(Bsh), mstrict.unsqueeze(1).to_broadcast([64, NCH, 64]), bet[:, 0:NCH, h:h + 1].to_broadcast([64, NCH, 64]), ALU.mult,
                     [tk("cst"), tk("bet")], [tk("UT%d_0" % h)])
            for h in range(4):
                qh = h // 2
                for n in range(NCH):
                    k.mm(ps_sc[0:64, h, n * 64:(n + 1) * 64], kT[:, qh, n * 64:(n + 1) * 64], qT[:, qh, n * 64:(n + 1) * 64],
                         [tk(kTn % qh), tk("qT%d" % qh)], [B_S[h]], inc=(n == NCH - 1))
                k.tt(attnT[:, h, 0:W], ps_sc[0:64, h, 0:W], Rr[:, h, 0:W], ALU.mult, [tk("R%d" % h)], [B_S[h], tk("attnT%d" % h)])
            for h in range(4):
                qh = h // 2
                for n in range(NCH):
                    k.mm(ps_sc[0:64, h, n * 64:(n + 1) * 64], kT[:, qh, n * 64:(n + 1) * 64], kT[:, qh, n * 64:(n + 1) * 64],
                         [tk(kTn % qh)], [B_S[h]], inc=(n == NCH - 1))
                k.tt(U[:, h, 0:W], ps_sc[0:64, h, 0:W], Rr[:, h, 0:W], ALU.mult, [tk("R%d" % h)], [B_S[h], tk("U%d" % h)])
                k.tt(U[:, h, 0:W], U[:, h, 0:W], UT[:, h, 0, 0:W], ALU.mult, [tk("UT%d_0" % h)], [tk("U%d" % h)])
            for h in range(4):
                for n in range(NCH):
                    k.tr(ps_sc[0:64, h, n * 64:(n + 1) * 64], U[:, h, n * 64:(n + 1) * 64], ident_f[0:64, 0:64], [tk("U%d" % h), tk("cst")], [B_S[h]],
                         inc=(n == NCH - 1))
                k.cp(UT[:, h, 0, 0:W], ps_sc[0:64, h, 0:W], [], [B_S[h], tk("UT%d_0" % h)])
            for h in range(4):
                k.stt(v3(Rr[:, h, 0:W]), v3(U[:, h, 0:W]), -1.0,
                      ident_f[0:64, 0:64].unsqueeze(1).to_broadcast([64, NCH, 64]), ALU.mult, ALU.add, [tk("U%d" % h), tk("cst")], [tk("R%d" % h)])
            W = NCH * 64
            cur = 0
            for lvl in range(1, 6):
                nxt = 1 - cur
                last = (lvl == 5)
                for h in range(4):
                    for n in range(NCH):
                        c = slice(n * 64, (n + 1) * 64)
                        k.mm(ps_sc[0:64, h, c], U[:, h, c], UT[:, h, cur, c], [tk("U%d" % h), tk("UT%d_%d" % (h, cur))], [B_S[h]], inc=(n == NCH - 1))
                    k.cp(UT[:, h, nxt, 0:W], ps_sc[0:64, h, 0:W], [], [B_S[h], tk("UT%d_%d" % (h, nxt))])
                step(g_next)
                if not last:
                    for h in range(4):
                        for n in range(NCH):
                            c = slice(n * 64, (n + 1) * 64)
                            k.mm(ps_sc[0:64, h, c], UT[:, h, cur, c], U[:, h, c], [tk("U%d" % h), tk("UT%d_%d" % (h, cur))], [B_S[h]], inc=(n == NCH - 1))
                        k.act(U[:, h, 0:W], ps_sc[0:64, h, 0:W], AF.Identity, [], [B_S[h], tk("U%d" % h)])
                    step(g_next)
                for h in range(4):
                    for n in range(NCH):
                        c = slice(n * 64, (n + 1) * 64)
                        k.mm(ps_sc[0:64, h, c], UT[:, h, nxt, c], Rr[:, h, c], [tk("UT%d_%d" % (h, nxt)), tk("R%d" % h)], [B_S[h]], inc=(n == NCH - 1))
                    if not last:
                        k.tt(Rr[:, h, 0:W], ps_sc[0:64, h, 0:W], Rr[:, h, 0:W], ALU.add, [], [B_S[h], tk("R%d" % h)])
                    else:
                        k.tt(Rb[:, h, 0:W], ps_sc[0:64, h, 0:W], Rr[:, h, 0:W], ALU.add, [tk("R%d" % h)], [B_S[h], tk("Rb%d" % h)])
                cur = nxt
                step(g_next)
            drain(g_next)
            for n in range(NCH):
                c = slice(n * 64, (n + 1) * 64)
                par = n % 2
                for h in range(4):
                    qh = h // 2
                    k.mm(ps_sc[0:64, h, 0:128], kT[:, qh, c], S_b[:, h, :], [tk(kTn % qh), tk("S_b%d" % h)], [B_S[h]])
                for h in range(4):
                    k.stt(rr[:, h, :], ps_sc[0:64, h, 0:128], ngam[:, n, h:h + 1], vtok[:, h, n, :], ALU.mult, ALU.add,
                          [tk("ngam"), tk("vtok%d" % h)], [B_S[h], tk("rr%d" % h)])
                for h in range(4):
                    k.mm(ps_sc[0:64, h, 128:256], Rb[:, h, c], rr[:, h, :], [tk("Rb%d" % h), tk("rr%d" % h)], [B_S[h]])
                for h in range(4):
                    k.act(vn[:, h, :], ps_sc[0:64, h, 128:256], AF.Identity, [tk("bet")], [B_S[h], tk("vn%d" % h)], scale=bet[:, n, h:h + 1])
                    k.ts(vnk[:, h, :], ps_sc[0:64, h, 128:256], bk[:, n, h:h + 1], ALU.mult, [tk("bk")], [B_S[h], tk("vnk%d" % h)])
                for h in range(4):
                    qh = h // 2
                    oc = slice(384 + par * 64, 384 + par * 64 + 64)
                    k.mm(ps_sc[:, h, oc], S_b[:, h, :], qdT[:, h, c], [tk("S_b%d" % h), tk("qdT%d" % h)], [B_S[h]], start=True, stop=False, inc=False)
                    k.mm(ps_sc[:, h, oc], vn[:, h, :], attnT[:, h, c], [tk("vn%d" % h), tk("attnT%d" % h)], [B_S[h]], start=False, stop=True, inc=False)
                    k.mm(ps_sc[:, h, 256:384], ktok[:, qh, n, :], vnk[:, h, :], [tk("ktok%d" % qh), tk("vnk%d" % h)], [B_S[h]])
                for h in range(4):
                    k.stt(S_f[:, h, :], S_f[:, h, :], glast[:, n, h:h + 1], ps_sc[:, h, 256:384], ALU.mult, ALU.add,
                          [tk("glast")], [B_S[h], tk("S_f%d" % h)])
                    k.act(S_b[:, h, :], S_f[:, h, :], AF.Identity, [tk("S_f%d" % h)], [tk("S_b%d" % h)])
                if par == 1:
                    ov = ps_sc[:, :, 384:512]
                    tc0 = (n - 1) * 64
                    k.act(osq[:].rearrange("p (h t) -> p h t", t=128), ov, AF.Square, [], B_S + [tk("osq")])
                    k.mm(ps_bc[:, :], ones_f, osq[:], [tk("cst"), tk("osq")], [B_BC])
                    k.act(otmp[:], ps_bc[:], AF.Ln, [], [B_BC, tk("otmp")], scale=1.0 / 128, bias=EPS)
                    k.act(otmp[:], otmp[:], AF.Exp, [], [tk("otmp")], scale=-0.5)
                    k.tt(otmp[:].rearrange("p (h t) -> p h t", t=128), ov, otmp[:].rearrange("p (h t) -> p h t", t=128), ALU.mult,
                         [], B_S + [tk("otmp")])
                    k.stt(og[:], otmp[:].rearrange("p (h t) -> p h t", t=128), onorm_s[:, 0:1], zs[:, :, tc0:tc0 + 128], ALU.mult, ALU.mult,
                          [tk("otmp"), tk("onorm"), tk("zs%d" % p)], [tk("og")])
                    if fz is None:
                        k.ld(s_o, oT[:, t0 + tc0:t0 + tc0 + 128].rearrange("(h e) t -> e h t", e=128), og[:], [], r=[tk("og")])
                    else:
                        ysl = (t0 + tc0) // 128 % 2
                        for half, (pst, btk) in enumerate(((ps_pj, B_PJ), (ps_ms, B_MS))):
                            for h in range(4):
                                k.mm(pst[:, :], og[:, h, :], w0_b[:, h, half * 512:(half + 1) * 512], [tk("og"), tk("w0b")], [btk],
                                     start=(h == 0), stop=(h == 3), inc=(h == 3))
                        k.act(yst[:, ysl, 0:512], ps_pj[:, :], AF.Identity, [], [B_PJ, tk("yst%d" % ysl)])
                        k.cp(yst[:, ysl, 512:1024], ps_ms[:, :], [], [B_MS, tk("yst%d" % ysl)])
                        pos0 = t0 + tc0 - 48
                        nrows = fz["y0p"].shape[0]
                        if pos0 < 0:
                            k.ld(s_y[ysl], fz["y0p"][0:128 + pos0, :], yst[-pos0:128, ysl, :], [], r=[tk("yst%d" % ysl), tk("y0rows")])
                            rend = 128 + pos0
                        else:
                            nr = min(128, nrows - pos0)
                            rend = pos0 + max(nr, 0)
                            if nr > 0:
                                k.ld(s_y[ysl], fz["y0p"][pos0:pos0 + nr, :], yst[0:nr, ysl, :], [], r=[tk("yst%d" % ysl), tk("y0rows")])
                        while cc_next[0] < nrows and rend >= min(nrows, cc_next[0] + CC_ROWS):
                            r0, r1 = cc_next[0], min(nrows, cc_next[0] + CC_ROWS)
                            P.cc("cc", fz["scc"], fz["y0p"][r0:r1, :], fz["y0f"][r0:r1, :], [], [tk("y0rows")])
                            cc_next[0] = r1
        P.finish("sp", [tk("og")] + ([tk("yst0"), tk("yst1")] if fz is not None else []))
        P.emit()
        return P.final_events()


def gdn_inputs(inp, core, NTOK):
    b, hg = core // 4, core % 4
    L = 16 + inp["x"].shape[1]
    xp = np.zeros((NTOK, 1024), np.float32)
    n_real = min(L, NTOK - 48)
    xp[48:64] = inp["meta_tokens"]
    xp[64:48 + n_real] = inp["x"][b, :n_real - 16]
    W = inp["gdn_w_in"][0]
    qcols = np.arange(2 * hg * 128, (2 * hg + 2) * 128)
    kcols = 1024 + qcols
    vcols = 2048 + np.arange(4 * hg * 128, (4 * hg + 4) * 128)
    zcols = 4096 + np.arange(4 * hg * 128, (4 * hg + 4) * 128)
    bcols = 6144 + np.arange(4 * hg, 4 * hg + 4)
    acols = 6160 + np.arange(4 * hg, 4 * hg + 4)
    wq = np.ascontiguousarray(W[:, np.concatenate([qcols, kcols, vcols, zcols])])
    wba = np.ascontiguousarray(W[:, np.concatenate([bcols, acols])])
    cw = inp["gdn_conv_w"][0][:, np.concatenate([qcols, kcols, vcols])]
    convw = np.ascontiguousarray(cw.reshape(4, 8, 128).transpose(2, 1, 0).reshape(128, 32))
    return {
        "xp": xp, "gpre": inp["pre_norm"][0:1].copy(), "wq": wq, "wba": wba, "convw": convw,
        "alog": inp["gdn_a_log"][0:1, 4 * hg:4 * hg + 4].copy(), "dtb": inp["gdn_dt_bias"][0:1, 4 * hg:4 * hg + 4].copy(),
        "onorm": inp["gdn_out_norm"][0].reshape(128, 1).copy(), "cst": _consts(),
    }


def build_wout(NBLK):
    nc = bass.Bass("TRN2", target_bir_lowering=False)

    def din(n, s, dt=F32):
        return nc.dram_tensor(n, list(s), dt, kind="ExternalInput").ap()

    NT = NBLK * 128
    oTin = din("oTin", [2048, NT], BF16)
    resid = din("resid", [NT, 1024])
    w = din("w", [2048, 1024])
    gpost = din("gpost", [1, 1024])
    out = nc.dram_tensor("out", [NT, 1024], F32, kind="ExternalOutput").ap()
    with contextlib.ExitStack() as st:
        P = Prog(nc, st)
        k = K(P)
        sb, ps = P.sb, P.ps
        wsb = sb("wsb", [128, 16, 1024], BF16)
        wstage = sb("wstage", [128, 2, 1024], F32)
        gp_b = sb("gp_b", [128, 1024], F32)
        oTs = sb("oTs", [128, 2, 16, 128], BF16)
        rs = sb("rs", [128, 2, 1024], F32)
        junk = sb("junk", [128, 512], BF16)
        ss = sb("ss", [128, 2], F32)
        rstd = sb("rstd", [128, 1], F32)
        ot = sb("ot", [128, 2, 1024], F32)
        ps_y = ps("ps_y", [128, 2, 512], F32)
        B_Y = [Tk("B_Y0"), Tk("B_Y1")]
        T = {}

        def tk(n):
            if n not in T:
                T[n] = Tk(n)
            return T[n]
        s_c = P.dmasem("c")
        s_w = [P.dmasem("w0"), P.dmasem("w1")]
        s_i = [P.dmasem("i0"), P.dmasem("i1")]
        s_o = [P.dmasem("o0"), P.dmasem("o1")]
        k.ld(s_c, gp_b[:], gpost[0:1, :].partition_broadcast(128), [tk("gp")])
        for kc in range(16):
            sl = kc % 2
            k.ld(s_w[sl], wstage[:, sl, :], w[kc * 128:(kc + 1) * 128, :], [tk("wst%d" % sl)])
            k.cp(wsb[:, kc, :], wstage[:, sl, :], [tk("wst%d" % sl)], [tk("wsb")], eng="pool")
        for blk in range(NBLK):
            sl = blk % 2
            c0 = blk * 128
            k.ld(s_i[sl], oTs[:, sl, :, :], oTin[:, c0:c0 + 128].rearrange("(k p) t -> p k t", p=128), [tk("in%d" % sl)])
            k.ld(s_i[sl], rs[:, sl, :], resid[c0:c0 + 128, :], [tk("in%d" % sl)])
            for half in range(2):
                for kc in range(16):
                    k.mm(ps_y[:, half, :], oTs[:, sl, kc, :], wsb[:, kc, half * 512:(half + 1) * 512], [tk("in%d" % sl), tk("wsb")], [B_Y[half]],
                         start=(kc == 0), stop=(kc == 15), inc=(kc == 15))
                k.act(junk[:], ps_y[:, half, :], AF.Square, [], [B_Y[half], tk("junk"), tk("ss")], accum_out=ss[:, half:half + 1])
            k.tt(rstd[:], ss[:, 0:1], ss[:, 1:2], ALU.add, [tk("ss")], [tk("rstd")])
            k.act(rstd[:], rstd[:], AF.Sqrt, [], [tk("rstd")], scale=1.0 / 1024, bias=EPS)
            k.rcp(rstd[:], rstd[:], [], [tk("rstd")])
            for half in range(2):
                hs = slice(half * 512, (half + 1) * 512)
                k.stt(ot[:, sl, hs], ps_y[:, half, :], rstd[:, 0:1], gp_b[:, hs], ALU.mult, ALU.mult, [tk("rstd"), tk("gp")], [B_Y[half], tk("ot%d" % sl)])
            k.tt(ot[:, sl, :], ot[:, sl, :], rs[:, sl, :], ALU.add, [tk("in%d" % sl)], [tk("ot%d" % sl)])
            k.ld(s_o[sl], out[c0:c0 + 128, :], ot[:, sl, :], [], r=[tk("ot%d" % sl)])
        P.finish("sp", [tk("ot0"), tk("ot1")])
        P.emit()
    return nc


SCALE = 192.0 ** -0.5


def _consts_mla():
    c = np.zeros((128, 384), np.float32)
    c[:, 0:128] = np.eye(128, dtype=np.float32)
    c[:, 128:256] = 1.0
    kk = np.arange(128)
    c[:, 256:384] = (kk[None, :] >= kk[:, None])
    return c


def mla_decl(nc, NTOK2):
    def din(n, s, dt=F32):
        return nc.dram_tensor(n, list(s), dt, kind="ExternalInput").ap()
    io = {}
    io["g1"] = din("g1", [1, 1024])
    io["gkv"] = din("gkv", [1, 1024])
    io["glat"] = din("glat", [128, 1])
    io["gq"] = din("gq", [1, 256])
    io["wkvd"] = din("wkvd", [1024, 256])
    io["wuk"] = din("wuk", [128, 512])
    io["wuv"] = din("wuv", [128, 512])
    io["wmi"] = din("wmi", [1024, 768])
    io["wqu"] = din("wqu", [256, 1024])
    io["cos2T"] = din("cos2T", [64, NTOK2])
    io["sinsT"] = din("sinsT", [64, NTOK2])
    io["cstm"] = din("cstm", [128, 384])
    return io


def build_mla(NTOK2):
    assert NTOK2 % 128 == 0
    nc = bass.Bass("TRN2", target_bir_lowering=False)
    io = mla_decl(nc, NTOK2)
    io["h1p"] = nc.dram_tensor("h1p", [NTOK2, 1024], F32, kind="ExternalInput").ap()
    io["o1T"] = nc.dram_tensor("o1T", [512, NTOK2], BF16, kind="ExternalOutput").ap()
    emit_mla(nc, None, io, NTOK2)
    return nc


def emit_mla(nc, semst, io, NTOK2, prew=(), fz=None):
    NBK = NTOK2 // 128
    g1, gkv, glat, gq, wkvd, wuk, wuv, wmi, wqu, cos2T, sinsT, cst = (
        io[n] for n in ("g1", "gkv", "glat", "gq", "wkvd", "wuk", "wuv", "wmi", "wqu", "cos2T", "sinsT", "cstm"))
    h1p = io.get("h1p")
    o1T = io.get("o1T")
    with contextlib.ExitStack() as st:
        P = Prog(nc, st, semst, "m", prew)
        k = K(P)
        sb, ps = P.sb, P.ps
        if fz is not None:
            gp0_b = sb("gp0_b", [128, 1024], F32)
            w1_b = sb("w1_b", [128, 4, 1024], BF16)
            ys = sb("ys", [128, 2, 1024], F32)
            ssy = sb("ssy", [128, 1], F32)
            y1st = sb("y1st", [128, 1024], F32)
        ckvT = sb("ckvT", [128, NTOK2], BF16)
        kropeT = sb("kropeT", [128, NTOK2], BF16)
        ckvtok = sb("ckvtok", [128, NBK, 129], BF16)
        wkvd_b = sb("wkvd_b", [128, 8, 256], BF16)
        wuk_b = sb("wuk_b", [128, 4, 128], BF16)
        wukT_b = sb("wukT_b", [128, 4, 128], BF16)
        wuv_b = sb("wuv_b", [128, 4, 128], BF16)
        wmi_b = sb("wmi_b", [128, 8, 768], BF16)
        wqu_b = sb("wqu_b", [128, 2, 1024], BF16)
        wstage = sb("wstage", [128, 2, 1024], F32)
        g1_b = sb("g1_b", [128, 1024], F32)
        gkv_b = sb("gkv_b", [128, 1024], F32)
        gq_b = sb("gq_b", [128, 256], F32)
        glat_s = sb("glat_s", [128, 1], F32)
        cst_s = sb("cst_s", [128, 384], F32)
        ident_b = sb("ident_b", [128, 128], BF16)
        tri_b = sb("tri_b", [128, 128], BF16)
        zb = sb("zb", [128, 512], BF16)
        ones_f = cst_s[:, 128:256]
        hs = sb("hs", [128, 4, 1024], F32)
        junk = sb("junk", [128, 1024], BF16)
        ss = sb("ss", [128, 4], F32)
        rstd = sb("rstd", [128, 4], F32)
        hn1 = sb("hn1", [128, 1024], BF16)
        hkv = sb("hkv", [128, 1024], BF16)
        hn1T = sb("hn1T", [128, 8, 512], BF16)
        hkvT = sb("hkvT", [128, 8, 512], BF16)
        cs = sb("cs", [64, 512], F32)
        sn = sb("sn", [64, 512], F32)
        ckf = sb("ckf", [128, 512], F32)
        sq = sb("sq", [128, 512], F32)
        rt = sb("rt", [128, 512], F32)
        ra = sb("ra", [64, 2, 512], F32)
        rbb = sb("rbb", [64, 2, 512], F32)
        ssq = sb("ssq", [128, 1], F32)
        cqn = sb("cqn", [128, 256], BF16)
        cqT = sb("cqT", [128, 2, 512], BF16)
        zs1 = sb("zs1", [128, 4, 512], F32)
        qnT = sb("qnT", [128, 2, 512], BF16)
        qpT = sb("qpT", [128, 4, 512], BF16)
        qrT = sb("qrT", [128, 4, 512], BF16)
        pT = sb("pT", [128, 3, 512], BF16)
        pacc = sb("pacc", [128, 2, 512], F32)
        rdb = sb("rdb", [128, 512], F32)
        ocn = sb("ocn", [128, 512], BF16)
        og1 = sb("og1", [128, 4, 512], BF16)
        ps_tr = ps("ps_tr", [128, 1024], BF16)
        ps_pj = ps("ps_pj", [128, 512], F32)
        ps_p2 = ps("ps_p2", [128, 512], F32)
        ps_v = ps("ps_v", [128, 512], F32)
        ps_s = ps("ps_s", [128, 3, 512], F32)
        ps_o = ps("ps_o", [128, 512], F32)
        B_TR, B_PJ, B_P2, B_V = Tk("B_TR"), Tk("B_PJ"), Tk("B_P2"), Tk("B_V")
        B_S = [Tk("B_S0"), Tk("B_S1"), Tk("B_S2")]
        B_O = Tk("B_O")
        T = {}

        def tk(n):
            if n not in T:
                T[n] = Tk(n)
            return T[n]

        s_c = P.dmasem("c")
        s_w = [P.dmasem("w0"), P.dmasem("w1")]
        s_x = P.dmasem("x")
        s_o = P.dmasem("o")
        k.ld(s_c, cst_s[:], cst[:, :], [tk("cst")])
        k.ld(s_c, g1_b[:], g1[0:1, :].partition_broadcast(128), [tk("g1")])
        k.ld(s_c, gkv_b[:], gkv[0:1, :].partition_broadcast(128), [tk("gkv")])
        k.ld(s_c, gq_b[:], gq[0:1, :].partition_broadcast(128), [tk("gq")])
        k.ld(s_c, glat_s[:], glat[:, :], [tk("glat")])
        if fz is not None:
            s_yl = [P.dmasem("yl0"), P.dmasem("yl1")]
            s_h = P.dmasem("h")
            k.ld(s_c, gp0_b[:], fz["gp0"][0:1, :].partition_broadcast(128), [tk("gp0")])
            tk("gp0").w = None
        for _n in ("cst", "g1", "gkv", "gq", "glat", "gp0"):
            tk(_n).w = (s_c, P.dcnt[s_c])
        k.cp(ident_b[:], cst_s[:, 0:128], [tk("cst")], [tk("identb")])
        k.cp(tri_b[:], cst_s[:, 256:384], [tk("cst")], [tk("trib")])
        P.op("dve", lambda e: e.memset(zb[:], 0.0), [], [tk("zb")])
        P.op("pool", lambda e: e.memset(ckvtok[:], 1.0), [], [tk("ckvtok")])
        P.op("pool", lambda e: e.memset(kropeT[:], 0.0), [], [tk("kropeT")])
        P.op("pool", lambda e: e.memset(qrT[:], 0.0), [], [tk("qrT%d" % h_) for h_ in range(4)])
        wl = []
        for kc in range(8):
            wl.append((wkvd[kc * 128:(kc + 1) * 128, :], 256, wkvd_b[:, kc, :]))
        wl.append((wuk[:, :], 512, wuk_b[:].rearrange("p h d -> p (h d)")))
        wl.append((wuv[:, :], 512, wuv_b[:].rearrange("p h d -> p (h d)")))
        for kc in range(8):
            wl.append((wmi[kc * 128:(kc + 1) * 128, :], 768, wmi_b[:, kc, :]))
        for c2 in range(2):
            wl.append((wqu[c2 * 128:(c2 + 1) * 128, :], 1024, wqu_b[:, c2, :]))
        if fz is not None:
            for h in range(4):
                wl.append((fz["w1"][h * 128:(h + 1) * 128, :], 1024, w1_b[:, h, :]))
        for i, (src, n, dst) in enumerate(wl):
            sl = i % 2
            k.ld(s_w[sl], wstage[:, sl, 0:n], src, [tk("wst%d" % sl)])
            k.cp(dst, wstage[:, sl, 0:n], [tk("wst%d" % sl)], [tk("wts")], eng="pool")
        for h in range(4):
            k.tr(ps_tr[:, h * 128:(h + 1) * 128], wuk_b[:, h, :], ident_b[:], [tk("wts"), tk("identb")], [B_TR], inc=(h == 3))
        k.act(wukT_b[:].rearrange("p h d -> p (h d)"), ps_tr[:, 0:512], AF.Identity, [], [B_TR, tk("wukT")])

        ntiles = (NTOK2 + 511) // 512
        cc_next = [0]
        for ti in range(ntiles):
            t0 = ti * 512
            TT = min(512, NTOK2 - t0)
            NS = TT // 128
            blk0 = t0 // 128
            hsrc = h1p[t0:t0 + TT, :] if fz is None else fz["xp"][48 + t0:48 + t0 + TT, :]
            k.ld(s_x, hs[:, 0:NS, :], hsrc.rearrange("(s p) d -> p s d", p=128), [tk("hs")])
            k.ld(s_x, cs[:, 0:TT], cos2T[:, t0:t0 + TT], [tk("cs")])
            k.ld(s_x, sn[:, 0:TT], sinsT[:, t0:t0 + TT], [tk("cs")])
            tk("hs").w = (s_x, P.dcnt[s_x])
            tk("cs").w = (s_x, P.dcnt[s_x])
            if fz is not None:
                for s in range(NS):
                    ysl = s % 2
                    ytk = tk("ys%d" % ysl)
                    k.ld(s_yl[ysl], ys[:, ysl, :], fz["y0f"][t0 + s * 128:t0 + (s + 1) * 128, :], [ytk])
                    k.act(junk[:], ys[:, ysl, :], AF.Square, [ytk], [tk("junk"), tk("ssy")], accum_out=ssy[:])
                    k.act(ssy[:], ssy[:], AF.Sqrt, [], [tk("ssy")], scale=1.0 / 1024, bias=EPS)
                    k.rcp(ssy[:], ssy[:], [], [tk("ssy")])
                    k.stt(ys[:, ysl, :], ys[:, ysl, :], ssy[:, 0:1], gp0_b[:], ALU.mult, ALU.mult, [tk("ssy"), tk("gp0")], [ytk])
                    k.tt(hs[:, s, :], hs[:, s, :], ys[:, ysl, :], ALU.add, [ytk], [tk("hs")])
                k.ld(s_h, fz["h1s"][t0:t0 + TT, :].rearrange("(s p) d -> p s d", p=128), hs[:, 0:NS, :], [], r=[tk("hs")])
            for s in range(NS):
                k.act(junk[:], hs[:, s, :], AF.Square, [tk("hs")], [tk("junk"), tk("ss")], accum_out=ss[:, s:s + 1])
            k.act(rstd[:, 0:NS], ss[:, 0:NS], AF.Sqrt, [tk("ss")], [tk("rstd")], scale=1.0 / 1024, bias=EPS)
            k.rcp(rstd[:, 0:NS], rstd[:, 0:NS], [], [tk("rstd")])
            for s in range(NS):
                for (gb, gt_, dstT, nm) in ((g1_b, "g1", hn1T, "hn1"), (gkv_b, "gkv", hkvT, "hkv")):
                    buf = hn1 if nm == "hn1" else hkv
                    k.stt(buf[:], hs[:, s, :], rstd[:, s:s + 1], gb[:], ALU.mult, ALU.mult, [tk("hs"), tk("rstd"), tk(gt_)], [tk(nm)])
                    for kc in range(8):
                        k.tr(ps_tr[:, kc * 128:(kc + 1) * 128], buf[:, kc * 128:(kc + 1) * 128], ident_b[:], [tk(nm), tk("identb")], [B_TR],
                             inc=(kc == 7))
                    k.act(dstT[:, :, s * 128:(s + 1) * 128], ps_tr[:].rearrange("p (k t) -> p k t", t=128), AF.Identity, [], [B_TR, tk(nm + "T")])
            tsl = slice(t0, t0 + TT)
            for kc in range(8):
                k.mm(ps_pj[:, 0:TT], wkvd_b[:, kc, 0:128], hkvT[:, kc, 0:TT], [tk("wts"), tk("hkvT")], [B_PJ], start=(kc == 0), stop=(kc == 7), inc=(kc == 7))
            k.act(ckf[:, 0:TT], ps_pj[:, 0:TT], AF.Identity, [], [B_PJ, tk("ckf")])
            k.act(sq[:, 0:TT], ckf[:, 0:TT], AF.Square, [tk("ckf")], [tk("sq")])
            k.mm(ps_p2[:, 0:TT], ones_f, sq[:, 0:TT], [tk("cst"), tk("sq")], [B_P2])
            k.act(rt[:, 0:TT], ps_p2[:, 0:TT], AF.Ln, [], [B_P2, tk("rt")], scale=1.0 / 128, bias=EPS)
            k.act(rt[:, 0:TT], rt[:, 0:TT], AF.Exp, [], [tk("rt")], scale=-0.5)
            k.stt(ckvT[:, tsl], ckf[:, 0:TT], glat_s[:, 0:1], rt[:, 0:TT], ALU.mult, ALU.mult, [tk("ckf"), tk("glat"), tk("rt")], [tk("ckvT")])
            for kc in range(8):
                k.mm(ps_pj[0:64, 0:TT], wkvd_b[:, kc, 128:192], hkvT[:, kc, 0:TT], [tk("wts"), tk("hkvT")], [B_PJ], start=(kc == 0), stop=(kc == 7), inc=(kc == 7))
            for kc in range(8):
                k.mm(ps_p2[0:64, 0:TT], wkvd_b[:, kc, 192:256], hkvT[:, kc, 0:TT], [tk("wts"), tk("hkvT")], [B_P2], start=(kc == 0), stop=(kc == 7), inc=(kc == 7))
            k.tt(ra[:, 0, 0:TT], ps_pj[0:64, 0:TT], cs[:, 0:TT], ALU.mult, [tk("cs")], [B_PJ, tk("ra0")])
            k.tt(rbb[:, 0, 0:TT], ps_p2[0:64, 0:TT], sn[:, 0:TT], ALU.mult, [tk("cs")], [B_P2, tk("rbb0")])
            k.tt(kropeT[0:64, tsl], ra[:, 0, 0:TT], rbb[:, 0, 0:TT], ALU.add, [tk("ra0"), tk("rbb0")], [tk("kropeT")])
            for s in range(NS):
                k.tr(ps_tr[:, s * 128:(s + 1) * 128], ckvT[:, t0 + s * 128:t0 + (s + 1) * 128], ident_b[:], [tk("ckvT"), tk("identb")], [B_TR], inc=(s == NS - 1))
            k.act(ckvtok[:, blk0:blk0 + NS, 0:128], ps_tr[:, 0:NS * 128].rearrange("p (s d) -> p s d", d=128), AF.Identity, [], [B_TR, tk("ckvtok")])
            for s in range(NS):
                for kc in range(8):
                    k.mm(ps_v[:, 0:256], hn1T[:, kc, s * 128:(s + 1) * 128], wmi_b[:, kc, 0:256], [tk("hn1T"), tk("wts")], [B_V], start=(kc == 0), stop=(kc == 7), inc=(kc == 7))
                k.act(junk[:, 0:256], ps_v[:, 0:256], AF.Square, [], [B_V, tk("junk"), tk("ssq")], accum_out=ssq[:])
                k.act(ssq[:], ssq[:], AF.Sqrt, [], [tk("ssq")], scale=1.0 / 256, bias=EPS)
                k.rcp(ssq[:], ssq[:], [], [tk("ssq")])
                k.stt(cqn[:], ps_v[:, 0:256], ssq[:, 0:1], gq_b[:], ALU.mult, ALU.mult, [tk("ssq"), tk("gq")], [B_V, tk("cqn")])
                for c2 in range(2):
                    k.tr(ps_tr[:, c2 * 128:(c2 + 1) * 128], cqn[:, c2 * 128:(c2 + 1) * 128], ident_b[:], [tk("cqn"), tk("identb")], [B_TR], inc=(c2 == 1))
                k.act(cqT[:, :, s * 128:(s + 1) * 128], ps_tr[:, 0:256].rearrange("p (c t) -> p c t", t=128), AF.Identity, [], [B_TR, tk("cqT")])
            for h in range(4):
                for kc in range(8):
                    k.mm(ps_pj[:, 0:TT], wmi_b[:, kc, 256 + h * 128:256 + (h + 1) * 128], hn1T[:, kc, 0:TT], [tk("wts"), tk("hn1T")], [B_PJ],
                         start=(kc == 0), stop=(kc == 7), inc=(kc == 7))
                k.act(zs1[:, h, 0:TT], ps_pj[:, 0:TT], AF.Silu, [], [B_PJ, tk("zs1")])
            for hp in range(2):
                hh = (2 * hp, 2 * hp + 1)
                sets = {hh[0]: (ps_pj, B_PJ, ps_p2, B_P2, 0), hh[1]: (ps_v, B_V, ps_o, B_O, 1)}
                for h in hh:
                    pa, ba, pb, bb, u = sets[h]
                    for c2 in range(2):
                        k.mm(pa[:, 0:TT], wqu_b[:, c2, h * 256:h * 256 + 128], cqT[:, c2, 0:TT], [tk("wts"), tk("cqT")], [ba], start=(c2 == 0), stop=(c2 == 1), inc=(c2 == 1))
                for h in hh:
                    pa, ba, pb, bb, u = sets[h]
                    k.act(qnT[:, u, 0:TT], pa[:, 0:TT], AF.Identity, [], [ba, tk("qnT%d" % u)])
                for h in hh:
                    pa, ba, pb, bb, u = sets[h]
                    k.mm(pb[:, 0:TT], wukT_b[:, h, :], qnT[:, u, 0:TT], [tk("wukT"), tk("qnT%d" % u)], [bb])
                for h in hh:
                    pa, ba, pb, bb, u = sets[h]
                    k.act(qpT[:, h, 0:TT], pb[:, 0:TT], AF.Identity, [], [bb, tk("qpT%d" % h)])
                for h in hh:
                    pa, ba, pb, bb, u = sets[h]
                    for c2 in range(2):
                        k.mm(pa[0:64, 0:TT], wqu_b[:, c2, h * 256 + 128:h * 256 + 192], cqT[:, c2, 0:TT], [tk("wts"), tk("cqT")], [ba], start=(c2 == 0), stop=(c2 == 1), inc=(c2 == 1))
                    for c2 in range(2):
                        k.mm(pb[0:64, 0:TT], wqu_b[:, c2, h * 256 + 192:h * 256 + 256], cqT[:, c2, 0:TT], [tk("wts"), tk("cqT")], [bb], start=(c2 == 0), stop=(c2 == 1), inc=(c2 == 1))
                for h in hh:
                    pa, ba, pb, bb, u = sets[h]
                    k.tt(ra[:, u, 0:TT], pa[0:64, 0:TT], cs[:, 0:TT], ALU.mult, [tk("cs")], [ba, tk("ra%d" % u)])
                    k.tt(rbb[:, u, 0:TT], pb[0:64, 0:TT], sn[:, 0:TT], ALU.mult, [tk("cs")], [bb, tk("rbb%d" % u)])
                    k.tt(qrT[0:64, h, 0:TT], ra[:, u, 0:TT], rbb[:, u, 0:TT], ALU.add, [tk("ra%d" % u), tk("rbb%d" % u)], [tk("qrT%d" % h)])
            nkb = blk0 + NS

            def emit_s(h, j):
                jj = j - blk0
                qlo = max(0, jj) * 128
                buf = j % 3
                ksl = slice(j * 128, (j + 1) * 128)
                k.mm(ps_s[:, buf, qlo:TT], ckvT[:, ksl], qpT[:, h, qlo:TT], [tk("ckvT"), tk("qpT%d" % h)], [B_S[buf]], start=True, stop=False, inc=False)
                k.mm(ps_s[:, buf, qlo:TT], kropeT[:, ksl], qrT[:, h, qlo:TT], [tk("kropeT"), tk("qrT%d" % h)], [B_S[buf]], start=False, stop=True)
                k.act(pT[:, buf, qlo:TT], ps_s[:, buf, qlo:TT], AF.Exp, [], [B_S[buf], tk("pT%d" % buf)], scale=SCALE)
                if jj >= 0:
                    k.tt(pT[:, buf, qlo:qlo + 128], pT[:, buf, qlo:qlo + 128], tri_b[:], ALU.mult, [tk("trib")], [tk("pT%d" % buf)])
                if j == 0:
                    k.cp(pacc[:, h % 2, 0:TT], pT[:, buf, 0:TT], [tk("pT%d" % buf)], [tk("pacc%d" % (h % 2))])
                else:
                    k.tt(pacc[:, h % 2, qlo:TT], pacc[:, h % 2, qlo:TT], pT[:, buf, qlo:TT], ALU.add, [tk("pT%d" % buf)], [tk("pacc%d" % (h % 2))])

            def emit_pv(h, j):
                jj = j - blk0
                qlo = max(0, jj) * 128
                buf = j % 3
                k.mm(ps_o[:, qlo:TT], ckvtok[:, j, 0:128], pT[:, buf, qlo:TT], [tk("pT%d" % buf), tk("ckvtok")], [B_O],
                     start=(j == 0), stop=(j == nkb - 1))

            for h in range(4):
                if h == 0:
                    emit_s(h, 0)
                    if nkb > 1:
                        emit_s(h, 1)
                for j in range(nkb):
                    if j + 2 < nkb:
                        emit_s(h, j + 2)
                    emit_pv(h, j)
                if h < 3:
                    emit_s(h + 1, 0)
                    if nkb > 1:
                        emit_s(h + 1, 1)
                k.mm(ps_v[:, 0:TT], ones_f, pacc[:, h % 2, 0:TT], [tk("cst"), tk("pacc%d" % (h % 2))], [B_V])
                k.act(rdb[:, 0:TT], ps_v[:, 0:TT], AF.Ln, [], [B_V, tk("rdb")])
                k.act(rdb[:, 0:TT], rdb[:, 0:TT], AF.Exp, [], [tk("rdb")], scale=-1.0)
                k.tt(ocn[:, 0:TT], ps_o[:, 0:TT], rdb[:, 0:TT], ALU.mult, [tk("rdb")], [B_O, tk("ocn")])
                k.mm(ps_pj[:, 0:TT], wuv_b[:, h, :], ocn[:, 0:TT], [tk("wts"), tk("ocn")], [B_PJ])
                k.tt(og1[:, h, 0:TT], ps_pj[:, 0:TT], zs1[:, h, 0:TT], ALU.mult, [tk("zs1")], [B_PJ, tk("og1")])
            if fz is None:
                k.ld(s_o, o1T[:, t0:t0 + TT].rearrange("(h e) t -> e h t", e=128), og1[:, :, 0:TT], [], r=[tk("og1")])
            else:
                for s in range(NS):
                    for half, (pst, btk) in enumerate(((ps_pj, B_PJ), (ps_p2, B_P2))):
                        for h in range(4):
                            k.mm(pst[:, :], og1[:, h, s * 128:(s + 1) * 128], w1_b[:, h, half * 512:(half + 1) * 512], [tk("og1"), tk("wts")], [btk],
                                 start=(h == 0), stop=(h == 3), inc=(h == 3))
                    k.act(y1st[:, 0:512], ps_pj[:, :], AF.Identity, [], [B_PJ, tk("y1st")])
                    k.cp(y1st[:, 512:1024], ps_p2[:, :], [], [B_P2, tk("y1st")])
                    k.ld(s_o, fz["y1p"][t0 + s * 128:t0 + (s + 1) * 128, :], y1st[:], [], r=[tk("y1st"), tk("y1rows")])
                rend = t0 + TT
                while cc_next[0] < NTOK2 and rend >= min(NTOK2, cc_next[0] + CC_ROWS):
                    r0, r1 = cc_next[0], min(NTOK2, cc_next[0] + CC_ROWS)
                    P.cc("cc", fz["scc"], fz["y1p"][r0:r1, :], fz["y1f"][r0:r1, :], [], [tk("y1rows")])
                    cc_next[0] = r1
        P.finish("sp", [tk("og1")] + ([tk("y1st"), tk("hs")] if fz is not None else []))
        P.emit()
        return P.final_events()


def rope_tables_T(n):
    inv = (np.float32(10000.0) ** (-(np.arange(0, 64, 2, dtype=np.float32)) / np.float32(64))).astype(np.float32)
    ang = (np.arange(n, dtype=np.float32)[:, None] * inv[None, :]).astype(np.float32)
    cos, sin = np.cos(ang).astype(np.float32), np.sin(ang).astype(np.float32)
    cos2T = np.ascontiguousarray(np.concatenate([cos, cos], 1).T)
    sinsT = np.ascontiguousarray(np.concatenate([-sin, sin], 1).T)
    return cos2T, sinsT


def mla_inputs(inp, core, h1b, NTOK2):
    hg = core % 4
    h1p = None
    if h1b is not None:
        L = h1b.shape[0]
        h1p = np.zeros((NTOK2, 1024), np.float32)
        h1p[:L] = h1b
    kd = inp["kv_w_down"]
    wkvd = np.ascontiguousarray(np.concatenate([kd[:, 0:128], kd[:, 128:192], kd[:, 160:192], kd[:, 128:160]], 1))
    ku = inp["kv_w_up"].reshape(128, 16, 256)[:, 4 * hg:4 * hg + 4]
    wuk = np.ascontiguousarray(ku[:, :, 0:128].reshape(128, 512))
    wuv = np.ascontiguousarray(ku[:, :, 128:256].reshape(128, 512))
    mi = inp["mla_w_in"][0]
    wmi = np.ascontiguousarray(np.concatenate([mi[:, 0:256], mi[:, 256 + 512 * hg:256 + 512 * (hg + 1)]], 1))
    qu = inp["mla_w_q_up"][0].reshape(256, 16, 192)[:, 4 * hg:4 * hg + 4]
    wqu = np.ascontiguousarray(np.concatenate([qu[:, :, 0:128], qu[:, :, 128:192], qu[:, :, 160:192], qu[:, :, 128:160]], 2).reshape(256, 1024))
    cos2T, sinsT = rope_tables_T(NTOK2)
    d = {} if h1p is None else {"h1p": h1p}
    d.update(_mla_rest(inp, wkvd, wuk, wuv, wmi, wqu, cos2T, sinsT))
    return d


def _mla_rest(inp, wkvd, wuk, wuv, wmi, wqu, cos2T, sinsT):
    return {
        "g1": inp["pre_norm"][1:2].copy(), "gkv": inp["kv_norm"].reshape(1, 1024).copy(),
        "glat": inp["kv_latent_norm"].reshape(128, 1).copy(), "gq": inp["mla_q_latent_norm"][0:1].copy(),
        "wkvd": wkvd, "wuk": wuk, "wuv": wuv, "wmi": wmi, "wqu": wqu, "cos2T": cos2T, "sinsT": sinsT, "cstm": _consts_mla(),
    }


def emit_fin(nc, semst, h1s, y1f, gp1, out, NBK, prew=()):
    with contextlib.ExitStack() as st:
        P = Prog(nc, st, semst, "f", prew)
        k = K(P)
        sb = P.sb
        gp_b = sb("gp_b", [128, 1024], F32)
        NB_ = 6
        hb = sb("hb", [128, NB_, 1024], F32)
        yb = sb("yb", [128, NB_, 1024], F32)
        junk = sb("junk", [128, 1024], BF16)
        ss = sb("ss", [128, 1], F32)
        T = {}

        def tk(n):
            if n not in T:
                T[n] = Tk(n)
            return T[n]
        s_c = P.dmasem("c")
        s_i = [P.dmasem("i%d" % i_) for i_ in range(NB_)]
        s_o = [P.dmasem("o%d" % i_) for i_ in range(NB_)]
        k.ld(s_c, gp_b[:], gp1[0:1, :].partition_broadcast(128), [tk("gp")])
        for blk in range(NBK):
            sl = blk % NB_
            rows = slice(blk * 128, (blk + 1) * 128)
            k.ld(s_i[sl], hb[:, sl, :], h1s[rows, :], [tk("hb%d" % sl)])
            k.ld(s_i[sl], yb[:, sl, :], y1f[rows, :], [tk("yb%d" % sl)])
            tk("hb%d" % sl).w = (s_i[sl], P.dcnt[s_i[sl]])
            k.act(junk[:], yb[:, sl, :], AF.Square, [tk("yb%d" % sl)], [tk("junk"), tk("ss")], accum_out=ss[:])
            k.act(ss[:], ss[:], AF.Sqrt, [], [tk("ss")], scale=1.0 / 1024, bias=EPS)
            k.rcp(ss[:], ss[:], [], [tk("ss")])
            k.stt(yb[:, sl, :], yb[:, sl, :], ss[:, 0:1], gp_b[:], ALU.mult, ALU.mult, [tk("ss"), tk("gp")], [tk("yb%d" % sl)])
            k.tt(yb[:, sl, :], yb[:, sl, :], hb[:, sl, :], ALU.add, [tk("hb%d" % sl)], [tk("yb%d" % sl)], eng=("pool" if blk % 3 == 0 else "dve"))
            k.ld(s_o[sl], out[rows, :], yb[:, sl, :], [], r=[tk("yb%d" % sl)], q="act")
        P.finish("sp", [tk("yb%d" % i_) for i_ in range(NB_)])
        P.emit()
        return P.final_events()


GROUPS = [[0, 1, 2, 3], [4, 5, 6, 7]]
CC_ROWS = 1024


def emit_allreduce(nc, ev, src, dst, scc):
    with nc.Block() as block:
        @block.gpsimd
        def _(g):
            for hsem, v in ev:
                g.wait_ge(hsem, v)
            rows = src.ap().shape[0]
            n = 0
            for r0 in range(0, rows, CC_ROWS):
                r1 = min(rows, r0 + CC_ROWS)
                g.collective_compute("AllReduce", ALU.add, replica_groups=GROUPS,
                                     ins=[src.ap()[r0:r1, :]], outs=[dst.ap()[r0:r1, :]]).then_inc(scc)
                n += 1
            g.wait_ge(scc, n)
    rows_ = src.ap().shape[0]
    return (rows_ + CC_ROWS - 1) // CC_ROWS


def build_fused(NTOK, NTOK2):
    nc = bass.Bass("TRN2", target_bir_lowering=False)

    def din(n, s_, dt=F32):
        return nc.dram_tensor(n, list(s_), dt, kind="ExternalInput").ap()
    ioG = gdn_decl(nc, NTOK)
    ioM = mla_decl(nc, NTOK2)
    w0 = din("w0", [512, 1024])
    gp0 = din("gp0", [1, 1024])
    w1 = din("w1", [512, 1024])
    gp1 = din("gp1", [1, 1024])
    out = nc.dram_tensor("out", [NTOK2, 1024], F32, kind="ExternalOutput").ap()
    y0p = nc.dram_tensor("y0p", [NTOK2, 1024], F32)
    y0f = nc.dram_tensor("y0f", [NTOK2, 1024], F32)
    h1s = nc.dram_tensor("h1s", [NTOK2, 1024], F32)
    y1p = nc.dram_tensor("y1p", [NTOK2, 1024], F32)
    y1f = nc.dram_tensor("y1f", [NTOK2, 1024], F32)
    with contextlib.ExitStack() as semst:
        scc0 = semst.enter_context(nc.semaphore("cc0"))
        scc1 = semst.enter_context(nc.semaphore("cc1"))
        ev = emit_gdn(nc, semst, ioG, NTOK, fz=dict(y0p=y0p.ap(), y0f=y0f.ap(), scc=scc0, w0=w0))
        ev = emit_mla(nc, semst, ioM, NTOK2, prew=ev,
                      fz=dict(xp=ioG["xp"], y0f=y0f.ap(), gp0=gp0, h1s=h1s.ap(), y1p=y1p.ap(), y1f=y1f.ap(), scc=scc1, w1=w1))
        emit_fin(nc, semst, h1s.ap(), y1f.ap(), gp1, out, NTOK2 // 128, prew=ev)
    return nc


def fused_inputs(inp, core, NTOK, NTOK2):
    hg = core % 4
    d = gdn_inputs(inp, core, NTOK)
    d.update(mla_inputs(inp, core, None, NTOK2))
    d["w0"] = np.ascontiguousarray(inp["gdn_w_out"][0][hg * 512:(hg + 1) * 512])
    d["w1"] = np.ascontiguousarray(inp["mla_w_out"][0][hg * 512:(hg + 1) * 512])
    d["gp0"] = inp["post_norm"][0:1].copy()
    d["gp1"] = inp["post_norm"][1:2].copy()
    return d


_NC_CACHE = {}


def _get(name, fn, *a):
    key = (name,) + a
    if key not in _NC_CACHE:
        _NC_CACHE[key] = fn(*a)
    return _NC_CACHE[key]


def kernel(**inputs):
    inp = {k_: np.ascontiguousarray(np.asarray(v)) for k_, v in inputs.items()}
    B, SEQ, D = inp["x"].shape
    L = SEQ + 16
    NTOK2 = ((L + 127) // 128) * 128
    NTOK = ((NTOK2 + 48 + 127) // 128) * 128
    cores = list(range(8))
    nc = _get("fused", build_fused, NTOK, NTOK2)
    res = run_bass_kernel_spmd(nc, [fused_inputs(inp, c, NTOK, NTOK2) for c in cores], core_ids=cores).results
    return np.stack([np.asarray(res[4 * b]["out"])[16:L] for b in range(B)], 0).astype(np.float32)
```

```python
import contextlib
import numpy as np
import ml_dtypes
import concourse.bass as bass
import concourse.mybir as mybir
from concourse.bass_utils import run_bass_kernel_spmd

F32 = mybir.dt.float32
BF16 = mybir.dt.bfloat16
AF = mybir.ActivationFunctionType
ALU = mybir.AluOpType
EPS = 1e-6


class Tk:
    __slots__ = ("name", "w", "r")

    def __init__(self, name):
        self.name = name
        self.w = None
        self.r = []


class Prog:
    ENGS = ("pe", "act", "dve", "pool", "sp")

    def __init__(self, nc, stack, semst=None, pfx="", prew=()):
        self.nc = nc
        self.stack = stack
        self.semst = semst if semst is not None else stack
        self.pfx = pfx
        self.prew = list(prew)
        self.ops = {e: [] for e in self.ENGS}
        self.cnt = {e: 0 for e in self.ENGS}
        self.known = {e: {} for e in self.ENGS}
        self.sems = {}
        self.dcnt = {}
        for e in self.ENGS:
            self.sems[e] = self.semst.enter_context(nc.semaphore(pfx + "s_" + e))

    def final_events(self):
        ev = [(self.sems[e], self.cnt[e]) for e in self.ENGS if self.cnt[e] > 0]
        ev += [(self.sems[n], c) for n, c in self.dcnt.items() if c > 0]
        return ev

    def sb(self, name, shape, dt):
        return self.stack.enter_context(self.nc.sbuf_tensor(self.pfx + name, list(shape), dt))

    def ps(self, name, shape, dt):
        return self.stack.enter_context(self.nc.psum_tensor(self.pfx + name, list(shape), dt))

    def dmasem(self, name):
        self.sems[name] = self.semst.enter_context(self.nc.semaphore(self.pfx + "d_" + name))
        self.dcnt[name] = 0
        return name

    def _need(self, eng, ev, waits):
        if ev is None:
            return
        key, val = ev
        if key == eng and eng == "pe":
            return
        if self.known[eng].get(key, 0) >= val:
            return
        if key in self.ENGS and key != eng:
            assert self.cnt[key] >= val, (eng, ev, self.cnt[key])
        self.known[eng][key] = val
        for i, (k, v) in enumerate(waits):
            if k == key:
                waits[i] = (k, max(v, val))
                return
        waits.append((key, val))

    def _deps(self, eng, reads, writes):
        waits = []
        for t in reads:
            self._need(eng, t.w, waits)
        for t in writes:
            self._need(eng, t.w, waits)
            for ev in t.r:
                self._need(eng, ev, waits)
        return waits

    def _mark(self, ev, reads, writes):
        for t in writes:
            t.w = ev
            t.r = []
        for t in reads:
            if t not in writes:
                t.r.append(ev)
                if len(t.r) > 8:
                    d = {}
                    for k, v in t.r:
                        d[k] = max(d.get(k, 0), v)
                    t.r = list(d.items())

    def op(self, eng, fn, reads=(), writes=(), inc=True):
        waits = self._deps(eng, reads, writes)
        if inc:
            self.cnt[eng] += 1
            ev = (eng, self.cnt[eng])
        else:
            ev = (eng, self.cnt[eng] + 1)
        self._mark(ev, reads, writes)
        self.ops[eng].append((fn, waits, ("c", inc)))

    def dma(self, q, sem, fn, reads=(), writes=()):
        waits = self._deps(q, reads, writes)
        self.dcnt[sem] += 16
        ev = (sem, self.dcnt[sem])
        self._mark(ev, reads, writes)
        self.ops[q].append((fn, waits, ("d", sem)))

    def cc(self, semname, hsem, src, dst, reads=(), writes=()):
        if semname not in self.sems:
            self.sems[semname] = hsem
            self.dcnt[semname] = 0
        waits = self._deps("pool", reads, writes)
        self.dcnt[semname] += 1
        ev = (semname, self.dcnt[semname])
        self._mark(ev, reads, writes)
        fn = lambda e: e.collective_compute("AllReduce", ALU.add, replica_groups=GROUPS, ins=[src], outs=[dst])
        self.ops["pool"].append((fn, waits, ("k", semname)))

    def finish(self, eng, tks):
        waits = []
        for t in tks:
            self._need(eng, t.w, waits)
            for ev in t.r:
                self._need(eng, ev, waits)
        self.ops[eng].append((None, waits, ("w", None)))

    def emit(self):
        nc, sems, ops = self.nc, self.sems, self.ops
        prew = self.prew
        with nc.Block() as block:
            def run(name, e):
                for hsem, v in prew:
                    e.wait_ge(hsem, v)
                for fn, waits, kind in ops[name]:
                    for k, v in waits:
                        e.wait_ge(sems[k], v)
                    if fn is None:
                        continue
                    ins = fn(e)
                    if kind[0] == "c":
                        if kind[1]:
                            ins.then_inc(sems[name], 1)
                    elif kind[0] == "k":
                        ins.then_inc(sems[kind[1]], 1)
                    else:
                        ins.then_inc(sems[kind[1]], 16)

            @block.tensor
            def _(e):
                run("pe", e)

            @block.scalar
            def _(e):
                run("act", e)

            @block.vector
            def _(e):
                run("dve", e)

            @block.gpsimd
            def _(e):
                run("pool", e)

            @block.sync
            def _(e):
                run("sp", e)


class K:
    def __init__(self, P):
        self.P = P

    def act(self, out, in_, func, r, w, **kw):
        self.P.op("act", lambda e: e.activation(out=out, in_=in_, func=func, **kw), r, w)

    def tt(self, out, in0, in1, op, r, w, eng="dve"):
        self.P.op(eng, lambda e: e.tensor_tensor(out=out, in0=in0, in1=in1, op=op), r, w)

    def ts(self, out, in0, s1, op0, r, w, s2=None, op1=None, eng="dve"):
        if op1 is None:
            self.P.op(eng, lambda e: e.tensor_scalar(out=out, in0=in0, scalar1=s1, scalar2=None, op0=op0), r, w)
        else:
            self.P.op(eng, lambda e: e.tensor_scalar(out=out, in0=in0, scalar1=s1, scalar2=s2, op0=op0, op1=op1), r, w)

    def stt(self, out, in0, scalar, in1, op0, op1, r, w, eng="dve"):
        self.P.op(eng, lambda e: e.scalar_tensor_tensor(out=out, in0=in0, scalar=scalar, in1=in1, op0=op0, op1=op1), r, w)

    def cp(self, out, in_, r, w, eng="dve"):
        self.P.op(eng, lambda e: e.tensor_copy(out=out, in_=in_), r, w)

    def rcp(self, out, in_, r, w):
        self.P.op("dve", lambda e: e.reciprocal(out=out, in_=in_), r, w)

    def mm(self, out, lhsT, rhs, r, w, start=True, stop=True, inc=True, sgc=False):
        self.P.op("pe", lambda e: e.matmul(out, lhsT=lhsT, rhs=rhs, start=start, stop=stop, skip_group_check=sgc), r, w, inc=inc)

    def tr(self, out, in_, ident, r, w, inc=True):
        self.P.op("pe", lambda e: e.transpose(out, in_, ident), r, w, inc=inc)

    def ld(self, sem, out, in_, w, r=(), q="sp"):
        self.P.dma(q, sem, lambda e: e.dma_start(out=out, in_=in_), r, w)


def _consts():
    c = np.zeros((128, 128 * 2 + 64 * 3), np.float32)
    c[:, 0:128] = np.eye(128, dtype=np.float32)
    c[:, 128:256] = 1.0
    kk = np.arange(64)
    c[0:64, 256:320] = (kk[:, None] <= kk[None, :])
    c[0:64, 320:384] = (kk[None, :] >= kk[:, None])
    c[0:64, 384:448] = (kk[None, :] > kk[:, None])
    return c


def gdn_decl(nc, NTOK):
    def din(n, s, dt=F32):
        return nc.dram_tensor(n, list(s), dt, kind="ExternalInput").ap()
    io = {}
    io["xp"] = din("xp", [NTOK, 1024])
    io["gpre"] = din("gpre", [1, 1024])
    io["wq"] = din("wq", [1024, 1536])
    io["wba"] = din("wba", [1024, 8])
    io["convw"] = din("convw", [128, 32])
    io["alog"] = din("alog", [1, 4])
    io["dtb"] = din("dtb", [1, 4])
    io["onorm"] = din("onorm", [128, 1])
    io["cst"] = din("cst", [128, 448])
    return io


def build_gdn(NTOK):
    assert NTOK % 128 == 0
    nc = bass.Bass("TRN2", target_bir_lowering=False)
    io = gdn_decl(nc, NTOK)
    io["oT"] = nc.dram_tensor("oT", [512, NTOK], BF16, kind="ExternalOutput").ap()
    emit_gdn(nc, None, io, NTOK)
    return nc


def emit_gdn(nc, semst, io, NTOK, prew=(), fz=None):
    xp, gpre, wq, wba, convw, alog, dtb, onorm, cst = (io[n] for n in ("xp", "gpre", "wq", "wba", "convw", "alog", "dtb", "onorm", "cst"))
    oT = io.get("oT")
    with contextlib.ExitStack() as st:
        P = Prog(nc, st, semst, "g", prew)
        k = K(P)
        sb, ps = P.sb, P.ps
        if fz is not None:
            w0_b = sb("w0_b", [128, 4, 1024], BF16)
            yst = sb("yst", [128, 2, 1024], F32)
        wb = sb("wb", [128, 8, 1536], BF16)
        wbab = sb("wbab", [128, 8, 8], BF16)
        gpre_b = sb("gpre_b", [128, 1024], F32)
        convw_s = sb("convw_s", [128, 32], F32)
        alog_b = sb("alog_b", [64, 4], F32)
        dtb_b = sb("dtb_b", [64, 4], F32)
        negA = sb("negA", [64, 4], F32)
        onorm_s = sb("onorm_s", [128, 1], F32)
        cst_s = sb("cst_s", [128, 448], F32)
        ident_b = sb("ident_b", [128, 128], BF16)
        xc = sb("xc", [128, 8, 3 + 512], F32)
        S_f = sb("S_f", [128, 4, 128], F32)
        S_b = sb("S_b", [128, 4, 128], BF16)
        ident_f = cst_s[:, 0:128]
        ones_f = cst_s[:, 128:256]
        tri = cst_s[0:64, 256:320]
        mincl = cst_s[0:64, 320:384]
        mstrict = cst_s[0:64, 384:448]
        xs = sb("xs", [128, 4, 1024], F32)
        wstage = xs[:].rearrange("p (a b) d -> p a (b d)", a=2)
        junk = sb("junk", [128, 1024], BF16)
        ss = sb("ss", [128, 4], F32)
        rstd = sb("rstd", [128, 4], F32)
        hn = sb("hn", [128, 2, 1024], BF16)
        hnT = sb("hnT", [128, 8, 512], BF16)
        acc = sb("acc", [128, 512], F32)
        qkf = sb("qkf", [128, 512], F32)
        sq = sb("sq", [128, 512], F32)
        rtmp = sb("rtmp", [128, 512], F32)
        qT = sb("qT", [128, 2, 512], BF16)
        kT2 = sb("kT2", [128, 2, 2, 512], BF16)
        vT = sb("vT", [128, 4, 512], BF16)
        zs2 = sb("zs2", [128, 2, 4, 512], F32)
        bet = sb("bet", [64, 8, 4], F32)
        gt = sb("gt", [64, 8, 4], F32)
        gg = sb("gg", [64, 8, 4], F32)
        gc = sb("gc", [64, 8, 4], F32)
        kap = sb("kap", [64, 8, 4], F32)
        ngam = sb("ngam", [64, 8, 4], F32)
        bk = sb("bk", [64, 8, 4], F32)
        glast = sb("glast", [128, 8, 4], F32)
        gB = sb("gB", [64, 8, 128], F32)
        egc = sb("egc", [128, 512], F32)
        qdT = sb("qdT", [128, 4, 512], BF16)
        attnT = sb("attnT", [64, 4, 512], BF16)
        U = sb("U", [64, 4, 512], F32)
        UT = sb("UT", [64, 4, 2, 512], F32)
        Rr = sb("Rr", [64, 4, 512], F32)
        Rb = sb("Rb", [64, 4, 512], BF16)
        ktok = sb("ktok", [64, 2, 8, 128], BF16)
        vtok = sb("vtok", [64, 4, 8, 128], BF16)
        rr = sb("rr", [64, 4, 128], BF16)
        vn = sb("vn", [64, 4, 128], BF16)
        vnk = sb("vnk", [64, 4, 128], BF16)
        osq = sb("osq", [128, 512], F32)
        otmp = sb("otmp", [128, 512], F32)
        og = sb("og", [128, 4, 128], BF16)
        ps_tr = ps("ps_tr", [128, 1024], BF16)
        ps_pj = ps("ps_pj", [128, 512], F32)
        ps_ms = ps("ps_ms", [128, 512], F32)
        ps_bc = ps("ps_bc", [128, 512], F32)
        ps_sc = ps("ps_sc", [128, 4, 512], F32)
        B_TR, B_PJ, B_MS, B_BC = Tk("B_TR"), Tk("B_PJ"), Tk("B_MS"), Tk("B_BC")
        B_S = [Tk("B_S%d" % h) for h in range(4)]

        T = {}

        def tk(n):
            if n not in T:
                T[n] = Tk(n)
            return T[n]

        s_c = P.dmasem("c")
        s_w = [P.dmasem("w0"), P.dmasem("w1")]
        s_x = P.dmasem("x")
        s_o = P.dmasem("o")
        s_y = [P.dmasem("y0"), P.dmasem("y1")]
        k.ld(s_c, cst_s[:], cst[:, :], [tk("cst")])
        k.ld(s_c, gpre_b[:], gpre[0:1, :].partition_broadcast(128), [tk("gpre")])
        k.ld(s_c, convw_s[:], convw[:, :], [tk("convw")])
        k.ld(s_c, alog_b[:], alog[0:1, :].partition_broadcast(64), [tk("alog")])
        k.ld(s_c, dtb_b[:], dtb[0:1, :].partition_broadcast(64), [tk("dtb")])
        k.ld(s_c, onorm_s[:], onorm[:, :], [tk("onorm")])
        for _n in ("cst", "gpre", "convw", "alog", "dtb", "onorm"):
            tk(_n).w = (s_c, P.dcnt[s_c])
        k.cp(ident_b[:], ident_f, [tk("cst")], [tk("identb")])
        k.act(negA[:], alog_b[:], AF.Exp, [tk("alog")], [tk("negA")])
        k.ts(negA[:], negA[:], -1.0, ALU.mult, [], [tk("negA")])
        for kc in range(8):
            sl = kc % 2
            k.ld(s_w[sl], wstage[:, sl, 0:1536], wq[kc * 128:(kc + 1) * 128, :], [tk("wst%d" % sl)])
            k.ld(s_w[sl], wstage[:, sl, 1536:1544], wba[kc * 128:(kc + 1) * 128, :], [tk("wst%d" % sl)])
            k.cp(wb[:, kc, :], wstage[:, sl, 0:1536], [tk("wst%d" % sl)], [tk("wb")], eng="pool")
            k.cp(wbab[:, kc, :], wstage[:, sl, 1536:1544], [tk("wst%d" % sl)], [tk("wb")], eng="pool")
        if fz is not None:
            for h in range(4):
                sl = h % 2
                k.ld(s_w[sl], wstage[:, sl, 0:1024], fz["w0"][h * 128:(h + 1) * 128, :], [tk("wst%d" % sl)])
                k.cp(w0_b[:, h, :], wstage[:, sl, 0:1024], [tk("wst%d" % sl)], [tk("w0b")], eng="pool")
        P.op("dve", lambda e: e.memset(S_f[:], 0.0), [], [tk("S_f0"), tk("S_f1"), tk("S_f2"), tk("S_f3")])
        P.op("dve", lambda e: e.memset(S_b[:], 0.0), [], [tk("S_b0"), tk("S_b1"), tk("S_b2"), tk("S_b3")])
        P.op("dve", lambda e: e.memset(xc[:], 0.0), [], [tk("xc%d" % g) for g in range(8)])

        def gen_AB(ti):
            t0 = ti * 512
            TT = min(512, NTOK - t0)
            NS = TT // 128
            p = ti % 2
            kT = kT2[:, p]
            zs = zs2[:, p]
            kTn = "kT%d_" % p + "%d"
            k.ld(s_x, xs[:, 0:NS, :], xp[t0:t0 + TT, :].rearrange("(s p) d -> p s d", p=128),
                 [tk("xs")] + ([tk("wst0"), tk("wst1")] if ti == 0 else []))
            for s in range(NS):
                k.act(junk[:], xs[:, s, :], AF.Square, [tk("xs")], [tk("junk"), tk("ss")], accum_out=ss[:, s:s + 1])
            k.act(rstd[:, 0:NS], ss[:, 0:NS], AF.Sqrt, [tk("ss")], [tk("rstd")], scale=1.0 / 1024, bias=EPS)
            k.rcp(rstd[:, 0:NS], rstd[:, 0:NS], [], [tk("rstd")])
            yield
            for s in range(NS):
                k.stt(hn[:, s % 2, :], xs[:, s, :], rstd[:, s:s + 1], gpre_b[:], ALU.mult, ALU.mult,
                      [tk("xs"), tk("rstd"), tk("gpre")], [tk("hn%d" % (s % 2))])
                for kc in range(8):
                    k.tr(ps_tr[:, kc * 128:(kc + 1) * 128], hn[:, s % 2, kc * 128:(kc + 1) * 128], ident_b[:],
                         [tk("hn%d" % (s % 2)), tk("identb")], [B_TR], inc=(kc == 7))
                k.act(hnT[:, :, s * 128:(s + 1) * 128], ps_tr[:].rearrange("p (k t) -> p k t", t=128), AF.Identity,
                      [], [B_TR, tk("hnT")])
                yield
            for g in range(12):
                for kc in range(8):
                    k.mm(ps_pj[:, 0:TT], wb[:, kc, g * 128:(g + 1) * 128], hnT[:, kc, 0:TT], [tk("wb"), tk("hnT")], [B_PJ],
                         start=(kc == 0), stop=(kc == 7), inc=(kc == 7))
                if g < 8:
                    xg = tk("xc%d" % g)
                    k.cp(xc[:, g, 0:3], xc[:, g, 512:515], [], [xg])
                    k.act(xc[:, g, 3:3 + TT], ps_pj[:, 0:TT], AF.Identity, [], [B_PJ, xg])
                    k.ts(acc[:, 0:TT], xc[:, g, 3:3 + TT], convw_s[:, g * 4 + 3:g * 4 + 4], ALU.mult, [xg, tk("convw")], [tk("acc")])
                    for j in (2, 1, 0):
                        k.stt(acc[:, 0:TT], xc[:, g, j:j + TT], convw_s[:, g * 4 + j:g * 4 + j + 1], acc[:, 0:TT], ALU.mult, ALU.add,
                              [xg, tk("convw")], [tk("acc")])
                    if TT < 512:
                        pass
                    if g < 4:
                        dst = qT[:, g, 0:TT] if g < 2 else kT[:, g - 2, 0:TT]
                        dtk = tk("qT%d" % g) if g < 2 else tk(kTn % (g - 2))
                        k.act(qkf[:, 0:TT], acc[:, 0:TT], AF.Silu, [tk("acc")], [tk("qkf")])
                        k.act(sq[:, 0:TT], qkf[:, 0:TT], AF.Square, [tk("qkf")], [tk("sq")])
                        k.mm(ps_bc[:, 0:TT], ones_f, sq[:, 0:TT], [tk("cst"), tk("sq")], [B_BC])
                        if g < 2:
                            k.act(rtmp[:, 0:TT], ps_bc[:, 0:TT], AF.Ln, [], [B_BC, tk("rtmp")], scale=128.0, bias=128.0 * EPS)
                        else:
                            k.act(rtmp[:, 0:TT], ps_bc[:, 0:TT], AF.Ln, [], [B_BC, tk("rtmp")], scale=1.0, bias=EPS)
                        k.act(rtmp[:, 0:TT], rtmp[:, 0:TT], AF.Exp, [], [tk("rtmp")], scale=-0.5)
                        k.tt(dst, qkf[:, 0:TT], rtmp[:, 0:TT], ALU.mult, [tk("qkf"), tk("rtmp")], [dtk])
                    else:
                        k.act(vT[:, g - 4, 0:TT], acc[:, 0:TT], AF.Silu, [tk("acc")], [tk("vT%d" % (g - 4))])
                else:
                    k.act(zs[:, g - 8, 0:TT], ps_pj[:, 0:TT], AF.Silu, [], [B_PJ, tk("zs%d" % p)])
                yield

        def step(g_, n=1):
            if g_ is None:
                return
            for _ in range(n):
                try:
                    next(g_)
                except StopIteration:
                    return

        def drain(g_):
            if g_ is None:
                return
            for _ in g_:
                pass

        ntiles = (NTOK + 511) // 512
        cc_next = [0]
        drain(gen_AB(0))
        for ti in range(ntiles):
            t0 = ti * 512
            TT = min(512, NTOK - t0)
            NS = TT // 128
            NCH = TT // 64
            p = ti % 2
            kT = kT2[:, p]
            zs = zs2[:, p]
            kTn = "kT%d_" % p + "%d"
            g_next = gen_AB(ti + 1) if ti + 1 < ntiles else None
            for n in range(NCH):
                for kc in range(8):
                    k.mm(ps_ms[0:64, n * 8:(n + 1) * 8], hnT[:, kc, n * 64:(n + 1) * 64], wbab[:, kc, :], [tk("hnT"), tk("wb")], [B_MS],
                         start=(kc == 0), stop=(kc == 7), inc=(kc == 7 and n == NCH - 1))
            bav = ps_ms[0:64, 0:NCH * 8].rearrange("p (n c) -> p n c", c=8)
            k.act(bet[:, 0:NCH, :], bav[:, :, 0:4], AF.Sigmoid, [], [B_MS, tk("bet")])
            k.tt(gt[:, 0:NCH, :], bav[:, :, 4:8], dtb_b[:].unsqueeze(1).to_broadcast([64, NCH, 4]), ALU.add, [tk("dtb")], [B_MS, tk("gt")])
            k.act(gt[:, 0:NCH, :], gt[:, 0:NCH, :], AF.Exp, [], [tk("gt")])
            k.act(gt[:, 0:NCH, :], gt[:, 0:NCH, :], AF.Ln, [], [tk("gt")], bias=1.0)
            k.tt(gg[:, 0:NCH, :], gt[:, 0:NCH, :], negA[:].unsqueeze(1).to_broadcast([64, NCH, 4]), ALU.mult, [tk("gt"), tk("negA")], [tk("gg")])
            ggf = gg[:, 0:NCH, :].rearrange("p n h -> p (n h)")
            k.mm(ps_ms[0:64, 64:64 + NCH * 4], tri, ggf, [tk("cst"), tk("gg")], [B_MS])
            k.mm(ps_ms[:, 128:128 + NCH * 4], ones_f[0:64, :], ggf, [tk("cst"), tk("gg")], [B_MS])
            gcv = ps_ms[0:64, 64:64 + NCH * 4].rearrange("p (n h) -> p n h", h=4)
            glv = ps_ms[:, 128:128 + NCH * 4].rearrange("p (n h) -> p n h", h=4)
            k.cp(gc[:, 0:NCH, :], gcv, [], [B_MS, tk("gc")])
            k.act(glast[:, 0:NCH, :], glv, AF.Exp, [], [B_MS, tk("glast")])
            k.tt(kap[:, 0:NCH, :], glv[0:64], gc[:, 0:NCH, :], ALU.subtract, [tk("gc")], [B_MS, tk("kap")])
            k.act(kap[:, 0:NCH, :], kap[:, 0:NCH, :], AF.Exp, [], [tk("kap")])
            k.act(ngam[:, 0:NCH, :], gc[:, 0:NCH, :], AF.Exp, [tk("gc")], [tk("ngam")])
            k.ts(ngam[:, 0:NCH, :], ngam[:, 0:NCH, :], -1.0, ALU.mult, [], [tk("ngam")])
            k.tt(bk[:, 0:NCH, :], bet[:, 0:NCH, :], kap[:, 0:NCH, :], ALU.mult, [tk("bet"), tk("kap")], [tk("bk")])
            for j in range(6):
                src = kT[:, j, :] if j < 2 else vT[:, j - 2, :]
                stk = tk(kTn % j) if j < 2 else tk("vT%d" % (j - 2))
                for n in range(NCH):
                    k.tr(ps_tr[0:64, n * 128:(n + 1) * 128], src[:, n * 64:(n + 1) * 64], ident_b[:], [stk, tk("identb")], [B_TR],
                         inc=(n == NCH - 1))
                dst = ktok[:, j, 0:NCH, :] if j < 2 else vtok[:, j - 2, 0:NCH, :]
                dtk = tk("ktok%d" % j) if j < 2 else tk("vtok%d" % (j - 2))
                k.act(dst, ps_tr[0:64, 0:NCH * 128].rearrange("p (n d) -> p n d", d=128), AF.Identity, [], [B_TR, dtk])
            W = NCH * 64
            v3 = lambda ap_: ap_.rearrange("p (n c) -> p n c", c=64)
            for h in range(4):
                k.cp(gB[:, 0:NCH, :], gg[:, 0:NCH, h:h + 1].to_broadcast([64, NCH, 128]), [tk("gg")], [tk("gB")])
                for n in range(NCH):
                    k.mm(ps_sc[:, h, n * 64:(n + 1) * 64], gB[:, n, :], tri, [tk("gB"), tk("cst")], [B_S[h]], inc=(n == NCH - 1))
            for h in range(4):
                qh = h // 2
                t1h = UT[:, h, 1, 0:W]
                k.act(egc[:, 0:W], ps_sc[:, h, 0:W], AF.Exp, [], [B_S[h], tk("egc")])
                k.tt(qdT[:, h, 0:W], qT[:, qh, 0:W], egc[:, 0:W], ALU.mult, [tk("qT%d" % qh), tk("egc")], [tk("qdT%d" % h)])
                k.tt(v3(t1h), v3(ps_sc[0:64, h, 0:W]), gc[:, 0:NCH, h:h + 1].to_broadcast([64, NCH, 64]),
                     ALU.subtract, [tk("gc")], [B_S[h], tk("UT%d_1" % h)])
            for h in range(4):
                t1h, Dmh, Bsh = UT[:, h, 1, 0:W], Rr[:, h, 0:W], UT[:, h, 0, 0:W]
                k.ts(t1h, t1h, 0.0, ALU.min, [], [tk("UT%d_1" % h)])
                k.act(t1h, t1h, AF.Exp, [], [tk("UT%d_1" % h)])
                k.tt(v3(Dmh), v3(t1h), mincl.unsqueeze(1).to_broadcast([64, NCH, 64]), ALU.mult, [tk("UT%d_1" % h), tk("cst")], [tk("R%d" % h)])
                k.tt(v3(Bsh), mstrict.unsqueeze(1).to_broadcast([64, NCH, 64]), bet[:, 0:NCH, h:h + 1].to_broadcast([64, NCH, 64]), ALU.mult,
                     [tk("cst"), tk("bet")], [tk("UT%d_0" % h)])
            for h in range(4):
                qh = h // 2
                for n in range(NCH):
                    k.mm(ps_sc[0:64, h, n * 64:(n + 1) * 64], kT[:, qh, n * 64:(n + 1) * 64], qT[:, qh, n * 64:(n + 1) * 64],
                         [tk(kTn % qh), tk("qT%d" % qh)], [B_S[h]], inc=(n == NCH - 1))
                k.tt(attnT[:, h, 0:W], ps_sc[0:64, h, 0:W], Rr[:, h, 0:W], ALU.mult, [tk("R%d" % h)], [B_S[h], tk("attnT%d" % h)])
            for h in range(4):
                qh = h // 2
                for n in range(NCH):
                    k.mm(ps_sc[0:64, h, n * 64:(n + 1) * 64], kT[:, qh, n * 64:(n + 1) * 64], kT[:, qh, n * 64:(n + 1) * 64],
                         [tk(kTn % qh)], [B_S[h]], inc=(n == NCH - 1))
                k.tt(U[:, h, 0:W], ps_sc[0:64, h, 0:W], Rr[:, h, 0:W], ALU.mult, [tk("R%d" % h)], [B_S[h], tk("U%d" % h)])
                k.tt(U[:, h, 0:W], U[:, h, 0:W], UT[:, h, 0, 0:W], ALU.mult, [tk("UT%d_0" % h)], [tk("U%d" % h)])
            for h in range(4):
                for n in range(NCH):
                    k.tr(ps_sc[0:64, h, n * 64:(n + 1) * 64], U[:, h, n * 64:(n + 1) * 64], ident_f[0:64, 0:64], [tk("U%d" % h), tk("cst")], [B_S[h]],
                         inc=(n == NCH - 1))
                k.cp(UT[:, h, 0, 0:W], ps_sc[0:64, h, 0:W], [], [B_S[h], tk("UT%d_0" % h)])
            for h in range(4):
                k.stt(v3(Rr[:, h, 0:W]), v3(U[:, h, 0:W]), -1.0,
                      ident_f[0:64, 0:64].unsqueeze(1).to_broadcast([64, NCH, 64]), ALU.mult, ALU.add, [tk("U%d" % h), tk("cst")], [tk("R%d" % h)])
            W = NCH * 64
            cur = 0
            for lvl in range(1, 6):
                nxt = 1 - cur
                last = (lvl == 5)
                for h in range(4):
                    for n in range(NCH):
                        c = slice(n * 64, (n + 1) * 64)
                        k.mm(ps_sc[0:64, h, c], U[:, h, c], UT[:, h, cur, c], [tk("U%d" % h), tk("UT%d_%d" % (h, cur))], [B_S[h]], inc=(n == NCH - 1))
                    k.cp(UT[:, h, nxt, 0:W], ps_sc[0:64, h, 0:W], [], [B_S[h], tk("UT%d_%d" % (h, nxt))])
                step(g_next)
                if not last:
                    for h in range(4):
                        for n in range(NCH):
                            c = slice(n * 64, (n + 1) * 64)
                            k.mm(ps_sc[0:64, h, c], UT[:, h, cur, c], U[:, h, c], [tk("U%d" % h), tk("UT%d_%d" % (h, cur))], [B_S[h]], inc=(n == NCH - 1))
                        k.act(U[:, h, 0:W], ps_sc[0:64, h, 0:W], AF.Identity, [], [B_S[h], tk("U%d" % h)])
                    step(g_next)
                for h in range(4):
                    for n in range(NCH):
                        c = slice(n * 64, (n + 1) * 64)
                        k.mm(ps_sc[0:64, h, c], UT[:, h, nxt, c], Rr[:, h, c], [tk("UT%d_%d" % (h, nxt)), tk("R%d" % h)], [B_S[h]], inc=(n == NCH - 1))
                    if not last:
                        k.tt(Rr[:, h, 0:W], ps_sc[0:64, h, 0:W], Rr[:, h, 0:W], ALU.add, [], [B_S[h], tk("R%d" % h)])
                    else:
                        k.tt(Rb[:, h, 0:W], ps_sc[0:64, h, 0:W], Rr[:, h, 0:W], ALU.add, [tk("R%d" % h)], [B_S[h], tk("Rb%d" % h)])
                cur = nxt
                step(g_next)
            drain(g_next)
            for n in range(NCH):
                c = slice(n * 64, (n + 1) * 64)
                par = n % 2
                for h in range(4):
                    qh = h // 2
                    k.mm(ps_sc[0:64, h, 0:128], kT[:, qh, c], S_b[:, h, :], [tk(kTn % qh), tk("S_b%d" % h)], [B_S[h]])
                for h in range(4):
                    k.stt(rr[:, h, :], ps_sc[0:64, h, 0:128], ngam[:, n, h:h + 1], vtok[:, h, n, :], ALU.mult, ALU.add,
                          [tk("ngam"), tk("vtok%d" % h)], [B_S[h], tk("rr%d" % h)])
                for h in range(4):
                    k.mm(ps_sc[0:64, h, 128:256], Rb[:, h, c], rr[:, h, :], [tk("Rb%d" % h), tk("rr%d" % h)], [B_S[h]])
                for h in range(4):
                    k.act(vn[:, h, :], ps_sc[0:64, h, 128:256], AF.Identity, [tk("bet")], [B_S[h], tk("vn%d" % h)], scale=bet[:, n, h:h + 1])
                    k.ts(vnk[:, h, :], ps_sc[0:64, h, 128:256], bk[:, n, h:h + 1], ALU.mult, [tk("bk")], [B_S[h], tk("vnk%d" % h)])
                for h in range(4):
                    qh = h // 2
                    oc = slice(384 + par * 64, 384 + par * 64 + 64)
                    k.mm(ps_sc[:, h, oc], S_b[:, h, :], qdT[:, h, c], [tk("S_b%d" % h), tk("qdT%d" % h)], [B_S[h]], start=True, stop=False, inc=False)
                    k.mm(ps_sc[:, h, oc], vn[:, h, :], attnT[:, h, c], [tk("vn%d" % h), tk("attnT%d" % h)], [B_S[h]], start=False, stop=True, inc=False)
                    k.mm(ps_sc[:, h, 256:384], ktok[:, qh, n, :], vnk[:, h, :], [tk("ktok%d" % qh), tk("vnk%d" % h)], [B_S[h]])
                for h in range(4):
                    k.stt(S_f[:, h, :], S_f[:, h, :], glast[:, n, h:h + 1], ps_sc[:, h, 256:384], ALU.mult, ALU.add,
                          [tk("glast")], [B_S[h], tk("S_f%d" % h)])
                    k.act(S_b[:, h, :], S_f[:, h, :], AF.Identity, [tk("S_f%d" % h)], [tk("S_b%d" % h)])
                if par == 1:
                    ov = ps_sc[:, :, 384:512]
                    tc0 = (n - 1) * 64
                    k.act(osq[:].rearrange("p (h t) -> p h t", t=128), ov, AF.Square, [], B_S + [tk("osq")])
                    k.mm(ps_bc[:, :], ones_f, osq[:], [tk("cst"), tk("osq")], [B_BC])
                    k.act(otmp[:], ps_bc[:], AF.Ln, [], [B_BC, tk("otmp")], scale=1.0 / 128, bias=EPS)
                    k.act(otmp[:], otmp[:], AF.Exp, [], [tk("otmp")], scale=-0.5)
                    k.tt(otmp[:].rearrange("p (h t) -> p h t", t=128), ov, otmp[:].rearrange("p (h t) -> p h t", t=128), ALU.mult,
                         [], B_S + [tk("otmp")])
                    k.stt(og[:], otmp[:].rearrange("p (h t) -> p h t", t=128), onorm_s[:, 0:1], zs[:, :, tc0:tc0 + 128], ALU.mult, ALU.mult,
                          [tk("otmp"), tk("onorm"), tk("zs%d" % p)], [tk("og")])
                    if fz is None:
                        k.ld(s_o, oT[:, t0 + tc0:t0 + tc0 + 128].rearrange("(h e) t -> e h t", e=128), og[:], [], r=[tk("og")])
                    else:
                        ysl = (t0 + tc0) // 128 % 2
                        for half, (pst, btk) in enumerate(((ps_pj, B_PJ), (ps_ms, B_MS))):
                            for h in range(4):
                                k.mm(pst[:, :], og[:, h, :], w0_b[:, h, half * 512:(half + 1) * 512], [tk("og"), tk("w0b")], [btk],
                                     start=(h == 0), stop=(h == 3), inc=(h == 3))
                        k.act(yst[:, ysl, 0:512], ps_pj[:, :], AF.Identity, [], [B_PJ, tk("yst%d" % ysl)])
                        k.cp(yst[:, ysl, 512:1024], ps_ms[:, :], [], [B_MS, tk("yst%d" % ysl)])
                        pos0 = t0 + tc0 - 48
                        nrows = fz["y0p"].shape[0]
                        if pos0 < 0:
                            k.ld(s_y[ysl], fz["y0p"][0:128 + pos0, :], yst[-pos0:128, ysl, :], [], r=[tk("yst%d" % ysl), tk("y0rows")])
                            rend = 128 + pos0
                        else:
                            nr = min(128, nrows - pos0)
                            rend = pos0 + max(nr, 0)
                            if nr > 0:
                                k.ld(s_y[ysl], fz["y0p"][pos0:pos0 + nr, :], yst[0:nr, ysl, :], [], r=[tk("yst%d" % ysl), tk("y0rows")])
                        while cc_next[0] < nrows and rend >= min(nrows, cc_next[0] + CC_ROWS):
                            r0, r1 = cc_next[0], min(nrows, cc_next[0] + CC_ROWS)
                            P.cc("cc", fz["scc"], fz["y0p"][r0:r1, :], fz["y0f"][r0:r1, :], [], [tk("y0rows")])
                            cc_next[0] = r1
        P.finish("sp", [tk("og")] + ([tk("yst0"), tk("yst1")] if fz is not None else []))
        P.emit()
        return P.final_events()


def gdn_inputs(inp, core, NTOK):
    b, hg = core // 4, core % 4
    L = 16 + inp["x"].shape[1]
    xp = np.zeros((NTOK, 1024), np.float32)
    n_real = min(L, NTOK - 48)
    xp[48:64] = inp["meta_tokens"]
    xp[64:48 + n_real] = inp["x"][b, :n_real - 16]
    W = inp["gdn_w_in"][0]
    qcols = np.arange(2 * hg * 128, (2 * hg + 2) * 128)
    kcols = 1024 + qcols
    vcols = 2048 + np.arange(4 * hg * 128, (4 * hg + 4) * 128)
    zcols = 4096 + np.arange(4 * hg * 128, (4 * hg + 4) * 128)
    bcols = 6144 + np.arange(4 * hg, 4 * hg + 4)
    acols = 6160 + np.arange(4 * hg, 4 * hg + 4)
    wq = np.ascontiguousarray(W[:, np.concatenate([qcols, kcols, vcols, zcols])])
    wba = np.ascontiguousarray(W[:, np.concatenate([bcols, acols])])
    cw = inp["gdn_conv_w"][0][:, np.concatenate([qcols, kcols, vcols])]
    convw = np.ascontiguousarray(cw.reshape(4, 8, 128).transpose(2, 1, 0).reshape(128, 32))
    return {
        "xp": xp, "gpre": inp["pre_norm"][0:1].copy(), "wq": wq, "wba": wba, "convw": convw,
        "alog": inp["gdn_a_log"][0:1, 4 * hg:4 * hg + 4].copy(), "dtb": inp["gdn_dt_bias"][0:1, 4 * hg:4 * hg + 4].copy(),
        "onorm": inp["gdn_out_norm"][0].reshape(128, 1).copy(), "cst": _consts(),
    }


def build_wout(NBLK):
    nc = bass.Bass("TRN2", target_bir_lowering=False)

    def din(n, s, dt=F32):
        return nc.dram_tensor(n, list(s), dt, kind="ExternalInput").ap()

    NT = NBLK * 128
    oTin = din("oTin", [2048, NT], BF16)
    resid = din("resid", [NT, 1024])
    w = din("w", [2048, 1024])
    gpost = din("gpost", [1, 1024])
    out = nc.dram_tensor("out", [NT, 1024], F32, kind="ExternalOutput").ap()
    with contextlib.ExitStack() as st:
        P = Prog(nc, st)
        k = K(P)
        sb, ps = P.sb, P.ps
        wsb = sb("wsb", [128, 16, 1024], BF16)
        wstage = sb("wstage", [128, 2, 1024], F32)
        gp_b = sb("gp_b", [128, 1024], F32)
        oTs = sb("oTs", [128, 2, 16, 128], BF16)
        rs = sb("rs", [128, 2, 1024], F32)
        junk = sb("junk", [128, 512], BF16)
        ss = sb("ss", [128, 2], F32)
        rstd = sb("rstd", [128, 1], F32)
        ot = sb("ot", [128, 2, 1024], F32)
        ps_y = ps("ps_y", [128, 2, 512], F32)
        B_Y = [Tk("B_Y0"), Tk("B_Y1")]
        T = {}

        def tk(n):
            if n not in T:
                T[n] = Tk(n)
            return T[n]
        s_c = P.dmasem("c")
        s_w = [P.dmasem("w0"), P.dmasem("w1")]
        s_i = [P.dmasem("i0"), P.dmasem("i1")]
        s_o = [P.dmasem("o0"), P.dmasem("o1")]
        k.ld(s_c, gp_b[:], gpost[0:1, :].partition_broadcast(128), [tk("gp")])
        for kc in range(16):
            sl = kc % 2
            k.ld(s_w[sl], wstage[:, sl, :], w[kc * 128:(kc + 1) * 128, :], [tk("wst%d" % sl)])
            k.cp(wsb[:, kc, :], wstage[:, sl, :], [tk("wst%d" % sl)], [tk("wsb")], eng="pool")
        for blk in range(NBLK):
            sl = blk % 2
            c0 = blk * 128
            k.ld(s_i[sl], oTs[:, sl, :, :], oTin[:, c0:c0 + 128].rearrange("(k p) t -> p k t", p=128), [tk("in%d" % sl)])
            k.ld(s_i[sl], rs[:, sl, :], resid[c0:c0 + 128, :], [tk("in%d" % sl)])
            for half in range(2):
                for kc in range(16):
                    k.mm(ps_y[:, half, :], oTs[:, sl, kc, :], wsb[:, kc, half * 512:(half + 1) * 512], [tk("in%d" % sl), tk("wsb")], [B_Y[half]],
                         start=(kc == 0), stop=(kc == 15), inc=(kc == 15))
                k.act(junk[:], ps_y[:, half, :], AF.Square, [], [B_Y[half], tk("junk"), tk("ss")], accum_out=ss[:, half:half + 1])
            k.tt(rstd[:], ss[:, 0:1], ss[:, 1:2], ALU.add, [tk("ss")], [tk("rstd")])
            k.act(rstd[:], rstd[:], AF.Sqrt, [], [tk("rstd")], scale=1.0 / 1024, bias=EPS)
            k.rcp(rstd[:], rstd[:], [], [tk("rstd")])
            for half in range(2):
                hs = slice(half * 512, (half + 1) * 512)
                k.stt(ot[:, sl, hs], ps_y[:, half, :], rstd[:, 0:1], gp_b[:, hs], ALU.mult, ALU.mult, [tk("rstd"), tk("gp")], [B_Y[half], tk("ot%d" % sl)])
            k.tt(ot[:, sl, :], ot[:, sl, :], rs[:, sl, :], ALU.add, [tk("in%d" % sl)], [tk("ot%d" % sl)])
            k.ld(s_o[sl], out[c0:c0 + 128, :], ot[:, sl, :], [], r=[tk("ot%d" % sl)])
        P.finish("sp", [tk("ot0"), tk("ot1")])
        P.emit()
    return nc


SCALE = 192.0 ** -0.5


def _consts_mla():
    c = np.zeros((128, 384), np.float32)
    c[:, 0:128] = np.eye(128, dtype=np.float32)
    c[:, 128:256] = 1.0
    kk = np.arange(128)
    c[:, 256:384] = (kk[None, :] >= kk[:, None])
    return c


def mla_decl(nc, NTOK2):
    def din(n, s, dt=F32):
        return nc.dram_tensor(n, list(s), dt, kind="ExternalInput").ap()
    io = {}
    io["g1"] = din("g1", [1, 1024])
    io["gkv"] = din("gkv", [1, 1024])
    io["glat"] = din("glat", [128, 1])
    io["gq"] = din("gq", [1, 256])
    io["wkvd"] = din("wkvd", [1024, 256])
    io["wuk"] = din("wuk", [128, 512])
    io["wuv"] = din("wuv", [128, 512])
    io["wmi"] = din("wmi", [1024, 768])
    io["wqu"] = din("wqu", [256, 1024])
    io["cos2T"] = din("cos2T", [64, NTOK2])
    io["sinsT"] = din("sinsT", [64, NTOK2])
    io["cstm"] = din("cstm", [128, 384])
    return io


def build_mla(NTOK2):
    assert NTOK2 % 128 == 0
    nc = bass.Bass("TRN2", target_bir_lowering=False)
    io = mla_decl(nc, NTOK2)
    io["h1p"] = nc.dram_tensor("h1p", [NTOK2, 1024], F32, kind="ExternalInput").ap()
    io["o1T"] = nc.dram_tensor("o1T", [512, NTOK2], BF16, kind="ExternalOutput").ap()
    emit_mla(nc, None, io, NTOK2)
    return nc


def emit_mla(nc, semst, io, NTOK2, prew=(), fz=None):
    NBK = NTOK2 // 128
    g1, gkv, glat, gq, wkvd, wuk, wuv, wmi, wqu, cos2T, sinsT, cst = (
        io[n] for n in ("g1", "gkv", "glat", "gq", "wkvd", "wuk", "wuv", "wmi", "wqu", "cos2T", "sinsT", "cstm"))
    h1p = io.get("h1p")
    o1T = io.get("o1T")
    with contextlib.ExitStack() as st:
        P = Prog(nc, st, semst, "m", prew)
        k = K(P)
        sb, ps = P.sb, P.ps
        if fz is not None:
            gp0_b = sb("gp0_b", [128, 1024], F32)
            w1_b = sb("w1_b", [128, 4, 1024], BF16)
            ys = sb("ys", [128, 2, 1024], F32)
            ssy = sb("ssy", [128, 1], F32)
            y1st = sb("y1st", [128, 1024], F32)
        ckvT = sb("ckvT", [128, NTOK2], BF16)
        kropeT = sb("kropeT", [128, NTOK2], BF16)
        ckvtok = sb("ckvtok", [128, NBK, 129], BF16)
        wkvd_b = sb("wkvd_b", [128, 8, 256], BF16)
        wuk_b = sb("wuk_b", [128, 4, 128], BF16)
        wukT_b = sb("wukT_b", [128, 4, 128], BF16)
        wuv_b = sb("wuv_b", [128, 4, 128], BF16)
        wmi_b = sb("wmi_b", [128, 8, 768], BF16)
        wqu_b = sb("wqu_b", [128, 2, 1024], BF16)
        wstage = sb("wstage", [128, 2, 1024], F32)
        g1_b = sb("g1_b", [128, 1024], F32)
        gkv_b = sb("gkv_b", [128, 1024], F32)
        gq_b = sb("gq_b", [128, 256], F32)
        glat_s = sb("glat_s", [128, 1], F32)
        cst_s = sb("cst_s", [128, 384], F32)
        ident_b = sb("ident_b", [128, 128], BF16)
        tri_b = sb("tri_b", [128, 128], BF16)
        zb = sb("zb", [128, 512], BF16)
        ones_f = cst_s[:, 128:256]
        hs = sb("hs", [128, 4, 1024], F32)
        junk = sb("junk", [128, 1024], BF16)
        ss = sb("ss", [128, 4], F32)
        rstd = sb("rstd", [128, 4], F32)
        hn1 = sb("hn1", [128, 1024], BF16)
        hkv = sb("hkv", [128, 1024], BF16)
        hn1T = sb("hn1T", [128, 8, 512], BF16)
        hkvT = sb("hkvT", [128, 8, 512], BF16)
        cs = sb("cs", [64, 512], F32)
        sn = sb("sn", [64, 512], F32)
        ckf = sb("ckf", [128, 512], F32)
        sq = sb("sq", [128, 512], F32)
        rt = sb("rt", [128, 512], F32)
        ra = sb("ra", [64, 2, 512], F32)
        rbb = sb("rbb", [64, 2, 512], F32)
        ssq = sb("ssq", [128, 1], F32)
        cqn = sb("cqn", [128, 256], BF16)
        cqT = sb("cqT", [128, 2, 512], BF16)
        zs1 = sb("zs1", [128, 4, 512], F32)
        qnT = sb("qnT", [128, 2, 512], BF16)
        qpT = sb("qpT", [128, 4, 512], BF16)
        qrT = sb("qrT", [128, 4, 512], BF16)
        pT = sb("pT", [128, 3, 512], BF16)
        pacc = sb("pacc", [128, 2, 512], F32)
        rdb = sb("rdb", [128, 512], F32)
        ocn = sb("ocn", [128, 512], BF16)
        og1 = sb("og1", [128, 4, 512], BF16)
        ps_tr = ps("ps_tr", [128, 1024], BF16)
        ps_pj = ps("ps_pj", [128, 512], F32)
        ps_p2 = ps("ps_p2", [128, 512], F32)
        ps_v = ps("ps_v", [128, 512], F32)
        ps_s = ps("ps_s", [128, 3, 512], F32)
        ps_o = ps("ps_o", [128, 512], F32)
        B_TR, B_PJ, B_P2, B_V = Tk("B_TR"), Tk("B_PJ"), Tk("B_P2"), Tk("B_V")
        B_S = [Tk("B_S0"), Tk("B_S1"), Tk("B_S2")]
        B_O = Tk("B_O")
        T = {}

        def tk(n):
            if n not in T:
                T[n] = Tk(n)
            return T[n]

        s_c = P.dmasem("c")
        s_w = [P.dmasem("w0"), P.dmasem("w1")]
        s_x = P.dmasem("x")
        s_o = P.dmasem("o")
        k.ld(s_c, cst_s[:], cst[:, :], [tk("cst")])
        k.ld(s_c, g1_b[:], g1[0:1, :].partition_broadcast(128), [tk("g1")])
        k.ld(s_c, gkv_b[:], gkv[0:1, :].partition_broadcast(128), [tk("gkv")])
        k.ld(s_c, gq_b[:], gq[0:1, :].partition_broadcast(128), [tk("gq")])
        k.ld(s_c, glat_s[:], glat[:, :], [tk("glat")])
        if fz is not None:
            s_yl = [P.dmasem("yl0"), P.dmasem("yl1")]
            s_h = P.dmasem("h")
            k.ld(s_c, gp0_b[:], fz["gp0"][0:1, :].partition_broadcast(128), [tk("gp0")])
            tk("gp0").w = None
        for _n in ("cst", "g1", "gkv", "gq", "glat", "gp0"):
            tk(_n).w = (s_c, P.dcnt[s_c])
        k.cp(ident_b[:], cst_s[:, 0:128], [tk("cst")], [tk("identb")])
        k.cp(tri_b[:], cst_s[:, 256:384], [tk("cst")], [tk("trib")])
        P.op("dve", lambda e: e.memset(zb[:], 0.0), [], [tk("zb")])
        P.op("pool", lambda e: e.memset(ckvtok[:], 1.0), [], [tk("ckvtok")])
        P.op("pool", lambda e: e.memset(kropeT[:], 0.0), [], [tk("kropeT")])
        P.op("pool", lambda e: e.memset(qrT[:], 0.0), [], [tk("qrT%d" % h_) for h_ in range(4)])
        wl = []
        for kc in range(8):
            wl.append((wkvd[kc * 128:(kc + 1) * 128, :], 256, wkvd_b[:, kc, :]))
        wl.append((wuk[:, :], 512, wuk_b[:].rearrange("p h d -> p (h d)")))
        wl.append((wuv[:, :], 512, wuv_b[:].rearrange("p h d -> p (h d)")))
        for kc in range(8):
            wl.append((wmi[kc * 128:(kc + 1) * 128, :], 768, wmi_b[:, kc, :]))
        for c2 in range(2):
            wl.append((wqu[c2 * 128:(c2 + 1) * 128, :], 1024, wqu_b[:, c2, :]))
        if fz is not None:
            for h in range(4):
                wl.append((fz["w1"][h * 128:(h + 1) * 128, :], 1024, w1_b[:, h, :]))
        for i, (src, n, dst) in enumerate(wl):
            sl = i % 2
            k.ld(s_w[sl], wstage[:, sl, 0:n], src, [tk("wst%d" % sl)])
            k.cp(dst, wstage[:, sl, 0:n], [tk("wst%d" % sl)], [tk("wts")], eng="pool")
        for h in range(4):
            k.tr(ps_tr[:, h * 128:(h + 1) * 128], wuk_b[:, h, :], ident_b[:], [tk("wts"), tk("identb")], [B_TR], inc=(h == 3))
        k.act(wukT_b[:].rearrange("p h d -> p (h d)"), ps_tr[:, 0:512], AF.Identity, [], [B_TR, tk("wukT")])

        ntiles = (NTOK2 + 511) // 512
        cc_next = [0]
        for ti in range(ntiles):
            t0 = ti * 512
            TT = min(512, NTOK2 - t0)
            NS = TT // 128
            blk0 = t0 // 128
            hsrc = h1p[t0:t0 + TT, :] if fz is None else fz["xp"][48 + t0:48 + t0 + TT, :]
            k.ld(s_x, hs[:, 0:NS, :], hsrc.rearrange("(s p) d -> p s d", p=128), [tk("hs")])
            k.ld(s_x, cs[:, 0:TT], cos2T[:, t0:t0 + TT], [tk("cs")])
            k.ld(s_x, sn[:, 0:TT], sinsT[:, t0:t0 + TT], [tk("cs")])
            tk("hs").w = (s_x, P.dcnt[s_x])
            tk("cs").w = (s_x, P.dcnt[s_x])
            if fz is not None:
                for s in range(NS):
                    ysl = s % 2
                    ytk = tk("ys%d" % ysl)
                    k.ld(s_yl[ysl], ys[:, ysl, :], fz["y0f"][t0 + s * 128:t0 + (s + 1) * 128, :], [ytk])
                    k.act(junk[:], ys[:, ysl, :], AF.Square, [ytk], [tk("junk"), tk("ssy")], accum_out=ssy[:])
                    k.act(ssy[:], ssy[:], AF.Sqrt, [], [tk("ssy")], scale=1.0 / 1024, bias=EPS)
                    k.rcp(ssy[:], ssy[:], [], [tk("ssy")])
                    k.stt(ys[:, ysl, :], ys[:, ysl, :], ssy[:, 0:1], gp0_b[:], ALU.mult, ALU.mult, [tk("ssy"), tk("gp0")], [ytk])
                    k.tt(hs[:, s, :], hs[:, s, :], ys[:, ysl, :], ALU.add, [ytk], [tk("hs")])
                k.ld(s_h, fz["h1s"][t0:t0 + TT, :].rearrange("(s p) d -> p s d", p=128), hs[:, 0:NS, :], [], r=[tk("hs")], q="act")
            for s in range(NS):
                k.act(junk[:], hs[:, s, :], AF.Square, [tk("hs")], [tk("junk"), tk("ss")], accum_out=ss[:, s:s + 1])
            k.act(rstd[:, 0:NS], ss[:, 0:NS], AF.Sqrt, [tk("ss")], [tk("rstd")], scale=1.0 / 1024, bias=EPS)
            k.rcp(rstd[:, 0:NS], rstd[:, 0:NS], [], [tk("rstd")])
            for s in range(NS):
                for (gb, gt_, dstT, nm) in ((g1_b, "g1", hn1T, "hn1"), (gkv_b, "gkv", hkvT, "hkv")):
                    buf = hn1 if nm == "hn1" else hkv
                    k.stt(buf[:], hs[:, s, :], rstd[:, s:s + 1], gb[:], ALU.mult, ALU.mult, [tk("hs"), tk("rstd"), tk(gt_)], [tk(nm)])
                    for kc in range(8):
                        k.tr(ps_tr[:, kc * 128:(kc + 1) * 128], buf[:, kc * 128:(kc + 1) * 128], ident_b[:], [tk(nm), tk("identb")], [B_TR],
                             inc=(kc == 7))
                    k.act(dstT[:, :, s * 128:(s + 1) * 128], ps_tr[:].rearrange("p (k t) -> p k t", t=128), AF.Identity, [], [B_TR, tk(nm + "T")])
            tsl = slice(t0, t0 + TT)
            for kc in range(8):
                k.mm(ps_pj[:, 0:TT], wkvd_b[:, kc, 0:128], hkvT[:, kc, 0:TT], [tk("wts"), tk("hkvT")], [B_PJ], start=(kc == 0), stop=(kc == 7), inc=(kc == 7))
            k.act(ckf[:, 0:TT], ps_pj[:, 0:TT], AF.Identity, [], [B_PJ, tk("ckf")])
            k.act(sq[:, 0:TT], ckf[:, 0:TT], AF.Square, [tk("ckf")], [tk("sq")])
            k.mm(ps_p2[:, 0:TT], ones_f, sq[:, 0:TT], [tk("cst"), tk("sq")], [B_P2])
            k.act(rt[:, 0:TT], ps_p2[:, 0:TT], AF.Ln, [], [B_P2, tk("rt")], scale=1.0 / 128, bias=EPS)
            k.act(rt[:, 0:TT], rt[:, 0:TT], AF.Exp, [], [tk("rt")], scale=-0.5)
            k.stt(ckvT[:, tsl], ckf[:, 0:TT], glat_s[:, 0:1], rt[:, 0:TT], ALU.mult, ALU.mult, [tk("ckf"), tk("glat"), tk("rt")], [tk("ckvT")])
            for kc in range(8):
                k.mm(ps_pj[0:64, 0:TT], wkvd_b[:, kc, 128:192], hkvT[:, kc, 0:TT], [tk("wts"), tk("hkvT")], [B_PJ], start=(kc == 0), stop=(kc == 7), inc=(kc == 7))
            for kc in range(8):
                k.mm(ps_p2[0:64, 0:TT], wkvd_b[:, kc, 192:256], hkvT[:, kc, 0:TT], [tk("wts"), tk("hkvT")], [B_P2], start=(kc == 0), stop=(kc == 7), inc=(kc == 7))
            k.tt(ra[:, 0, 0:TT], ps_pj[0:64, 0:TT], cs[:, 0:TT], ALU.mult, [tk("cs")], [B_PJ, tk("ra0")])
            k.tt(rbb[:, 0, 0:TT], ps_p2[0:64, 0:TT], sn[:, 0:TT], ALU.mult, [tk("cs")], [B_P2, tk("rbb0")])
            k.tt(kropeT[0:64, tsl], ra[:, 0, 0:TT], rbb[:, 0, 0:TT], ALU.add, [tk("ra0"), tk("rbb0")], [tk("kropeT")])
            for s in range(NS):
                k.tr(ps_tr[:, s * 128:(s + 1) * 128], ckvT[:, t0 + s * 128:t0 + (s + 1) * 128], ident_b[:], [tk("ckvT"), tk("identb")], [B_TR], inc=(s == NS - 1))
            k.act(ckvtok[:, blk0:blk0 + NS, 0:128], ps_tr[:, 0:NS * 128].rearrange("p (s d) -> p s d", d=128), AF.Identity, [], [B_TR, tk("ckvtok")])
            for s in range(NS):
                for kc in range(8):
                    k.mm(ps_v[:, 0:256], hn1T[:, kc, s * 128:(s + 1) * 128], wmi_b[:, kc, 0:256], [tk("hn1T"), tk("wts")], [B_V], start=(kc == 0), stop=(kc == 7), inc=(kc == 7))
                k.act(junk[:, 0:256], ps_v[:, 0:256], AF.Square, [], [B_V, tk("junk"), tk("ssq")], accum_out=ssq[:])
                k.act(ssq[:], ssq[:], AF.Sqrt, [], [tk("ssq")], scale=1.0 / 256, bias=EPS)
                k.rcp(ssq[:], ssq[:], [], [tk("ssq")])
                k.stt(cqn[:], ps_v[:, 0:256], ssq[:, 0:1], gq_b[:], ALU.mult, ALU.mult, [tk("ssq"), tk("gq")], [B_V, tk("cqn")])
                for c2 in range(2):
                    k.tr(ps_tr[:, c2 * 128:(c2 + 1) * 128], cqn[:, c2 * 128:(c2 + 1) * 128], ident_b[:], [tk("cqn"), tk("identb")], [B_TR], inc=(c2 == 1))
                k.act(cqT[:, :, s * 128:(s + 1) * 128], ps_tr[:, 0:256].rearrange("p (c t) -> p c t", t=128), AF.Identity, [], [B_TR, tk("cqT")])
            for h in range(4):
                for kc in range(8):
                    k.mm(ps_pj[:, 0:TT], wmi_b[:, kc, 256 + h * 128:256 + (h + 1) * 128], hn1T[:, kc, 0:TT], [tk("wts"), tk("hn1T")], [B_PJ],
                         start=(kc == 0), stop=(kc == 7), inc=(kc == 7))
                k.act(zs1[:, h, 0:TT], ps_pj[:, 0:TT], AF.Silu, [], [B_PJ, tk("zs1")])
            for hp in range(2):
                hh = (2 * hp, 2 * hp + 1)
                sets = {hh[0]: (ps_pj, B_PJ, ps_p2, B_P2, 0), hh[1]: (ps_v, B_V, ps_o, B_O, 1)}
                for h in hh:
                    pa, ba, pb, bb, u = sets[h]
                    for c2 in range(2):
                        k.mm(pa[:, 0:TT], wqu_b[:, c2, h * 256:h * 256 + 128], cqT[:, c2, 0:TT], [tk("wts"), tk("cqT")], [ba], start=(c2 == 0), stop=(c2 == 1), inc=(c2 == 1))
                for h in hh:
                    pa, ba, pb, bb, u = sets[h]
                    k.act(qnT[:, u, 0:TT], pa[:, 0:TT], AF.Identity, [], [ba, tk("qnT%d" % u)])
                for h in hh:
                    pa, ba, pb, bb, u = sets[h]
                    k.mm(pb[:, 0:TT], wukT_b[:, h, :], qnT[:, u, 0:TT], [tk("wukT"), tk("qnT%d" % u)], [bb])
                for h in hh:
                    pa, ba, pb, bb, u = sets[h]
                    k.act(qpT[:, h, 0:TT], pb[:, 0:TT], AF.Identity, [], [bb, tk("qpT%d" % h)])
                for h in hh:
                    pa, ba, pb, bb, u = sets[h]
                    for c2 in range(2):
                        k.mm(pa[0:64, 0:TT], wqu_b[:, c2, h * 256 + 128:h * 256 + 192], cqT[:, c2, 0:TT], [tk("wts"), tk("cqT")], [ba], start=(c2 == 0), stop=(c2 == 1), inc=(c2 == 1))
                    for c2 in range(2):
                        k.mm(pb[0:64, 0:TT], wqu_b[:, c2, h * 256 + 192:h * 256 + 256], cqT[:, c2, 0:TT], [tk("wts"), tk("cqT")], [bb], start=(c2 == 0), stop=(c2 == 1), inc=(c2 == 1))
                for h in hh:
                    pa, ba, pb, bb, u = sets[h]
                    k.tt(ra[:, u, 0:TT], pa[0:64, 0:TT], cs[:, 0:TT], ALU.mult, [tk("cs")], [ba, tk("ra%d" % u)])
                    k.tt(rbb[:, u, 0:TT], pb[0:64, 0:TT], sn[:, 0:TT], ALU.mult, [tk("cs")], [bb, tk("rbb%d" % u)])
                    k.tt(qrT[0:64, h, 0:TT], ra[:, u, 0:TT], rbb[:, u, 0:TT], ALU.add, [tk("ra%d" % u), tk("rbb%d" % u)], [tk("qrT%d" % h)])
            nkb = blk0 + NS

            def emit_s(h, j):
                jj = j - blk0
                qlo = max(0, jj) * 128
                buf = j % 3
                ksl = slice(j * 128, (j + 1) * 128)
                k.mm(ps_s[:, buf, qlo:TT], ckvT[:, ksl], qpT[:, h, qlo:TT], [tk("ckvT"), tk("qpT%d" % h)], [B_S[buf]], start=True, stop=False, inc=False)
                k.mm(ps_s[:, buf, qlo:TT], kropeT[:, ksl], qrT[:, h, qlo:TT], [tk("kropeT"), tk("qrT%d" % h)], [B_S[buf]], start=False, stop=True)
                k.act(pT[:, buf, qlo:TT], ps_s[:, buf, qlo:TT], AF.Exp, [], [B_S[buf], tk("pT%d" % buf)], scale=SCALE)
                if jj >= 0:
                    k.tt(pT[:, buf, qlo:qlo + 128], pT[:, buf, qlo:qlo + 128], tri_b[:], ALU.mult, [tk("trib")], [tk("pT%d" % buf)])
                if j == 0:
                    k.cp(pacc[:, h % 2, 0:TT], pT[:, buf, 0:TT], [tk("pT%d" % buf)], [tk("pacc%d" % (h % 2))])
                else:
                    k.tt(pacc[:, h % 2, qlo:TT], pacc[:, h % 2, qlo:TT], pT[:, buf, qlo:TT], ALU.add, [tk("pT%d" % buf)], [tk("pacc%d" % (h % 2))])

            def emit_pv(h, j):
                jj = j - blk0
                qlo = max(0, jj) * 128
                buf = j % 3
                k.mm(ps_o[:, qlo:TT], ckvtok[:, j, 0:128], pT[:, buf, qlo:TT], [tk("pT%d" % buf), tk("ckvtok")], [B_O],
                     start=(j == 0), stop=(j == nkb - 1))

            for h in range(4):
                if h == 0:
                    emit_s(h, 0)
                    if nkb > 1:
                        emit_s(h, 1)
                for j in range(nkb):
                    if j + 2 < nkb:
                        emit_s(h, j + 2)
                    emit_pv(h, j)
                if h < 3:
                    emit_s(h + 1, 0)
                    if nkb > 1:
                        emit_s(h + 1, 1)
                k.mm(ps_v[:, 0:TT], ones_f, pacc[:, h % 2, 0:TT], [tk("cst"), tk("pacc%d" % (h % 2))], [B_V])
                k.act(rdb[:, 0:TT], ps_v[:, 0:TT], AF.Ln, [], [B_V, tk("rdb")])
                k.act(rdb[:, 0:TT], rdb[:, 0:TT], AF.Exp, [], [tk("rdb")], scale=-1.0)
                k.tt(ocn[:, 0:TT], ps_o[:, 0:TT], rdb[:, 0:TT], ALU.mult, [tk("rdb")], [B_O, tk("ocn")])
                k.mm(ps_pj[:, 0:TT], wuv_b[:, h, :], ocn[:, 0:TT], [tk("wts"), tk("ocn")], [B_PJ])
                k.tt(og1[:, h, 0:TT], ps_pj[:, 0:TT], zs1[:, h, 0:TT], ALU.mult, [tk("zs1")], [B_PJ, tk("og1")])
            if fz is None:
                k.ld(s_o, o1T[:, t0:t0 + TT].rearrange("(h e) t -> e h t", e=128), og1[:, :, 0:TT], [], r=[tk("og1")])
            else:
                for s in range(NS):
                    for half, (pst, btk) in enumerate(((ps_pj, B_PJ), (ps_p2, B_P2))):
                        for h in range(4):
                            k.mm(pst[:, :], og1[:, h, s * 128:(s + 1) * 128], w1_b[:, h, half * 512:(half + 1) * 512], [tk("og1"), tk("wts")], [btk],
                                 start=(h == 0), stop=(h == 3), inc=(h == 3))
                    k.act(y1st[:, 0:512], ps_pj[:, :], AF.Identity, [], [B_PJ, tk("y1st")])
                    k.cp(y1st[:, 512:1024], ps_p2[:, :], [], [B_P2, tk("y1st")])
                    k.ld(s_o, fz["y1p"][t0 + s * 128:t0 + (s + 1) * 128, :], y1st[:], [], r=[tk("y1st"), tk("y1rows")], q="act")
                rend = t0 + TT
                while cc_next[0] < NTOK2 and rend >= min(NTOK2, cc_next[0] + CC_ROWS):
                    r0, r1 = cc_next[0], min(NTOK2, cc_next[0] + CC_ROWS)
                    P.cc("cc", fz["scc"], fz["y1p"][r0:r1, :], fz["y1f"][r0:r1, :], [], [tk("y1rows")])
                    cc_next[0] = r1
        P.finish("sp", [tk("og1")] + ([tk("y1st"), tk("hs")] if fz is not None else []))
        P.emit()
        return P.final_events()


def rope_tables_T(n):
    inv = (np.float32(10000.0) ** (-(np.arange(0, 64, 2, dtype=np.float32)) / np.float32(64))).astype(np.float32)
    ang = (np.arange(n, dtype=np.float32)[:, None] * inv[None, :]).astype(np.float32)
    cos, sin = np.cos(ang).astype(np.float32), np.sin(ang).astype(np.float32)
    cos2T = np.ascontiguousarray(np.concatenate([cos, cos], 1).T)
    sinsT = np.ascontiguousarray(np.concatenate([-sin, sin], 1).T)
    return cos2T, sinsT


def mla_inputs(inp, core, h1b, NTOK2):
    hg = core % 4
    h1p = None
    if h1b is not None:
        L = h1b.shape[0]
        h1p = np.zeros((NTOK2, 1024), np.float32)
        h1p[:L] = h1b
    kd = inp["kv_w_down"]
    wkvd = np.ascontiguousarray(np.concatenate([kd[:, 0:128], kd[:, 128:192], kd[:, 160:192], kd[:, 128:160]], 1))
    ku = inp["kv_w_up"].reshape(128, 16, 256)[:, 4 * hg:4 * hg + 4]
    wuk = np.ascontiguousarray(ku[:, :, 0:128].reshape(128, 512))
    wuv = np.ascontiguousarray(ku[:, :, 128:256].reshape(128, 512))
    mi = inp["mla_w_in"][0]
    wmi = np.ascontiguousarray(np.concatenate([mi[:, 0:256], mi[:, 256 + 512 * hg:256 + 512 * (hg + 1)]], 1))
    qu = inp["mla_w_q_up"][0].reshape(256, 16, 192)[:, 4 * hg:4 * hg + 4]
    wqu = np.ascontiguousarray(np.concatenate([qu[:, :, 0:128], qu[:, :, 128:192], qu[:, :, 160:192], qu[:, :, 128:160]], 2).reshape(256, 1024))
    cos2T, sinsT = rope_tables_T(NTOK2)
    d = {} if h1p is None else {"h1p": h1p}
    d.update(_mla_rest(inp, wkvd, wuk, wuv, wmi, wqu, cos2T, sinsT))
    return d


def _mla_rest(inp, wkvd, wuk, wuv, wmi, wqu, cos2T, sinsT):
    return {
        "g1": inp["pre_norm"][1:2].copy(), "gkv": inp["kv_norm"].reshape(1, 1024).copy(),
        "glat": inp["kv_latent_norm"].reshape(128, 1).copy(), "gq": inp["mla_q_latent_norm"][0:1].copy(),
        "wkvd": wkvd, "wuk": wuk, "wuv": wuv, "wmi": wmi, "wqu": wqu, "cos2T": cos2T, "sinsT": sinsT, "cstm": _consts_mla(),
    }


def emit_fin(nc, semst, h1s, y1f, gp1, out, NBK, prew=()):
    with contextlib.ExitStack() as st:
        P = Prog(nc, st, semst, "f", prew)
        k = K(P)
        sb = P.sb
        gp_b = sb("gp_b", [128, 1024], F32)
        NB_ = 6
        hb = sb("hb", [128, NB_, 1024], F32)
        yb = sb("yb", [128, NB_, 1024], F32)
        junk = sb("junk", [128, 1024], BF16)
        ss = sb("ss", [128, 1], F32)
        T = {}

        def tk(n):
            if n not in T:
                T[n] = Tk(n)
            return T[n]
        s_c = P.dmasem("c")
        s_i = [P.dmasem("i%d" % i_) for i_ in range(NB_)]
        s_o = [P.dmasem("o%d" % i_) for i_ in range(NB_)]
        k.ld(s_c, gp_b[:], gp1[0:1, :].partition_broadcast(128), [tk("gp")])
        for blk in range(NBK):
            sl = blk % NB_
            rows = slice(blk * 128, (blk + 1) * 128)
            k.ld(s_i[sl], hb[:, sl, :], h1s[rows, :], [tk("hb%d" % sl)])
            k.ld(s_i[sl], yb[:, sl, :], y1f[rows, :], [tk("yb%d" % sl)])
            tk("hb%d" % sl).w = (s_i[sl], P.dcnt[s_i[sl]])
            k.act(junk[:], yb[:, sl, :], AF.Square, [tk("yb%d" % sl)], [tk("junk"), tk("ss")], accum_out=ss[:])
            k.act(ss[:], ss[:], AF.Sqrt, [], [tk("ss")], scale=1.0 / 1024, bias=EPS)
            k.rcp(ss[:], ss[:], [], [tk("ss")])
            k.stt(yb[:, sl, :], yb[:, sl, :], ss[:, 0:1], gp_b[:], ALU.mult, ALU.mult, [tk("ss"), tk("gp")], [tk("yb%d" % sl)])
            k.tt(yb[:, sl, :], yb[:, sl, :], hb[:, sl, :], ALU.add, [tk("hb%d" % sl)], [tk("yb%d" % sl)], eng=("pool" if blk % 3 == 0 else "dve"))
            k.ld(s_o[sl], out[rows, :], yb[:, sl, :], [], r=[tk("yb%d" % sl)], q="act")
        P.finish("sp", [tk("yb%d" % i_) for i_ in range(NB_)])
        P.emit()
        return P.final_events()


GROUPS = [[0, 1, 2, 3], [4, 5, 6, 7]]
CC_ROWS = 1024


def emit_allreduce(nc, ev, src, dst, scc):
    with nc.Block() as block:
        @block.gpsimd
        def _(g):
            for hsem, v in ev:
                g.wait_ge(hsem, v)
            rows = src.ap().shape[0]
            n = 0
            for r0 in range(0, rows, CC_ROWS):
                r1 = min(rows, r0 + CC_ROWS)
                g.collective_compute("AllReduce", ALU.add, replica_groups=GROUPS,
                                     ins=[src.ap()[r0:r1, :]], outs=[dst.ap()[r0:r1, :]]).then_inc(scc)
                n += 1
            g.wait_ge(scc, n)
    rows_ = src.ap().shape[0]
    return (rows_ + CC_ROWS - 1) // CC_ROWS


def build_fused(NTOK, NTOK2):
    nc = bass.Bass("TRN2", target_bir_lowering=False)

    def din(n, s_, dt=F32):
        return nc.dram_tensor(n, list(s_), dt, kind="ExternalInput").ap()
    ioG = gdn_decl(nc, NTOK)
    ioM = mla_decl(nc, NTOK2)
    w0 = din("w0", [512, 1024])
    gp0 = din("gp0", [1, 1024])
    w1 = din("w1", [512, 1024])
    gp1 = din("gp1", [1, 1024])
    out = nc.dram_tensor("out", [NTOK2, 1024], F32, kind="ExternalOutput").ap()
    y0p = nc.dram_tensor("y0p", [NTOK2, 1024], F32)
    y0f = nc.dram_tensor("y0f", [NTOK2, 1024], F32)
    h1s = nc.dram_tensor("h1s", [NTOK2, 1024], F32)
    y1p = nc.dram_tensor("y1p", [NTOK2, 1024], F32)
    y1f = nc.dram_tensor("y1f", [NTOK2, 1024], F32)
    with contextlib.ExitStack() as semst:
        scc0 = semst.enter_context(nc.semaphore("cc0"))
        scc1 = semst.enter_context(nc.semaphore("cc1"))
        ev = emit_gdn(nc, semst, ioG, NTOK, fz=dict(y0p=y0p.ap(), y0f=y0f.ap(), scc=scc0, w0=w0))
        ev = emit_mla(nc, semst, ioM, NTOK2, prew=ev,
                      fz=dict(xp=ioG["xp"], y0f=y0f.ap(), gp0=gp0, h1s=h1s.ap(), y1p=y1p.ap(), y1f=y1f.ap(), scc=scc1, w1=w1))
        emit_fin(nc, semst, h1s.ap(), y1f.ap(), gp1, out, NTOK2 // 128, prew=ev)
    return nc


def fused_inputs(inp, core, NTOK, NTOK2):
    hg = core % 4
    d = gdn_inputs(inp, core, NTOK)
    d.update(mla_inputs(inp, core, None, NTOK2))
    d["w0"] = np.ascontiguousarray(inp["gdn_w_out"][0][hg * 512:(hg + 1) * 512])
    d["w1"] = np.ascontiguousarray(inp["mla_w_out"][0][hg * 512:(hg + 1) * 512])
    d["gp0"] = inp["post_norm"][0:1].copy()
    d["gp1"] = inp["post_norm"][1:2].copy()
    return d


_NC_CACHE = {}


def _get(name, fn, *a):
    key = (name,) + a
    if key not in _NC_CACHE:
        _NC_CACHE[key] = fn(*a)
    return _NC_CACHE[key]


def kernel(**inputs):
    inp = {k_: np.ascontiguousarray(np.asarray(v)) for k_, v in inputs.items()}
    B, SEQ, D = inp["x"].shape
    L = SEQ + 16
    NTOK2 = ((L + 127) // 128) * 128
    NTOK = ((NTOK2 + 48 + 127) // 128) * 128
    cores = list(range(8))
    nc = _get("fused", build_fused, NTOK, NTOK2)
    res = run_bass_kernel_spmd(nc, [fused_inputs(inp, c, NTOK, NTOK2) for c in cores], core_ids=cores).results
    return np.stack([np.asarray(res[4 * b]["out"])[16:L] for b in range(B)], 0).astype(np.float32)
```

```python
import contextlib
import numpy as np
import ml_dtypes
import concourse.bass as bass
import concourse.mybir as mybir
from concourse.bass_utils import run_bass_kernel_spmd

F32 = mybir.dt.float32
BF16 = mybir.dt.bfloat16
AF = mybir.ActivationFunctionType
ALU = mybir.AluOpType
EPS = 1e-6


class Tk:
    __slots__ = ("name", "w", "r")

    def __init__(self, name):
        self.name = name
        self.w = None
        self.r = []


class Prog:
    ENGS = ("pe", "act", "dve", "pool", "sp")

    def __init__(self, nc, stack, semst=None, pfx="", prew=()):
        self.nc = nc
        self.stack = stack
        self.semst = semst if semst is not None else stack
        self.pfx = pfx
        self.prew = list(prew)
        self.ops = {e: [] for e in self.ENGS}
        self.cnt = {e: 0 for e in self.ENGS}
        self.known = {e: {} for e in self.ENGS}
        self.sems = {}
        self.dcnt = {}
        for e in self.ENGS:
            self.sems[e] = self.semst.enter_context(nc.semaphore(pfx + "s_" + e))

    def final_events(self):
        ev = [(self.sems[e], self.cnt[e]) for e in self.ENGS if self.cnt[e] > 0]
        ev += [(self.sems[n], c) for n, c in self.dcnt.items() if c > 0]
        return ev

    def sb(self, name, shape, dt):
        return self.stack.enter_context(self.nc.sbuf_tensor(self.pfx + name, list(shape), dt))

    def ps(self, name, shape, dt):
        return self.stack.enter_context(self.nc.psum_tensor(self.pfx + name, list(shape), dt))

    def dmasem(self, name):
        self.sems[name] = self.semst.enter_context(self.nc.semaphore(self.pfx + "d_" + name))
        self.dcnt[name] = 0
        return name

    def _need(self, eng, ev, waits):
        if ev is None:
            return
        key, val = ev
        if key == eng and eng == "pe":
            return
        if self.known[eng].get(key, 0) >= val:
            return
        if key in self.ENGS and key != eng:
            assert self.cnt[key] >= val, (eng, ev, self.cnt[key])
        self.known[eng][key] = val
        for i, (k, v) in enumerate(waits):
            if k == key:
                waits[i] = (k, max(v, val))
                return
        waits.append((key, val))

    def _deps(self, eng, reads, writes):
        waits = []
        for t in reads:
            self._need(eng, t.w, waits)
        for t in writes:
            self._need(eng, t.w, waits)
            for ev in t.r:
                self._need(eng, ev, waits)
        return waits

    def _mark(self, ev, reads, writes):
        for t in writes:
            t.w = ev
            t.r = []
        for t in reads:
            if t not in writes:
                t.r.append(ev)
                if len(t.r) > 8:
                    d = {}
                    for k, v in t.r:
                        d[k] = max(d.get(k, 0), v)
                    t.r = list(d.items())

    def op(self, eng, fn, reads=(), writes=(), inc=True):
        waits = self._deps(eng, reads, writes)
        if inc:
            self.cnt[eng] += 1
            ev = (eng, self.cnt[eng])
        else:
            ev = (eng, self.cnt[eng] + 1)
        self._mark(ev, reads, writes)
        self.ops[eng].append((fn, waits, ("c", inc)))

    def dma(self, q, sem, fn, reads=(), writes=()):
        waits = self._deps(q, reads, writes)
        self.dcnt[sem] += 16
        ev = (sem, self.dcnt[sem])
        self._mark(ev, reads, writes)
        self.ops[q].append((fn, waits, ("d", sem)))

    def cc(self, semname, hsem, src, dst, reads=(), writes=()):
        if semname not in self.sems:
            self.sems[semname] = hsem
            self.dcnt[semname] = 0
        waits = self._deps("pool", reads, writes)
        self.dcnt[semname] += 1
        ev = (semname, self.dcnt[semname])
        self._mark(ev, reads, writes)
        fn = lambda e: e.collective_compute("AllReduce", ALU.add, replica_groups=GROUPS, ins=[src], outs=[dst])
        self.ops["pool"].append((fn, waits, ("k", semname)))

    def finish(self, eng, tks):
        waits = []
        for t in tks:
            self._need(eng, t.w, waits)
            for ev in t.r:
                self._need(eng, ev, waits)
        self.ops[eng].append((None, waits, ("w", None)))

    def emit(self):
        nc, sems, ops = self.nc, self.sems, self.ops
        prew = self.prew
        with nc.Block() as block:
            def run(name, e):
                for hsem, v in prew:
                    e.wait_ge(hsem, v)
                for fn, waits, kind in ops[name]:
                    for k, v in waits:
                        e.wait_ge(sems[k], v)
                    if fn is None:
                        continue
                    ins = fn(e)
                    if kind[0] == "c":
                        if kind[1]:
                            ins.then_inc(sems[name], 1)
                    elif kind[0] == "k":
                        ins.then_inc(sems[kind[1]], 1)
                    else:
                        ins.then_inc(sems[kind[1]], 16)

            @block.tensor
            def _(e):
                run("pe", e)

            @block.scalar
            def _(e):
                run("act", e)

            @block.vector
            def _(e):
                run("dve", e)

            @block.gpsimd
            def _(e):
                run("pool", e)

            @block.sync
            def _(e):
                run("sp", e)


class K:
    def __init__(self, P):
        self.P = P

    def act(self, out, in_, func, r, w, **kw):
        self.P.op("act", lambda e: e.activation(out=out, in_=in_, func=func, **kw), r, w)

    def tt(self, out, in0, in1, op, r, w, eng="dve"):
        self.P.op(eng, lambda e: e.tensor_tensor(out=out, in0=in0, in1=in1, op=op), r, w)

    def ts(self, out, in0, s1, op0, r, w, s2=None, op1=None, eng="dve"):
        if op1 is None:
            self.P.op(eng, lambda e: e.tensor_scalar(out=out, in0=in0, scalar1=s1, scalar2=None, op0=op0), r, w)
        else:
            self.P.op(eng, lambda e: e.tensor_scalar(out=out, in0=in0, scalar1=s1, scalar2=s2, op0=op0, op1=op1), r, w)

    def stt(self, out, in0, scalar, in1, op0, op1, r, w, eng="dve"):
        self.P.op(eng, lambda e: e.scalar_tensor_tensor(out=out, in0=in0, scalar=scalar, in1=in1, op0=op0, op1=op1), r, w)

    def cp(self, out, in_, r, w, eng="dve"):
        self.P.op(eng, lambda e: e.tensor_copy(out=out, in_=in_), r, w)

    def rcp(self, out, in_, r, w):
        self.P.op("dve", lambda e: e.reciprocal(out=out, in_=in_), r, w)

    def mm(self, out, lhsT, rhs, r, w, start=True, stop=True, inc=True, sgc=False):
        self.P.op("pe", lambda e: e.matmul(out, lhsT=lhsT, rhs=rhs, start=start, stop=stop, skip_group_check=sgc), r, w, inc=inc)

    def tr(self, out, in_, ident, r, w, inc=True):
        self.P.op("pe", lambda e: e.transpose(out, in_, ident), r, w, inc=inc)

    def ld(self, sem, out, in_, w, r=(), q="sp"):
        self.P.dma(q, sem, lambda e: e.dma_start(out=out, in_=in_), r, w)


def _consts():
    c = np.zeros((128, 128 * 2 + 64 * 3), np.float32)
    c[:, 0:128] = np.eye(128, dtype=np.float32)
    c[:, 128:256] = 1.0
    kk = np.arange(64)
    c[0:64, 256:320] = (kk[:, None] <= kk[None, :])
    c[0:64, 320:384] = (kk[None, :] >= kk[:, None])
    c[0:64, 384:448] = (kk[None, :] > kk[:, None])
    return c


def gdn_decl(nc, NTOK):
    def din(n, s, dt=F32):
        return nc.dram_tensor(n, list(s), dt, kind="ExternalInput").ap()
    io = {}
    io["xp"] = din("xp", [NTOK, 1024])
    io["gpre"] = din("gpre", [1, 1024])
    io["wq"] = din("wq", [1024, 1536])
    io["wba"] = din("wba", [1024, 8])
    io["convw"] = din("convw", [128, 32])
    io["alog"] = din("alog", [1, 4])
    io["dtb"] = din("dtb", [1, 4])
    io["onorm"] = din("onorm", [128, 1])
    io["cst"] = din("cst", [128, 448])
    return io


def build_gdn(NTOK):
    assert NTOK % 128 == 0
    nc = bass.Bass("TRN2", target_bir_lowering=False)
    io = gdn_decl(nc, NTOK)
    io["oT"] = nc.dram_tensor("oT", [512, NTOK], BF16, kind="ExternalOutput").ap()
    emit_gdn(nc, None, io, NTOK)
    return nc


def emit_gdn(nc, semst, io, NTOK, prew=(), fz=None):
    xp, gpre, wq, wba, convw, alog, dtb, onorm, cst = (io[n] for n in ("xp", "gpre", "wq", "wba", "convw", "alog", "dtb", "onorm", "cst"))
    oT = io.get("oT")
    with contextlib.ExitStack() as st:
        P = Prog(nc, st, semst, "g", prew)
        k = K(P)
        sb, ps = P.sb, P.ps
        if fz is not None:
            w0_b = sb("w0_b", [128, 4, 1024], BF16)
            yst = sb("yst", [128, 2, 1024], F32)
        wb = sb("wb", [128, 8, 1536], BF16)
        wbab = sb("wbab", [128, 8, 8], BF16)
        gpre_b = sb("gpre_b", [128, 1024], F32)
        convw_s = sb("convw_s", [128, 32], F32)
        alog_b = sb("alog_b", [64, 4], F32)
        dtb_b = sb("dtb_b", [64, 4], F32)
        negA = sb("negA", [64, 4], F32)
        onorm_s = sb("onorm_s", [128, 1], F32)
        cst_s = sb("cst_s", [128, 448], F32)
        ident_b = sb("ident_b", [128, 128], BF16)
        xc = sb("xc", [128, 8, 3 + 512], F32)
        S_f = sb("S_f", [128, 4, 128], F32)
        S_b = sb("S_b", [128, 4, 128], BF16)
        ident_f = cst_s[:, 0:128]
        ones_f = cst_s[:, 128:256]
        tri = cst_s[0:64, 256:320]
        mincl = cst_s[0:64, 320:384]
        mstrict = cst_s[0:64, 384:448]
        xs = sb("xs", [128, 4, 1024], F32)
        wstage = xs[:].rearrange("p (a b) d -> p a (b d)", a=2)
        junk = sb("junk", [128, 1024], BF16)
        ss = sb("ss", [128, 4], F32)
        rstd = sb("rstd", [128, 4], F32)
        hn = sb("hn", [128, 2, 1024], BF16)
        hnT = sb("hnT", [128, 8, 512], BF16)
        acc = sb("acc", [128, 512], F32)
        qkf = sb("qkf", [128, 512], F32)
        sq = sb("sq", [128, 512], F32)
        rtmp = sb("rtmp", [128, 512], F32)
        qT = sb("qT", [128, 2, 512], BF16)
        kT2 = sb("kT2", [128, 2, 2, 512], BF16)
        vT = sb("vT", [128, 4, 512], BF16)
        zs2 = sb("zs2", [128, 2, 4, 512], F32)
        bet = sb("bet", [64, 8, 4], F32)
        gt = sb("gt", [64, 8, 4], F32)
        gg = sb("gg", [64, 8, 4], F32)
        gc = sb("gc", [64, 8, 4], F32)
        kap = sb("kap", [64, 8, 4], F32)
        ngam = sb("ngam", [64, 8, 4], F32)
        bk = sb("bk", [64, 8, 4], F32)
        glast = sb("glast", [128, 8, 4], F32)
        gB = sb("gB", [64, 8, 128], F32)
        egc = sb("egc", [128, 512], F32)
        qdT = sb("qdT", [128, 4, 512], BF16)
        attnT = sb("attnT", [64, 4, 512], BF16)
        U = sb("U", [64, 4, 512], F32)
        UT = sb("UT", [64, 4, 2, 512], F32)
        Rr = sb("Rr", [64, 4, 512], F32)
        Rb = sb("Rb", [64, 4, 512], BF16)
        ktok = sb("ktok", [64, 2, 8, 128], BF16)
        vtok = sb("vtok", [64, 4, 8, 128], BF16)
        rr = sb("rr", [64, 4, 128], BF16)
        vn = sb("vn", [64, 4, 128], BF16)
        vnk = sb("vnk", [64, 4, 128], BF16)
        osq = sb("osq", [128, 512], F32)
        otmp = sb("otmp", [128, 512], F32)
        og = sb("og", [128, 4, 128], BF16)
        ps_tr = ps("ps_tr", [128, 1024], BF16)
        ps_pj = ps("ps_pj", [128, 512], F32)
        ps_ms = ps("ps_ms", [128, 512], F32)
        ps_bc = ps("ps_bc", [128, 512], F32)
        ps_sc = ps("ps_sc", [128, 4, 512], F32)
        B_TR, B_PJ, B_MS, B_BC = Tk("B_TR"), Tk("B_PJ"), Tk("B_MS"), Tk("B_BC")
        B_S = [Tk("B_S%d" % h) for h in range(4)]

        T = {}

        def tk(n):
            if n not in T:
                T[n] = Tk(n)
            return T[n]

        s_c = P.dmasem("c")
        s_w = [P.dmasem("w0"), P.dmasem("w1")]
        s_x = P.dmasem("x")
        s_o = P.dmasem("o")
        s_y = [P.dmasem("y0"), P.dmasem("y1")]
        k.ld(s_c, cst_s[:], cst[:, :], [tk("cst")])
        k.ld(s_c, gpre_b[:], gpre[0:1, :].partition_broadcast(128), [tk("gpre")])
        k.ld(s_c, convw_s[:], convw[:, :], [tk("convw")])
        k.ld(s_c, alog_b[:], alog[0:1, :].partition_broadcast(64), [tk("alog")])
        k.ld(s_c, dtb_b[:], dtb[0:1, :].partition_broadcast(64), [tk("dtb")])
        k.ld(s_c, onorm_s[:], onorm[:, :], [tk("onorm")])
        for _n in ("cst", "gpre", "convw", "alog", "dtb", "onorm"):
            tk(_n).w = (s_c, P.dcnt[s_c])
        k.cp(ident_b[:], ident_f, [tk("cst")], [tk("identb")])
        k.act(negA[:], alog_b[:], AF.Exp, [tk("alog")], [tk("negA")])
        k.ts(negA[:], negA[:], -1.0, ALU.mult, [], [tk("negA")])
        for kc in range(8):
            sl = kc % 2
            k.ld(s_w[sl], wstage[:, sl, 0:1536], wq[kc * 128:(kc + 1) * 128, :], [tk("wst%d" % sl)])
            k.ld(s_w[sl], wstage[:, sl, 1536:1544], wba[kc * 128:(kc + 1) * 128, :], [tk("wst%d" % sl)])
            k.cp(wb[:, kc, :], wstage[:, sl, 0:1536], [tk("wst%d" % sl)], [tk("wb")], eng="pool")
            k.cp(wbab[:, kc, :], wstage[:, sl, 1536:1544], [tk("wst%d" % sl)], [tk("wb")], eng="pool")
        if fz is not None:
            for h in range(4):
                sl = h % 2
                k.ld(s_w[sl], wstage[:, sl, 0:1024], fz["w0"][h * 128:(h + 1) * 128, :], [tk("wst%d" % sl)])
                k.cp(w0_b[:, h, :], wstage[:, sl, 0:1024], [tk("wst%d" % sl)], [tk("w0b")], eng="pool")
        P.op("dve", lambda e: e.memset(S_f[:], 0.0), [], [tk("S_f0"), tk("S_f1"), tk("S_f2"), tk("S_f3")])
        P.op("dve", lambda e: e.memset(S_b[:], 0.0), [], [tk("S_b0"), tk("S_b1"), tk("S_b2"), tk("S_b3")])
        P.op("dve", lambda e: e.memset(xc[:], 0.0), [], [tk("xc%d" % g) for g in range(8)])

        def gen_AB(ti):
            t0 = ti * 512
            TT = min(512, NTOK - t0)
            NS = TT // 128
            p = ti % 2
            kT = kT2[:, p]
            zs = zs2[:, p]
            kTn = "kT%d_" % p + "%d"
            for s in range(NS):
                k.act(junk[:], xs[:, s, :], AF.Square, [tk("xs")], [tk("junk"), tk("ss")], accum_out=ss[:, s:s + 1])
            k.act(rstd[:, 0:NS], ss[:, 0:NS], AF.Sqrt, [tk("ss")], [tk("rstd")], scale=1.0 / 1024, bias=EPS)
            k.rcp(rstd[:, 0:NS], rstd[:, 0:NS], [], [tk("rstd")])
            yield
            for s in range(NS):
                k.stt(hn[:, s % 2, :], xs[:, s, :], rstd[:, s:s + 1], gpre_b[:], ALU.mult, ALU.mult,
                      [tk("xs"), tk("rstd"), tk("gpre")], [tk("hn%d" % (s % 2))])
                for kc in range(8):
                    k.tr(ps_tr[:, kc * 128:(kc + 1) * 128], hn[:, s % 2, kc * 128:(kc + 1) * 128], ident_b[:],
                         [tk("hn%d" % (s % 2)), tk("identb")], [B_TR], inc=(kc == 7))
                k.act(hnT[:, :, s * 128:(s + 1) * 128], ps_tr[:].rearrange("p (k t) -> p k t", t=128), AF.Identity,
                      [], [B_TR, tk("hnT")])
                yield
            for g in range(12):
                for kc in range(8):
                    k.mm(ps_pj[:, 0:TT], wb[:, kc, g * 128:(g + 1) * 128], hnT[:, kc, 0:TT], [tk("wb"), tk("hnT")], [B_PJ],
                         start=(kc == 0), stop=(kc == 7), inc=(kc == 7))
                if g < 8:
                    xg = tk("xc%d" % g)
                    k.cp(xc[:, g, 0:3], xc[:, g, 512:515], [], [xg])
                    k.act(xc[:, g, 3:3 + TT], ps_pj[:, 0:TT], AF.Identity, [], [B_PJ, xg])
                    k.ts(acc[:, 0:TT], xc[:, g, 3:3 + TT], convw_s[:, g * 4 + 3:g * 4 + 4], ALU.mult, [xg, tk("convw")], [tk("acc")])
                    for j in (2, 1, 0):
                        k.stt(acc[:, 0:TT], xc[:, g, j:j + TT], convw_s[:, g * 4 + j:g * 4 + j + 1], acc[:, 0:TT], ALU.mult, ALU.add,
                              [xg, tk("convw")], [tk("acc")])
                    if TT < 512:
                        pass
                    if g < 4:
                        dst = qT[:, g, 0:TT] if g < 2 else kT[:, g - 2, 0:TT]
                        dtk = tk("qT%d" % g) if g < 2 else tk(kTn % (g - 2))
                        k.act(qkf[:, 0:TT], acc[:, 0:TT], AF.Silu, [tk("acc")], [tk("qkf")])
                        k.act(sq[:, 0:TT], qkf[:, 0:TT], AF.Square, [tk("qkf")], [tk("sq")])
                        k.mm(ps_bc[:, 0:TT], ones_f, sq[:, 0:TT], [tk("cst"), tk("sq")], [B_BC])
                        if g < 2:
                            k.act(rtmp[:, 0:TT], ps_bc[:, 0:TT], AF.Ln, [], [B_BC, tk("rtmp")], scale=128.0, bias=128.0 * EPS)
                        else:
                            k.act(rtmp[:, 0:TT], ps_bc[:, 0:TT], AF.Ln, [], [B_BC, tk("rtmp")], scale=1.0, bias=EPS)
                        k.act(rtmp[:, 0:TT], rtmp[:, 0:TT], AF.Exp, [], [tk("rtmp")], scale=-0.5)
                        k.tt(dst, qkf[:, 0:TT], rtmp[:, 0:TT], ALU.mult, [tk("qkf"), tk("rtmp")], [dtk])
                    else:
                        k.act(vT[:, g - 4, 0:TT], acc[:, 0:TT], AF.Silu, [tk("acc")], [tk("vT%d" % (g - 4))])
                else:
                    k.act(zs[:, g - 8, 0:TT], ps_pj[:, 0:TT], AF.Silu, [], [B_PJ, tk("zs%d" % p)])
                yield

        def step(g_, n=1):
            if g_ is None:
                return
            for _ in range(n):
                try:
                    next(g_)
                except StopIteration:
                    return

        def drain(g_):
            if g_ is None:
                return
            for _ in g_:
                pass

        def load_x(tj):
            t0_ = tj * 512
            TT_ = min(512, NTOK - t0_)
            k.ld(s_x, xs[:, 0:TT_ // 128, :], xp[t0_:t0_ + TT_, :].rearrange("(s p) d -> p s d", p=128),
                 [tk("xs")] + ([tk("wst0"), tk("wst1")] if tj == 0 else []))

        ntiles = (NTOK + 511) // 512
        cc_next = [0]
        load_x(0)
        drain(gen_AB(0))
        for ti in range(ntiles):
            t0 = ti * 512
            TT = min(512, NTOK - t0)
            NS = TT // 128
            NCH = TT // 64
            if ti + 1 < ntiles:
                load_x(ti + 1)
            p = ti % 2
            kT = kT2[:, p]
            zs = zs2[:, p]
            kTn = "kT%d_" % p + "%d"
            g_next = gen_AB(ti + 1) if ti + 1 < ntiles else None
            for n in range(NCH):
                for kc in range(8):
                    k.mm(ps_ms[0:64, n * 8:(n + 1) * 8], hnT[:, kc, n * 64:(n + 1) * 64], wbab[:, kc, :], [tk("hnT"), tk("wb")], [B_MS],
                         start=(kc == 0), stop=(kc == 7), inc=(kc == 7 and n == NCH - 1))
            bav = ps_ms[0:64, 0:NCH * 8].rearrange("p (n c) -> p n c", c=8)
            k.act(bet[:, 0:NCH, :], bav[:, :, 0:4], AF.Sigmoid, [], [B_MS, tk("bet")])
            k.tt(gt[:, 0:NCH, :], bav[:, :, 4:8], dtb_b[:].unsqueeze(1).to_broadcast([64, NCH, 4]), ALU.add, [tk("dtb")], [B_MS, tk("gt")])
            k.act(gt[:, 0:NCH, :], gt[:, 0:NCH, :], AF.Exp, [], [tk("gt")])
            k.act(gt[:, 0:NCH, :], gt[:, 0:NCH, :], AF.Ln, [], [tk("gt")], bias=1.0)
            k.tt(gg[:, 0:NCH, :], gt[:, 0:NCH, :], negA[:].unsqueeze(1).to_broadcast([64, NCH, 4]), ALU.mult, [tk("gt"), tk("negA")], [tk("gg")])
            ggf = gg[:, 0:NCH, :].rearrange("p n h -> p (n h)")
            k.mm(ps_ms[0:64, 64:64 + NCH * 4], tri, ggf, [tk("cst"), tk("gg")], [B_MS])
            k.mm(ps_ms[:, 128:128 + NCH * 4], ones_f[0:64, :], ggf, [tk("cst"), tk("gg")], [B_MS])
            gcv = ps_ms[0:64, 64:64 + NCH * 4].rearrange("p (n h) -> p n h", h=4)
            glv = ps_ms[:, 128:128 + NCH * 4].rearrange("p (n h) -> p n h", h=4)
            k.cp(gc[:, 0:NCH, :], gcv, [], [B_MS, tk("gc")])
            k.act(glast[:, 0:NCH, :], glv, AF.Exp, [], [B_MS, tk("glast")])
            k.tt(kap[:, 0:NCH, :], glv[0:64], gc[:, 0:NCH, :], ALU.subtract, [tk("gc")], [B_MS, tk("kap")])
            k.act(kap[:, 0:NCH, :], kap[:, 0:NCH, :], AF.Exp, [], [tk("kap")])
            k.act(ngam[:, 0:NCH, :], gc[:, 0:NCH, :], AF.Exp, [tk("gc")], [tk("ngam")])
            k.ts(ngam[:, 0:NCH, :], ngam[:, 0:NCH, :], -1.0, ALU.mult, [], [tk("ngam")])
            k.tt(bk[:, 0:NCH, :], bet[:, 0:NCH, :], kap[:, 0:NCH, :], ALU.mult, [tk("bet"), tk("kap")], [tk("bk")])
            for j in range(6):
                src = kT[:, j, :] if j < 2 else vT[:, j - 2, :]
                stk = tk(kTn % j) if j < 2 else tk("vT%d" % (j - 2))
                for n in range(NCH):
                    k.tr(ps_tr[0:64, n * 128:(n + 1) * 128], src[:, n * 64:(n + 1) * 64], ident_b[:], [stk, tk("identb")], [B_TR],
                         inc=(n == NCH - 1))
                dst = ktok[:, j, 0:NCH, :] if j < 2 else vtok[:, j - 2, 0:NCH, :]
                dtk = tk("ktok%d" % j) if j < 2 else tk("vtok%d" % (j - 2))
                k.act(dst, ps_tr[0:64, 0:NCH * 128].rearrange("p (n d) -> p n d", d=128), AF.Identity, [], [B_TR, dtk])
            W = NCH * 64
            v3 = lambda ap_: ap_.rearrange("p (n c) -> p n c", c=64)
            for h in range(4):
                k.cp(gB[:, 0:NCH, :], gg[:, 0:NCH, h:h + 1].to_broadcast([64, NCH, 128]), [tk("gg")], [tk("gB")])
                for n in range(NCH):
                    k.mm(ps_sc[:, h, n * 64:(n + 1) * 64], gB[:, n, :], tri, [tk("gB"), tk("cst")], [B_S[h]], inc=(n == NCH - 1))
            for h in range(4):
                qh = h // 2
                t1h = UT[:, h, 1, 0:W]
                k.act(egc[:, 0:W], ps_sc[:, h, 0:W], AF.Exp, [], [B_S[h], tk("egc")])
                k.tt(qdT[:, h, 0:W], qT[:, qh, 0:W], egc[:, 0:W], ALU.mult, [tk("qT%d" % qh), tk("egc")], [tk("qdT%d" % h)])
                k.tt(v3(t1h), v3(ps_sc[0:64, h, 0:W]), gc[:, 0:NCH, h:h + 1].to_broadcast([64, NCH, 64]),
                     ALU.subtract, [tk("gc")], [B_S[h], tk("UT%d_1" % h)])
            for h in range(4):
                t1h, Dmh, Bsh = UT[:, h, 1, 0:W], Rr[:, h, 0:W], UT[:, h, 0, 0:W]
                k.ts(t1h, t1h, 0.0, ALU.min, [], [tk("UT%d_1" % h)])
                k.act(t1h, t1h, AF.Exp, [], [tk("UT%d_1" % h)])
                k.tt(v3(Dmh), v3(t1h), mincl.unsqueeze(1).to_broadcast([64, NCH, 64]), ALU.mult, [tk("UT%d_1" % h), tk("cst")], [tk("R%d" % h)])
                k.tt(v3(Bsh), mstrict.unsqueeze(1).to_broadcast([64, NCH, 64]), bet[:, 0:NCH, h:h + 1].to_broadcast([64, NCH, 64]), ALU.mult,
                     [tk("cst"), tk("bet")], [tk("UT%d_0" % h)])
            for h in range(4):
                qh = h // 2
                for n in range(NCH):
                    k.mm(ps_sc[0:64, h, n * 64:(n + 1) * 64], kT[:, qh, n * 64:(n + 1) * 64], qT[:, qh, n * 64:(n + 1) * 64],
                         [tk(kTn % qh), tk("qT%d" % qh)], [B_S[h]], inc=(n == NCH - 1))
                k.tt(attnT[:, h, 0:W], ps_sc[0:64, h, 0:W], Rr[:, h, 0:W], ALU.mult, [tk("R%d" % h)], [B_S[h], tk("attnT%d" % h)])
            for h in range(4):
                qh = h // 2
                for n in range(NCH):
                    k.mm(ps_sc[0:64, h, n * 64:(n + 1) * 64], kT[:, qh, n * 64:(n + 1) * 64], kT[:, qh, n * 64:(n + 1) * 64],
                         [tk(kTn % qh)], [B_S[h]], inc=(n == NCH - 1))
                k.tt(U[:, h, 0:W], ps_sc[0:64, h, 0:W], Rr[:, h, 0:W], ALU.mult, [tk("R%d" % h)], [B_S[h], tk("U%d" % h)])
                k.tt(U[:, h, 0:W], U[:, h, 0:W], UT[:, h, 0, 0:W], ALU.mult, [tk("UT%d_0" % h)], [tk("U%d" % h)])
            for h in range(4):
                for n in range(NCH):
                    k.tr(ps_sc[0:64, h, n * 64:(n + 1) * 64], U[:, h, n * 64:(n + 1) * 64], ident_f[0:64, 0:64], [tk("U%d" % h), tk("cst")], [B_S[h]],
                         inc=(n == NCH - 1))
                k.cp(UT[:, h, 0, 0:W], ps_sc[0:64, h, 0:W], [], [B_S[h], tk("UT%d_0" % h)])
            for h in range(4):
                k.stt(v3(Rr[:, h, 0:W]), v3(U[:, h, 0:W]), -1.0,
                      ident_f[0:64, 0:64].unsqueeze(1).to_broadcast([64, NCH, 64]), ALU.mult, ALU.add, [tk("U%d" % h), tk("cst")], [tk("R%d" % h)])
            W = NCH * 64
            cur = 0
            for lvl in range(1, 6):
                nxt = 1 - cur
                last = (lvl == 5)
                for h in range(4):
                    for n in range(NCH):
                        c = slice(n * 64, (n + 1) * 64)
                        k.mm(ps_sc[0:64, h, c], U[:, h, c], UT[:, h, cur, c], [tk("U%d" % h), tk("UT%d_%d" % (h, cur))], [B_S[h]], inc=(n == NCH - 1))
                    k.cp(UT[:, h, nxt, 0:W], ps_sc[0:64, h, 0:W], [], [B_S[h], tk("UT%d_%d" % (h, nxt))])
                step(g_next)
                if not last:
                    for h in range(4):
                        for n in range(NCH):
                            c = slice(n * 64, (n + 1) * 64)
                            k.mm(ps_sc[0:64, h, c], UT[:, h, cur, c], U[:, h, c], [tk("U%d" % h), tk("UT%d_%d" % (h, cur))], [B_S[h]], inc=(n == NCH - 1))
                        k.act(U[:, h, 0:W], ps_sc[0:64, h, 0:W], AF.Identity, [], [B_S[h], tk("U%d" % h)])
                    step(g_next)
                for h in range(4):
                    for n in range(NCH):
                        c = slice(n * 64, (n + 1) * 64)
                        k.mm(ps_sc[0:64, h, c], UT[:, h, nxt, c], Rr[:, h, c], [tk("UT%d_%d" % (h, nxt)), tk("R%d" % h)], [B_S[h]], inc=(n == NCH - 1))
                    if not last:
                        k.tt(Rr[:, h, 0:W], ps_sc[0:64, h, 0:W], Rr[:, h, 0:W], ALU.add, [], [B_S[h], tk("R%d" % h)])
                    else:
                        k.tt(Rb[:, h, 0:W], ps_sc[0:64, h, 0:W], Rr[:, h, 0:W], ALU.add, [tk("R%d" % h)], [B_S[h], tk("Rb%d" % h)])
                cur = nxt
                step(g_next)
            drain(g_next)
            for n in range(NCH):
                c = slice(n * 64, (n + 1) * 64)
                par = n % 2
                for h in range(4):
                    qh = h // 2
                    k.mm(ps_sc[0:64, h, 0:128], kT[:, qh, c], S_b[:, h, :], [tk(kTn % qh), tk("S_b%d" % h)], [B_S[h]])
                for h in range(4):
                    k.stt(rr[:, h, :], ps_sc[0:64, h, 0:128], ngam[:, n, h:h + 1], vtok[:, h, n, :], ALU.mult, ALU.add,
                          [tk("ngam"), tk("vtok%d" % h)], [B_S[h], tk("rr%d" % h)])
                for h in range(4):
                    k.mm(ps_sc[0:64, h, 128:256], Rb[:, h, c], rr[:, h, :], [tk("Rb%d" % h), tk("rr%d" % h)], [B_S[h]])
                for h in range(4):
                    k.act(vn[:, h, :], ps_sc[0:64, h, 128:256], AF.Identity, [tk("bet")], [B_S[h], tk("vn%d" % h)], scale=bet[:, n, h:h + 1])
                    k.ts(vnk[:, h, :], ps_sc[0:64, h, 128:256], bk[:, n, h:h + 1], ALU.mult, [tk("bk")], [B_S[h], tk("vnk%d" % h)])
                for h in range(4):
                    qh = h // 2
                    oc = slice(384 + par * 64, 384 + par * 64 + 64)
                    k.mm(ps_sc[:, h, oc], S_b[:, h, :], qdT[:, h, c], [tk("S_b%d" % h), tk("qdT%d" % h)], [B_S[h]], start=True, stop=False, inc=False)
                    k.mm(ps_sc[:, h, oc], vn[:, h, :], attnT[:, h, c], [tk("vn%d" % h), tk("attnT%d" % h)], [B_S[h]], start=False, stop=True, inc=False)
                    k.mm(ps_sc[:, h, 256:384], ktok[:, qh, n, :], vnk[:, h, :], [tk("ktok%d" % qh), tk("vnk%d" % h)], [B_S[h]])
                for h in range(4):
                    k.stt(S_f[:, h, :], S_f[:, h, :], glast[:, n, h:h + 1], ps_sc[:, h, 256:384], ALU.mult, ALU.add,
                          [tk("glast")], [B_S[h], tk("S_f%d" % h)])
                    k.act(S_b[:, h, :], S_f[:, h, :], AF.Identity, [tk("S_f%d" % h)], [tk("S_b%d" % h)])
                if par == 1:
                    ov = ps_sc[:, :, 384:512]
                    tc0 = (n - 1) * 64
                    k.act(osq[:].rearrange("p (h t) -> p h t", t=128), ov, AF.Square, [], B_S + [tk("osq")])
                    k.mm(ps_bc[:, :], ones_f, osq[:], [tk("cst"), tk("osq")], [B_BC])
                    k.act(otmp[:], ps_bc[:], AF.Ln, [], [B_BC, tk("otmp")], scale=1.0 / 128, bias=EPS)
                    k.act(otmp[:], otmp[:], AF.Exp, [], [tk("otmp")], scale=-0.5)
                    k.tt(otmp[:].rearrange("p (h t) -> p h t", t=128), ov, otmp[:].rearrange("p (h t) -> p h t", t=128), ALU.mult,
                         [], B_S + [tk("otmp")])
                    k.stt(og[:], otmp[:].rearrange("p (h t) -> p h t", t=128), onorm_s[:, 0:1], zs[:, :, tc0:tc0 + 128], ALU.mult, ALU.mult,
                          [tk("otmp"), tk("onorm"), tk("zs%d" % p)], [tk("og")])
                    if fz is None:
                        k.ld(s_o, oT[:, t0 + tc0:t0 + tc0 + 128].rearrange("(h e) t -> e h t", e=128), og[:], [], r=[tk("og")])
                    else:
                        ysl = (t0 + tc0) // 128 % 2
                        for half, (pst, btk) in enumerate(((ps_pj, B_PJ), (ps_ms, B_MS))):
                            for h in range(4):
                                k.mm(pst[:, :], og[:, h, :], w0_b[:, h, half * 512:(half + 1) * 512], [tk("og"), tk("w0b")], [btk],
                                     start=(h == 0), stop=(h == 3), inc=(h == 3))
                        k.act(yst[:, ysl, 0:512], ps_pj[:, :], AF.Identity, [], [B_PJ, tk("yst%d" % ysl)])
                        k.cp(yst[:, ysl, 512:1024], ps_ms[:, :], [], [B_MS, tk("yst%d" % ysl)])
                        pos0 = t0 + tc0 - 48
                        nrows = fz["y0p"].shape[0]
                        if pos0 < 0:
                            k.ld(s_y[ysl], fz["y0p"][0:128 + pos0, :], yst[-pos0:128, ysl, :], [], r=[tk("yst%d" % ysl), tk("y0rows")])
                            rend = 128 + pos0
                        else:
                            nr = min(128, nrows - pos0)
                            rend = pos0 + max(nr, 0)
                            if nr > 0:
                                k.ld(s_y[ysl], fz["y0p"][pos0:pos0 + nr, :], yst[0:nr, ysl, :], [], r=[tk("yst%d" % ysl), tk("y0rows")])
                        while cc_next[0] < nrows and rend >= min(nrows, cc_next[0] + CC_ROWS):
                            r0, r1 = cc_next[0], min(nrows, cc_next[0] + CC_ROWS)
                            P.cc("cc", fz["scc"], fz["y0p"][r0:r1, :], fz["y0f"][r0:r1, :], [], [tk("y0rows")])
                            cc_next[0] = r1
        P.finish("sp", [tk("og")] + ([tk("yst0"), tk("yst1")] if fz is not None else []))
        P.emit()
        return P.final_events()


def gdn_inputs(inp, core, NTOK):
    b, hg = core // 4, core % 4
    L = 16 + inp["x"].shape[1]
    xp = np.zeros((NTOK, 1024), np.float32)
    n_real = min(L, NTOK - 48)
    xp[48:64] = inp["meta_tokens"]
    xp[64:48 + n_real] = inp["x"][b, :n_real - 16]
    W = inp["gdn_w_in"][0]
    qcols = np.arange(2 * hg * 128, (2 * hg + 2) * 128)
    kcols = 1024 + qcols
    vcols = 2048 + np.arange(4 * hg * 128, (4 * hg + 4) * 128)
    zcols = 4096 + np.arange(4 * hg * 128, (4 * hg + 4) * 128)
    bcols = 6144 + np.arange(4 * hg, 4 * hg + 4)
    acols = 6160 + np.arange(4 * hg, 4 * hg + 4)
    wq = np.ascontiguousarray(W[:, np.concatenate([qcols, kcols, vcols, zcols])])
    wba = np.ascontiguousarray(W[:, np.concatenate([bcols, acols])])
    cw = inp["gdn_conv_w"][0][:, np.concatenate([qcols, kcols, vcols])]
    convw = np.ascontiguousarray(cw.reshape(4, 8, 128).transpose(2, 1, 0).reshape(128, 32))
    return {
        "xp": xp, "gpre": inp["pre_norm"][0:1].copy(), "wq": wq, "wba": wba, "convw": convw,
        "alog": inp["gdn_a_log"][0:1, 4 * hg:4 * hg + 4].copy(), "dtb": inp["gdn_dt_bias"][0:1, 4 * hg:4 * hg + 4].copy(),
        "onorm": inp["gdn_out_norm"][0].reshape(128, 1).copy(), "cst": _consts(),
    }


def build_wout(NBLK):
    nc = bass.Bass("TRN2", target_bir_lowering=False)

    def din(n, s, dt=F32):
        return nc.dram_tensor(n, list(s), dt, kind="ExternalInput").ap()

    NT = NBLK * 128
    oTin = din("oTin", [2048, NT], BF16)
    resid = din("resid", [NT, 1024])
    w = din("w", [2048, 1024])
    gpost = din("gpost", [1, 1024])
    out = nc.dram_tensor("out", [NT, 1024], F32, kind="ExternalOutput").ap()
    with contextlib.ExitStack() as st:
        P = Prog(nc, st)
        k = K(P)
        sb, ps = P.sb, P.ps
        wsb = sb("wsb", [128, 16, 1024], BF16)
        wstage = sb("wstage", [128, 2, 1024], F32)
        gp_b = sb("gp_b", [128, 1024], F32)
        oTs = sb("oTs", [128, 2, 16, 128], BF16)
        rs = sb("rs", [128, 2, 1024], F32)
        junk = sb("junk", [128, 512], BF16)
        ss = sb("ss", [128, 2], F32)
        rstd = sb("rstd", [128, 1], F32)
        ot = sb("ot", [128, 2, 1024], F32)
        ps_y = ps("ps_y", [128, 2, 512], F32)
        B_Y = [Tk("B_Y0"), Tk("B_Y1")]
        T = {}

        def tk(n):
            if n not in T:
                T[n] = Tk(n)
            return T[n]
        s_c = P.dmasem("c")
        s_w = [P.dmasem("w0"), P.dmasem("w1")]
        s_i = [P.dmasem("i0"), P.dmasem("i1")]
        s_o = [P.dmasem("o0"), P.dmasem("o1")]
        k.ld(s_c, gp_b[:], gpost[0:1, :].partition_broadcast(128), [tk("gp")])
        for kc in range(16):
            sl = kc % 2
            k.ld(s_w[sl], wstage[:, sl, :], w[kc * 128:(kc + 1) * 128, :], [tk("wst%d" % sl)])
            k.cp(wsb[:, kc, :], wstage[:, sl, :], [tk("wst%d" % sl)], [tk("wsb")], eng="pool")
        for blk in range(NBLK):
            sl = blk % 2
            c0 = blk * 128
            k.ld(s_i[sl], oTs[:, sl, :, :], oTin[:, c0:c0 + 128].rearrange("(k p) t -> p k t", p=128), [tk("in%d" % sl)])
            k.ld(s_i[sl], rs[:, sl, :], resid[c0:c0 + 128, :], [tk("in%d" % sl)])
            for half in range(2):
                for kc in range(16):
                    k.mm(ps_y[:, half, :], oTs[:, sl, kc, :], wsb[:, kc, half * 512:(half + 1) * 512], [tk("in%d" % sl), tk("wsb")], [B_Y[half]],
                         start=(kc == 0), stop=(kc == 15), inc=(kc == 15))
                k.act(junk[:], ps_y[:, half, :], AF.Square, [], [B_Y[half], tk("junk"), tk("ss")], accum_out=ss[:, half:half + 1])
            k.tt(rstd[:], ss[:, 0:1], ss[:, 1:2], ALU.add, [tk("ss")], [tk("rstd")])
            k.act(rstd[:], rstd[:], AF.Sqrt, [], [tk("rstd")], scale=1.0 / 1024, bias=EPS)
            k.rcp(rstd[:], rstd[:], [], [tk("rstd")])
            for half in range(2):
                hs = slice(half * 512, (half + 1) * 512)
                k.stt(ot[:, sl, hs], ps_y[:, half, :], rstd[:, 0:1], gp_b[:, hs], ALU.mult, ALU.mult, [tk("rstd"), tk("gp")], [B_Y[half], tk("ot%d" % sl)])
            k.tt(ot[:, sl, :], ot[:, sl, :], rs[:, sl, :], ALU.add, [tk("in%d" % sl)], [tk("ot%d" % sl)])
            k.ld(s_o[sl], out[c0:c0 + 128, :], ot[:, sl, :], [], r=[tk("ot%d" % sl)])
        P.finish("sp", [tk("ot0"), tk("ot1")])
        P.emit()
    return nc


SCALE = 192.0 ** -0.5


def _consts_mla():
    c = np.zeros((128, 384), np.float32)
    c[:, 0:128] = np.eye(128, dtype=np.float32)
    c[:, 128:256] = 1.0
    kk = np.arange(128)
    c[:, 256:384] = (kk[None, :] >= kk[:, None])
    return c


def mla_decl(nc, NTOK2):
    def din(n, s, dt=F32):
        return nc.dram_tensor(n, list(s), dt, kind="ExternalInput").ap()
    io = {}
    io["g1"] = din("g1", [1, 1024])
    io["gkv"] = din("gkv", [1, 1024])
    io["glat"] = din("glat", [128, 1])
    io["gq"] = din("gq", [1, 256])
    io["wkvd"] = din("wkvd", [1024, 256])
    io["wuk"] = din("wuk", [128, 512])
    io["wuv"] = din("wuv", [128, 512])
    io["wmi"] = din("wmi", [1024, 768])
    io["wqu"] = din("wqu", [256, 1024])
    io["cos2T"] = din("cos2T", [64, NTOK2])
    io["sinsT"] = din("sinsT", [64, NTOK2])
    io["cstm"] = din("cstm", [128, 384])
    return io


def build_mla(NTOK2):
    assert NTOK2 % 128 == 0
    nc = bass.Bass("TRN2", target_bir_lowering=False)
    io = mla_decl(nc, NTOK2)
    io["h1p"] = nc.dram_tensor("h1p", [NTOK2, 1024], F32, kind="ExternalInput").ap()
    io["o1T"] = nc.dram_tensor("o1T", [512, NTOK2], BF16, kind="ExternalOutput").ap()
    emit_mla(nc, None, io, NTOK2)
    return nc


def emit_mla(nc, semst, io, NTOK2, prew=(), fz=None):
    NBK = NTOK2 // 128
    g1, gkv, glat, gq, wkvd, wuk, wuv, wmi, wqu, cos2T, sinsT, cst = (
        io[n] for n in ("g1", "gkv", "glat", "gq", "wkvd", "wuk", "wuv", "wmi", "wqu", "cos2T", "sinsT", "cstm"))
    h1p = io.get("h1p")
    o1T = io.get("o1T")
    with contextlib.ExitStack() as st:
        P = Prog(nc, st, semst, "m", prew)
        k = K(P)
        sb, ps = P.sb, P.ps
        if fz is not None:
            gp0_b = sb("gp0_b", [128, 1024], F32)
            w1_b = sb("w1_b", [128, 4, 1024], BF16)
            ys = sb("ys", [128, 2, 1024], F32)
            ssy = sb("ssy", [128, 1], F32)
            y1st = sb("y1st", [128, 1024], F32)
        ckvT = sb("ckvT", [128, NTOK2], BF16)
        kropeT = sb("kropeT", [128, NTOK2], BF16)
        ckvtok = sb("ckvtok", [128, NBK, 129], BF16)
        wkvd_b = sb("wkvd_b", [128, 8, 256], BF16)
        wuk_b = sb("wuk_b", [128, 4, 128], BF16)
        wukT_b = sb("wukT_b", [128, 4, 128], BF16)
        wuv_b = sb("wuv_b", [128, 4, 128], BF16)
        wmi_b = sb("wmi_b", [128, 8, 768], BF16)
        wqu_b = sb("wqu_b", [128, 2, 1024], BF16)
        wstage = sb("wstage", [128, 2, 1024], F32)
        g1_b = sb("g1_b", [128, 1024], F32)
        gkv_b = sb("gkv_b", [128, 1024], F32)
        gq_b = sb("gq_b", [128, 256], F32)
        glat_s = sb("glat_s", [128, 1], F32)
        cst_s = sb("cst_s", [128, 384], F32)
        ident_b = sb("ident_b", [128, 128], BF16)
        tri_b = sb("tri_b", [128, 128], BF16)
        zb = sb("zb", [128, 512], BF16)
        ones_f = cst_s[:, 128:256]
        hs = sb("hs", [128, 4, 1024], F32)
        junk = sb("junk", [128, 1024], BF16)
        ss = sb("ss", [128, 4], F32)
        rstd = sb("rstd", [128, 4], F32)
        hn1 = sb("hn1", [128, 1024], BF16)
        hkv = sb("hkv", [128, 1024], BF16)
        hn1T = sb("hn1T", [128, 8, 512], BF16)
        hkvT = sb("hkvT", [128, 8, 512], BF16)
        cs = sb("cs", [64, 512], F32)
        sn = sb("sn", [64, 512], F32)
        ckf = sb("ckf", [128, 512], F32)
        sq = sb("sq", [128, 512], F32)
        rt = sb("rt", [128, 512], F32)
        ra = sb("ra", [64, 2, 512], F32)
        rbb = sb("rbb", [64, 2, 512], F32)
        ssq = sb("ssq", [128, 1], F32)
        cqn = sb("cqn", [128, 256], BF16)
        cqT = sb("cqT", [128, 2, 512], BF16)
        zs1 = sb("zs1", [128, 4, 512], F32)
        qnT = sb("qnT", [128, 2, 512], BF16)
        qpT = sb("qpT", [128, 4, 512], BF16)
        qrT = sb("qrT", [128, 4, 512], BF16)
        pT = sb("pT", [128, 3, 512], BF16)
        pacc = sb("pacc", [128, 2, 512], F32)
        rdb = sb("rdb", [128, 512], F32)
        ocn = sb("ocn", [128, 512], BF16)
        og1 = sb("og1", [128, 4, 512], BF16)
        ps_tr = ps("ps_tr", [128, 1024], BF16)
        ps_pj = ps("ps_pj", [128, 512], F32)
        ps_p2 = ps("ps_p2", [128, 512], F32)
        ps_v = ps("ps_v", [128, 512], F32)
        ps_s = ps("ps_s", [128, 3, 512], F32)
        ps_o = ps("ps_o", [128, 512], F32)
        B_TR, B_PJ, B_P2, B_V = Tk("B_TR"), Tk("B_PJ"), Tk("B_P2"), Tk("B_V")
        B_S = [Tk("B_S0"), Tk("B_S1"), Tk("B_S2")]
        B_O = Tk("B_O")
        T = {}

        def tk(n):
            if n not in T:
                T[n] = Tk(n)
            return T[n]

        s_c = P.dmasem("c")
        s_w = [P.dmasem("w0"), P.dmasem("w1")]
        s_x = P.dmasem("x")
        s_o = P.dmasem("o")
        k.ld(s_c, cst_s[:], cst[:, :], [tk("cst")])
        k.ld(s_c, g1_b[:], g1[0:1, :].partition_broadcast(128), [tk("g1")])
        k.ld(s_c, gkv_b[:], gkv[0:1, :].partition_broadcast(128), [tk("gkv")])
        k.ld(s_c, gq_b[:], gq[0:1, :].partition_broadcast(128), [tk("gq")])
        k.ld(s_c, glat_s[:], glat[:, :], [tk("glat")])
        if fz is not None:
            s_yl = [P.dmasem("yl0"), P.dmasem("yl1")]
            s_h = P.dmasem("h")
            k.ld(s_c, gp0_b[:], fz["gp0"][0:1, :].partition_broadcast(128), [tk("gp0")])
            tk("gp0").w = None
        for _n in ("cst", "g1", "gkv", "gq", "glat", "gp0"):
            tk(_n).w = (s_c, P.dcnt[s_c])
        k.cp(ident_b[:], cst_s[:, 0:128], [tk("cst")], [tk("identb")])
        k.cp(tri_b[:], cst_s[:, 256:384], [tk("cst")], [tk("trib")])
        P.op("dve", lambda e: e.memset(zb[:], 0.0), [], [tk("zb")])
        P.op("pool", lambda e: e.memset(ckvtok[:], 1.0), [], [tk("ckvtok")])
        P.op("pool", lambda e: e.memset(kropeT[:], 0.0), [], [tk("kropeT")])
        P.op("pool", lambda e: e.memset(qrT[:], 0.0), [], [tk("qrT%d" % h_) for h_ in range(4)])
        wl = []
        for kc in range(8):
            wl.append((wkvd[kc * 128:(kc + 1) * 128, :], 256, wkvd_b[:, kc, :]))
        wl.append((wuk[:, :], 512, wuk_b[:].rearrange("p h d -> p (h d)")))
        wl.append((wuv[:, :], 512, wuv_b[:].rearrange("p h d -> p (h d)")))
        for kc in range(8):
            wl.append((wmi[kc * 128:(kc + 1) * 128, :], 768, wmi_b[:, kc, :]))
        for c2 in range(2):
            wl.append((wqu[c2 * 128:(c2 + 1) * 128, :], 1024, wqu_b[:, c2, :]))
        if fz is not None:
            for h in range(4):
                wl.append((fz["w1"][h * 128:(h + 1) * 128, :], 1024, w1_b[:, h, :]))
        for i, (src, n, dst) in enumerate(wl):
            sl = i % 2
            k.ld(s_w[sl], wstage[:, sl, 0:n], src, [tk("wst%d" % sl)])
            k.cp(dst, wstage[:, sl, 0:n], [tk("wst%d" % sl)], [tk("wts")], eng="pool")
        for h in range(4):
            k.tr(ps_tr[:, h * 128:(h + 1) * 128], wuk_b[:, h, :], ident_b[:], [tk("wts"), tk("identb")], [B_TR], inc=(h == 3))
        k.act(wukT_b[:].rearrange("p h d -> p (h d)"), ps_tr[:, 0:512], AF.Identity, [], [B_TR, tk("wukT")])

        ntiles = (NTOK2 + 511) // 512
        cc_next = [0]
        for ti in range(ntiles):
            t0 = ti * 512
            TT = min(512, NTOK2 - t0)
            NS = TT // 128
            blk0 = t0 // 128
            hsrc = h1p[t0:t0 + TT, :] if fz is None else fz["xp"][48 + t0:48 + t0 + TT, :]
            k.ld(s_x, hs[:, 0:NS, :], hsrc.rearrange("(s p) d -> p s d", p=128), [tk("hs")])
            k.ld(s_x, cs[:, 0:TT], cos2T[:, t0:t0 + TT], [tk("cs")])
            k.ld(s_x, sn[:, 0:TT], sinsT[:, t0:t0 + TT], [tk("cs")])
            tk("hs").w = (s_x, P.dcnt[s_x])
            tk("cs").w = (s_x, P.dcnt[s_x])
            if fz is not None:
                for s in range(NS):
                    ysl = s % 2
                    ytk = tk("ys%d" % ysl)
                    k.ld(s_yl[ysl], ys[:, ysl, :], fz["y0f"][t0 + s * 128:t0 + (s + 1) * 128, :], [ytk])
                    k.act(junk[:], ys[:, ysl, :], AF.Square, [ytk], [tk("junk"), tk("ssy")], accum_out=ssy[:])
                    k.act(ssy[:], ssy[:], AF.Sqrt, [], [tk("ssy")], scale=1.0 / 1024, bias=EPS)
                    k.rcp(ssy[:], ssy[:], [], [tk("ssy")])
                    k.stt(ys[:, ysl, :], ys[:, ysl, :], ssy[:, 0:1], gp0_b[:], ALU.mult, ALU.mult, [tk("ssy"), tk("gp0")], [ytk])
                    k.tt(hs[:, s, :], hs[:, s, :], ys[:, ysl, :], ALU.add, [ytk], [tk("hs")])
                k.ld(s_h, fz["h1s"][t0:t0 + TT, :].rearrange("(s p) d -> p s d", p=128), hs[:, 0:NS, :], [], r=[tk("hs")], q="act")
            for s in range(NS):
                k.act(junk[:], hs[:, s, :], AF.Square, [tk("hs")], [tk("junk"), tk("ss")], accum_out=ss[:, s:s + 1])
            k.act(rstd[:, 0:NS], ss[:, 0:NS], AF.Sqrt, [tk("ss")], [tk("rstd")], scale=1.0 / 1024, bias=EPS)
            k.rcp(rstd[:, 0:NS], rstd[:, 0:NS], [], [tk("rstd")])
            for s in range(NS):
                for (gb, gt_, dstT, nm) in ((g1_b, "g1", hn1T, "hn1"), (gkv_b, "gkv", hkvT, "hkv")):
                    buf = hn1 if nm == "hn1" else hkv
                    k.stt(buf[:], hs[:, s, :], rstd[:, s:s + 1], gb[:], ALU.mult, ALU.mult, [tk("hs"), tk("rstd"), tk(gt_)], [tk(nm)])
                    for kc in range(8):
                        k.tr(ps_tr[:, kc * 128:(kc + 1) * 128], buf[:, kc * 128:(kc + 1) * 128], ident_b[:], [tk(nm), tk("identb")], [B_TR],
                             inc=(kc == 7))
                    k.act(dstT[:, :, s * 128:(s + 1) * 128], ps_tr[:].rearrange("p (k t) -> p k t", t=128), AF.Identity, [], [B_TR, tk(nm + "T")])
            tsl = slice(t0, t0 + TT)
            for kc in range(8):
                k.mm(ps_pj[:, 0:TT], wkvd_b[:, kc, 0:128], hkvT[:, kc, 0:TT], [tk("wts"), tk("hkvT")], [B_PJ], start=(kc == 0), stop=(kc == 7), inc=(kc == 7))
            k.act(ckf[:, 0:TT], ps_pj[:, 0:TT], AF.Identity, [], [B_PJ, tk("ckf")])
            k.act(sq[:, 0:TT], ckf[:, 0:TT], AF.Square, [tk("ckf")], [tk("sq")])
            k.mm(ps_p2[:, 0:TT], ones_f, sq[:, 0:TT], [tk("cst"), tk("sq")], [B_P2])
            k.act(rt[:, 0:TT], ps_p2[:, 0:TT], AF.Ln, [], [B_P2, tk("rt")], scale=1.0 / 128, bias=EPS)
            k.act(rt[:, 0:TT], rt[:, 0:TT], AF.Exp, [], [tk("rt")], scale=-0.5)
            k.stt(ckvT[:, tsl], ckf[:, 0:TT], glat_s[:, 0:1], rt[:, 0:TT], ALU.mult, ALU.mult, [tk("ckf"), tk("glat"), tk("rt")], [tk("ckvT")])
            for kc in range(8):
                k.mm(ps_pj[0:64, 0:TT], wkvd_b[:, kc, 128:192], hkvT[:, kc, 0:TT], [tk("wts"), tk("hkvT")], [B_PJ], start=(kc == 0), stop=(kc == 7), inc=(kc == 7))
            for kc in range(8):
                k.mm(ps_p2[0:64, 0:TT], wkvd_b[:, kc, 192:256], hkvT[:, kc, 0:TT], [tk("wts"), tk("hkvT")], [B_P2], start=(kc == 0), stop=(kc == 7), inc=(kc == 7))
            k.tt(ra[:, 0, 0:TT], ps_pj[0:64, 0:TT], cs[:, 0:TT], ALU.mult, [tk("cs")], [B_PJ, tk("ra0")])
            k.tt(rbb[:, 0, 0:TT], ps_p2[0:64, 0:TT], sn[:, 0:TT], ALU.mult, [tk("cs")], [B_P2, tk("rbb0")])
            k.tt(kropeT[0:64, tsl], ra[:, 0, 0:TT], rbb[:, 0, 0:TT], ALU.add, [tk("ra0"), tk("rbb0")], [tk("kropeT")])
            for s in range(NS):
                k.tr(ps_tr[:, s * 128:(s + 1) * 128], ckvT[:, t0 + s * 128:t0 + (s + 1) * 128], ident_b[:], [tk("ckvT"), tk("identb")], [B_TR], inc=(s == NS - 1))
            k.act(ckvtok[:, blk0:blk0 + NS, 0:128], ps_tr[:, 0:NS * 128].rearrange("p (s d) -> p s d", d=128), AF.Identity, [], [B_TR, tk("ckvtok")])
            for s in range(NS):
                for kc in range(8):
                    k.mm(ps_v[:, 0:256], hn1T[:, kc, s * 128:(s + 1) * 128], wmi_b[:, kc, 0:256], [tk("hn1T"), tk("wts")], [B_V], start=(kc == 0), stop=(kc == 7), inc=(kc == 7))
                k.act(junk[:, 0:256], ps_v[:, 0:256], AF.Square, [], [B_V, tk("junk"), tk("ssq")], accum_out=ssq[:])
                k.act(ssq[:], ssq[:], AF.Sqrt, [], [tk("ssq")], scale=1.0 / 256, bias=EPS)
                k.rcp(ssq[:], ssq[:], [], [tk("ssq")])
                k.stt(cqn[:], ps_v[:, 0:256], ssq[:, 0:1], gq_b[:], ALU.mult, ALU.mult, [tk("ssq"), tk("gq")], [B_V, tk("cqn")])
                for c2 in range(2):
                    k.tr(ps_tr[:, c2 * 128:(c2 + 1) * 128], cqn[:, c2 * 128:(c2 + 1) * 128], ident_b[:], [tk("cqn"), tk("identb")], [B_TR], inc=(c2 == 1))
                k.act(cqT[:, :, s * 128:(s + 1) * 128], ps_tr[:, 0:256].rearrange("p (c t) -> p c t", t=128), AF.Identity, [], [B_TR, tk("cqT")])
            for h in range(4):
                for kc in range(8):
                    k.mm(ps_pj[:, 0:TT], wmi_b[:, kc, 256 + h * 128:256 + (h + 1) * 128], hn1T[:, kc, 0:TT], [tk("wts"), tk("hn1T")], [B_PJ],
                         start=(kc == 0), stop=(kc == 7), inc=(kc == 7))
                k.act(zs1[:, h, 0:TT], ps_pj[:, 0:TT], AF.Silu, [], [B_PJ, tk("zs1")])
            for hp in range(2):
                hh = (2 * hp, 2 * hp + 1)
                sets = {hh[0]: (ps_pj, B_PJ, ps_p2, B_P2, 0), hh[1]: (ps_v, B_V, ps_o, B_O, 1)}
                for h in hh:
                    pa, ba, pb, bb, u = sets[h]
                    for c2 in range(2):
                        k.mm(pa[:, 0:TT], wqu_b[:, c2, h * 256:h * 256 + 128], cqT[:, c2, 0:TT], [tk("wts"), tk("cqT")], [ba], start=(c2 == 0), stop=(c2 == 1), inc=(c2 == 1))
                for h in hh:
                    pa, ba, pb, bb, u = sets[h]
                    k.act(qnT[:, u, 0:TT], pa[:, 0:TT], AF.Identity, [], [ba, tk("qnT%d" % u)])
                for h in hh:
                    pa, ba, pb, bb, u = sets[h]
                    k.mm(pb[:, 0:TT], wukT_b[:, h, :], qnT[:, u, 0:TT], [tk("wukT"), tk("qnT%d" % u)], [bb])
                for h in hh:
                    pa, ba, pb, bb, u = sets[h]
                    k.act(qpT[:, h, 0:TT], pb[:, 0:TT], AF.Identity, [], [bb, tk("qpT%d" % h)])
                for h in hh:
                    pa, ba, pb, bb, u = sets[h]
                    for c2 in range(2):
                        k.mm(pa[0:64, 0:TT], wqu_b[:, c2, h * 256 + 128:h * 256 + 192], cqT[:, c2, 0:TT], [tk("wts"), tk("cqT")], [ba], start=(c2 == 0), stop=(c2 == 1), inc=(c2 == 1))
                    for c2 in range(2):
                        k.mm(pb[0:64, 0:TT], wqu_b[:, c2, h * 256 + 192:h * 256 + 256], cqT[:, c2, 0:TT], [tk("wts"), tk("cqT")], [bb], start=(c2 == 0), stop=(c2 == 1), inc=(c2 == 1))
                for h in hh:
                    pa, ba, pb, bb, u = sets[h]
                    k.tt(ra[:, u, 0:TT], pa[0:64, 0:TT], cs[:, 0:TT], ALU.mult, [tk("cs")], [ba, tk("ra%d" % u)])
                    k.tt(rbb[:, u, 0:TT], pb[0:64, 0:TT], sn[:, 0:TT], ALU.mult, [tk("cs")], [bb, tk("rbb%d" % u)])
                    k.tt(qrT[0:64, h, 0:TT], ra[:, u, 0:TT], rbb[:, u, 0:TT], ALU.add, [tk("ra%d" % u), tk("rbb%d" % u)], [tk("qrT%d" % h)])
            nkb = blk0 + NS

            def emit_s(h, j):
                jj = j - blk0
                qlo = max(0, jj) * 128
                buf = j % 3
                ksl = slice(j * 128, (j + 1) * 128)
                k.mm(ps_s[:, buf, qlo:TT], ckvT[:, ksl], qpT[:, h, qlo:TT], [tk("ckvT"), tk("qpT%d" % h)], [B_S[buf]], start=True, stop=False, inc=False)
                k.mm(ps_s[:, buf, qlo:TT], kropeT[:, ksl], qrT[:, h, qlo:TT], [tk("kropeT"), tk("qrT%d" % h)], [B_S[buf]], start=False, stop=True)
                k.act(pT[:, buf, qlo:TT], ps_s[:, buf, qlo:TT], AF.Exp, [], [B_S[buf], tk("pT%d" % buf)], scale=SCALE)
                if jj >= 0:
                    k.tt(pT[:, buf, qlo:qlo + 128], pT[:, buf, qlo:qlo + 128], tri_b[:], ALU.mult, [tk("trib")], [tk("pT%d" % buf)])
                if j == 0:
                    k.cp(pacc[:, h % 2, 0:TT], pT[:, buf, 0:TT], [tk("pT%d" % buf)], [tk("pacc%d" % (h % 2))])
                else:
                    k.tt(pacc[:, h % 2, qlo:TT], pacc[:, h % 2, qlo:TT], pT[:, buf, qlo:TT], ALU.add, [tk("pT%d" % buf)], [tk("pacc%d" % (h % 2))])

            def emit_pv(h, j):
                jj = j - blk0
                qlo = max(0, jj) * 128
                buf = j % 3
                k.mm(ps_o[:, qlo:TT], ckvtok[:, j, 0:128], pT[:, buf, qlo:TT], [tk("pT%d" % buf), tk("ckvtok")], [B_O],
                     start=(j == 0), stop=(j == nkb - 1))

            for h in range(4):
                if h == 0:
                    emit_s(h, 0)
                    if nkb > 1:
                        emit_s(h, 1)
                for j in range(nkb):
                    if j + 2 < nkb:
                        emit_s(h, j + 2)
                    emit_pv(h, j)
                if h < 3:
                    emit_s(h + 1, 0)
                    if nkb > 1:
                        emit_s(h + 1, 1)
                k.mm(ps_v[:, 0:TT], ones_f, pacc[:, h % 2, 0:TT], [tk("cst"), tk("pacc%d" % (h % 2))], [B_V])
                k.act(rdb[:, 0:TT], ps_v[:, 0:TT], AF.Ln, [], [B_V, tk("rdb")])
                k.act(rdb[:, 0:TT], rdb[:, 0:TT], AF.Exp, [], [tk("rdb")], scale=-1.0)
                k.tt(ocn[:, 0:TT], ps_o[:, 0:TT], rdb[:, 0:TT], ALU.mult, [tk("rdb")], [B_O, tk("ocn")])
                k.mm(ps_pj[:, 0:TT], wuv_b[:, h, :], ocn[:, 0:TT], [tk("wts"), tk("ocn")], [B_PJ])
                k.tt(og1[:, h, 0:TT], ps_pj[:, 0:TT], zs1[:, h, 0:TT], ALU.mult, [tk("zs1")], [B_PJ, tk("og1")])
            if fz is None:
                k.ld(s_o, o1T[:, t0:t0 + TT].rearrange("(h e) t -> e h t", e=128), og1[:, :, 0:TT], [], r=[tk("og1")])
            else:
                for s in range(NS):
                    for half, (pst, btk) in enumerate(((ps_pj, B_PJ), (ps_p2, B_P2))):
                        for h in range(4):
                            k.mm(pst[:, :], og1[:, h, s * 128:(s + 1) * 128], w1_b[:, h, half * 512:(half + 1) * 512], [tk("og1"), tk("wts")], [btk],
                                 start=(h == 0), stop=(h == 3), inc=(h == 3))
                    k.act(y1st[:, 0:512], ps_pj[:, :], AF.Identity, [], [B_PJ, tk("y1st")])
                    k.cp(y1st[:, 512:1024], ps_p2[:, :], [], [B_P2, tk("y1st")])
                    k.ld(s_o, fz["y1p"][t0 + s * 128:t0 + (s + 1) * 128, :], y1st[:], [], r=[tk("y1st"), tk("y1rows")], q="act")
                rend = t0 + TT
                while cc_next[0] < NTOK2 and rend >= min(NTOK2, cc_next[0] + CC_ROWS):
                    r0, r1 = cc_next[0], min(NTOK2, cc_next[0] + CC_ROWS)
                    P.cc("cc", fz["scc"], fz["y1p"][r0:r1, :], fz["y1f"][r0:r1, :], [], [tk("y1rows")])
                    cc_next[0] = r1
        P.finish("sp", [tk("og1")] + ([tk("y1st"), tk("hs")] if fz is not None else []))
        P.emit()
        return P.final_events()


def rope_tables_T(n):
    inv = (np.float32(10000.0) ** (-(np.arange(0, 64, 2, dtype=np.float32)) / np.float32(64))).astype(np.float32)
    ang = (np.arange(n, dtype=np.float32)[:, None] * inv[None, :]).astype(np.float32)
    cos, sin = np.cos(ang).astype(np.float32), np.sin(ang).astype(np.float32)
    cos2T = np.ascontiguousarray(np.concatenate([cos, cos], 1).T)
    sinsT = np.ascontiguousarray(np.concatenate([-sin, sin], 1).T)
    return cos2T, sinsT


def mla_inputs(inp, core, h1b, NTOK2):
    hg = core % 4
    h1p = None
    if h1b is not None:
        L = h1b.shape[0]
        h1p = np.zeros((NTOK2, 1024), np.float32)
        h1p[:L] = h1b
    kd = inp["kv_w_down"]
    wkvd = np.ascontiguousarray(np.concatenate([kd[:, 0:128], kd[:, 128:192], kd[:, 160:192], kd[:, 128:160]], 1))
    ku = inp["kv_w_up"].reshape(128, 16, 256)[:, 4 * hg:4 * hg + 4]
    wuk = np.ascontiguousarray(ku[:, :, 0:128].reshape(128, 512))
    wuv = np.ascontiguousarray(ku[:, :, 128:256].reshape(128, 512))
    mi = inp["mla_w_in"][0]
    wmi = np.ascontiguousarray(np.concatenate([mi[:, 0:256], mi[:, 256 + 512 * hg:256 + 512 * (hg + 1)]], 1))
    qu = inp["mla_w_q_up"][0].reshape(256, 16, 192)[:, 4 * hg:4 * hg + 4]
    wqu = np.ascontiguousarray(np.concatenate([qu[:, :, 0:128], qu[:, :, 128:192], qu[:, :, 160:192], qu[:, :, 128:160]], 2).reshape(256, 1024))
    cos2T, sinsT = rope_tables_T(NTOK2)
    d = {} if h1p is None else {"h1p": h1p}
    d.update(_mla_rest(inp, wkvd, wuk, wuv, wmi, wqu, cos2T, sinsT))
    return d


def _mla_rest(inp, wkvd, wuk, wuv, wmi, wqu, cos2T, sinsT):
    return {
        "g1": inp["pre_norm"][1:2].copy(), "gkv": inp["kv_norm"].reshape(1, 1024).copy(),
        "glat": inp["kv_latent_norm"].reshape(128, 1).copy(), "gq": inp["mla_q_latent_norm"][0:1].copy(),
        "wkvd": wkvd, "wuk": wuk, "wuv": wuv, "wmi": wmi, "wqu": wqu, "cos2T": cos2T, "sinsT": sinsT, "cstm": _consts_mla(),
    }


def emit_fin(nc, semst, h1s, y1f, gp1, out, NBK, prew=()):
    with contextlib.ExitStack() as st:
        P = Prog(nc, st, semst, "f", prew)
        k = K(P)
        sb = P.sb
        gp_b = sb("gp_b", [128, 1024], F32)
        NB_ = 6
        hb = sb("hb", [128, NB_, 1024], F32)
        yb = sb("yb", [128, NB_, 1024], F32)
        junk = sb("junk", [128, 1024], BF16)
        ss = sb("ss", [128, 1], F32)
        T = {}

        def tk(n):
            if n not in T:
                T[n] = Tk(n)
            return T[n]
        s_c = P.dmasem("c")
        s_i = [P.dmasem("i%d" % i_) for i_ in range(NB_)]
        s_o = [P.dmasem("o%d" % i_) for i_ in range(NB_)]
        k.ld(s_c, gp_b[:], gp1[0:1, :].partition_broadcast(128), [tk("gp")])
        for blk in range(NBK):
            sl = blk % NB_
            rows = slice(blk * 128, (blk + 1) * 128)
            k.ld(s_i[sl], hb[:, sl, :], h1s[rows, :], [tk("hb%d" % sl)])
            k.ld(s_i[sl], yb[:, sl, :], y1f[rows, :], [tk("yb%d" % sl)])
            tk("hb%d" % sl).w = (s_i[sl], P.dcnt[s_i[sl]])
            k.act(junk[:], yb[:, sl, :], AF.Square, [tk("yb%d" % sl)], [tk("junk"), tk("ss")], accum_out=ss[:])
            k.act(ss[:], ss[:], AF.Sqrt, [], [tk("ss")], scale=1.0 / 1024, bias=EPS)
            k.rcp(ss[:], ss[:], [], [tk("ss")])
            k.stt(yb[:, sl, :], yb[:, sl, :], ss[:, 0:1], gp_b[:], ALU.mult, ALU.mult, [tk("ss"), tk("gp")], [tk("yb%d" % sl)])
            k.tt(yb[:, sl, :], yb[:, sl, :], hb[:, sl, :], ALU.add, [tk("hb%d" % sl)], [tk("yb%d" % sl)], eng=("pool" if blk % 3 == 0 else "dve"))
            k.ld(s_o[sl], out[rows, :], yb[:, sl, :], [], r=[tk("yb%d" % sl)], q="act")
        P.finish("sp", [tk("yb%d" % i_) for i_ in range(NB_)])
        P.emit()
        return P.final_events()


GROUPS = [[0, 1, 2, 3], [4, 5, 6, 7]]
CC_ROWS = 1024


def emit_allreduce(nc, ev, src, dst, scc):
    with nc.Block() as block:
        @block.gpsimd
        def _(g):
            for hsem, v in ev:
                g.wait_ge(hsem, v)
            rows = src.ap().shape[0]
            n = 0
            for r0 in range(0, rows, CC_ROWS):
                r1 = min(rows, r0 + CC_ROWS)
                g.collective_compute("AllReduce", ALU.add, replica_groups=GROUPS,
                                     ins=[src.ap()[r0:r1, :]], outs=[dst.ap()[r0:r1, :]]).then_inc(scc)
                n += 1
            g.wait_ge(scc, n)
    rows_ = src.ap().shape[0]
    return (rows_ + CC_ROWS - 1) // CC_ROWS


def build_fused(NTOK, NTOK2):
    nc = bass.Bass("TRN2", target_bir_lowering=False)

    def din(n, s_, dt=F32):
        return nc.dram_tensor(n, list(s_), dt, kind="ExternalInput").ap()
    ioG = gdn_decl(nc, NTOK)
    ioM = mla_decl(nc, NTOK2)
    w0 = din("w0", [512, 1024])
    gp0 = din("gp0", [1, 1024])
    w1 = din("w1", [512, 1024])
    gp1 = din("gp1", [1, 1024])
    out = nc.dram_tensor("out", [NTOK2, 1024], F32, kind="ExternalOutput").ap()
    y0p = nc.dram_tensor("y0p", [NTOK2, 1024], F32)
    y0f = nc.dram_tensor("y0f", [NTOK2, 1024], F32)
    h1s = nc.dram_tensor("h1s", [NTOK2, 1024], F32)
    y1p = nc.dram_tensor("y1p", [NTOK2, 1024], F32)
    y1f = nc.dram_tensor("y1f", [NTOK2, 1024], F32)
    with contextlib.ExitStack() as semst:
        scc0 = semst.enter_context(nc.semaphore("cc0"))
        scc1 = semst.enter_context(nc.semaphore("cc1"))
        ev = emit_gdn(nc, semst, ioG, NTOK, fz=dict(y0p=y0p.ap(), y0f=y0f.ap(), scc=scc0, w0=w0))
        ev = emit_mla(nc, semst, ioM, NTOK2, prew=ev,
                      fz=dict(xp=ioG["xp"], y0f=y0f.ap(), gp0=gp0, h1s=h1s.ap(), y1p=y1p.ap(), y1f=y1f.ap(), scc=scc1, w1=w1))
        emit_fin(nc, semst, h1s.ap(), y1f.ap(), gp1, out, NTOK2 // 128, prew=ev)
    return nc


def fused_inputs(inp, core, NTOK, NTOK2):
    hg = core % 4
    d = gdn_inputs(inp, core, NTOK)
    d.update(mla_inputs(inp, core, None, NTOK2))
    d["w0"] = np.ascontiguousarray(inp["gdn_w_out"][0][hg * 512:(hg + 1) * 512])
    d["w1"] = np.ascontiguousarray(inp["mla_w_out"][0][hg * 512:(hg + 1) * 512])
    d["gp0"] = inp["post_norm"][0:1].copy()
    d["gp1"] = inp["post_norm"][1:2].copy()
    return d


_NC_CACHE = {}


def _get(name, fn, *a):
    key = (name,) + a
    if key not in _NC_CACHE:
        _NC_CACHE[key] = fn(*a)
    return _NC_CACHE[key]


def kernel(**inputs):
    inp = {k_: np.ascontiguousarray(np.asarray(v)) for k_, v in inputs.items()}
    B, SEQ, D = inp["x"].shape
    L = SEQ + 16
    NTOK2 = ((L + 127) // 128) * 128
    NTOK = ((NTOK2 + 48 + 127) // 128) * 128
    cores = list(range(8))
    nc = _get("fused", build_fused, NTOK, NTOK2)
    res = run_bass_kernel_spmd(nc, [fused_inputs(inp, c, NTOK, NTOK2) for c in cores], core_ids=cores).results
    return np.stack([np.asarray(res[4 * b]["out"])[16:L] for b in range(B)], 0).astype(np.float32)
```
